# Optimizing a Trainium2 kernel written in Bass

```python
import numpy as np
import jax, jax.numpy as jnp
from jax import lax

D_MODEL = 1024
BATCH = 4
SEQ = 4096
DEPTH = 1

PLE_DIM = 256
NSA_HEADS = 8
NSA_GROUPS = 2
NSA_HPG = NSA_HEADS // NSA_GROUPS
HEAD_DIM = 64
CMP_LEN = 32
CMP_STRIDE = 16
CMP_HIDDEN = 256
SEL_LEN = 64
SEL_TOP = 16
WINDOW = 512
Q_BLOCK = 128
FORCE_SCORE = 1e4
CONV_CH = 512
CONV_WIDTH = 31
D_FF = 2816
FFN_CONV_WIDTH = 3
EPS = 1e-6
NEG = -1e30

NSA_Q = NSA_HEADS * HEAD_DIM
NSA_KV = 3 * 2 * NSA_GROUPS * HEAD_DIM
NSA_GATE = 3 * NSA_HEADS
CONV_IN = 2 * CONV_CH
MERGE = 2 * D_MODEL
N_IN = NSA_Q + NSA_KV + NSA_GATE + CONV_IN + MERGE
IN_SPLITS = (NSA_Q, NSA_Q + NSA_KV, NSA_Q + NSA_KV + NSA_GATE, NSA_Q + NSA_KV + NSA_GATE + CONV_IN)

kernel_name = "hybrid_nsa_conformer_convffn_ple"


def rmsnorm(x, g):
    xf = x.astype(jnp.float32)
    y = xf * lax.rsqrt(jnp.mean(xf * xf, axis=-1, keepdims=True) + EPS)
    return (y * g.astype(jnp.float32)).astype(x.dtype)


def layernorm(x, g, b):
    xf = x.astype(jnp.float32)
    mu = jnp.mean(xf, axis=-1, keepdims=True)
    xc = xf - mu
    y = xc * lax.rsqrt(jnp.mean(xc * xc, axis=-1, keepdims=True) + EPS)
    return (y * g.astype(jnp.float32) + b.astype(jnp.float32)).astype(x.dtype)


def causal_dwconv(x, w, b):
    k = w.shape[0]
    c = x.shape[-1]
    y = lax.conv_general_dilated(x, w[:, None, :].astype(x.dtype), window_strides=(1,),
                                 padding=((k - 1, 0),), dimension_numbers=('NWC', 'WIO', 'NWC'),
                                 feature_group_count=c)
    return y + b


def masked_softmax(logits, mask):
    logits = jnp.where(mask, logits.astype(jnp.float32), NEG)
    return jnp.where(mask, jax.nn.softmax(logits, axis=-1), 0.0)


def compress(kv, pos, w1, w2):
    s = kv.shape[2]
    n_cmp = (s - CMP_LEN) // CMP_STRIDE + 1
    idx = np.arange(n_cmp)[:, None] * CMP_STRIDE + np.arange(CMP_LEN)[None, :]
    blocks = kv[:, :, idx] + pos
    flat = blocks.reshape(blocks.shape[0], blocks.shape[1], n_cmp, CMP_LEN * HEAD_DIM)
    return jax.nn.gelu(flat @ w1) @ w2


def cmp_to_sel_matrix(n_cmp, n_sel):
    c0 = np.arange(n_cmp) * CMP_STRIDE
    j0 = np.arange(n_sel) * SEL_LEN
    lo = np.maximum(c0[:, None], j0[None, :])
    hi = np.minimum(c0[:, None] + CMP_LEN, j0[None, :] + SEL_LEN)
    return (np.maximum(hi - lo, 0) / CMP_LEN).astype(np.float32)


def nsa_attention(q, kc, vc, ks, vs, kw, vw):
    b, g, hg, s, dh = q.shape
    scale = dh ** -0.5
    n_cmp = kc.shape[2]
    n_sel = s // SEL_LEN
    top = min(SEL_TOP, n_sel)
    cmp_end = jnp.arange(n_cmp) * CMP_STRIDE + CMP_LEN - 1
    sel_map = jnp.asarray(cmp_to_sel_matrix(n_cmp, n_sel))
    ks_blk = ks.reshape(b, g, n_sel, SEL_LEN, dh)
    vs_blk = vs.reshape(b, g, n_sel, SEL_LEN, dh)
    kw_pad = jnp.pad(kw, ((0, 0), (0, 0), (WINDOW, 0), (0, 0)))
    vw_pad = jnp.pad(vw, ((0, 0), (0, 0), (WINDOW, 0), (0, 0)))
    bi = jnp.arange(b)[:, None, None, None]
    gi = jnp.arange(g)[None, :, None, None]
    jsel = jnp.arange(n_sel)

    def query_block(qb_idx):
        t0 = qb_idx * Q_BLOCK
        qb = lax.dynamic_slice_in_dim(q, t0, Q_BLOCK, axis=3) * scale
        t = t0 + jnp.arange(Q_BLOCK)
        lc = jnp.einsum('bghqd,bgcd->bghqc', qb, kc)
        pc = masked_softmax(lc, cmp_end[None, :] <= t[:, None])
        o_c = jnp.einsum('bghqc,bgcd->bghqd', pc.astype(vc.dtype), vc)
        imp = jnp.einsum('bghqc,cj->bgqj', pc, sel_map)
        cur = (t // SEL_LEN)[:, None]
        forced = (jsel[None, :] == 0) | (jsel[None, :] == cur) | (jsel[None, :] == cur - 1)
        valid = jsel[None, :] <= cur
        imp = jnp.where(valid, jnp.where(forced, FORCE_SCORE, imp), NEG)
        _, sel_idx = lax.top_k(imp, top)
        kg = ks_blk[bi, gi, sel_idx]
        vg = vs_blk[bi, gi, sel_idx]
        ls = jnp.einsum('bghqd,bgqnkd->bghqnk', qb, kg).reshape(b, g, hg, Q_BLOCK, top * SEL_LEN)
        kpos = sel_idx[..., None] * SEL_LEN + jnp.arange(SEL_LEN)
        ms = (kpos <= t[None, None, :, None, None]).reshape(b, g, 1, Q_BLOCK, top * SEL_LEN)
        ps = masked_softmax(ls, ms).reshape(b, g, hg, Q_BLOCK, top, SEL_LEN)
        o_s = jnp.einsum('bghqnk,bgqnkd->bghqd', ps.astype(vg.dtype), vg)
        kwb = lax.dynamic_slice_in_dim(kw_pad, t0, WINDOW + Q_BLOCK, axis=2)
        vwb = lax.dynamic_slice_in_dim(vw_pad, t0, WINDOW + Q_BLOCK, axis=2)
        spos = t0 - WINDOW + jnp.arange(WINDOW + Q_BLOCK)
        dist = t[:, None] - spos[None, :]
        mw = (dist >= 0) & (dist < WINDOW) & (spos[None, :] >= 0)
        lw = jnp.einsum('bghqd,bgkd->bghqk', qb, kwb)
        pw = masked_softmax(lw, mw)
        o_w = jnp.einsum('bghqk,bgkd->bghqd', pw.astype(vwb.dtype), vwb)
        return (o_c, o_s, o_w)

    o_c, o_s, o_w = lax.map(query_block, jnp.arange(s // Q_BLOCK))

    def unblock(o):
        return jnp.transpose(o, (1, 2, 3, 0, 4, 5)).reshape(b, g, hg, s, dh)

    return (unblock(o_c), unblock(o_s), unblock(o_w))


def setup_inputs(seed: int = 0) -> dict:
    key = jax.random.key(seed)
    ks = jax.random.split(key, 32)
    f32 = jnp.float32

    def nrm(k, shape, fan_in):
        return jax.random.normal(k, shape, f32) * (fan_in ** -0.5)

    def gain(k, shape):
        return 1.0 + 0.05 * jax.random.normal(k, shape, f32)

    def small(k, shape):
        return 0.02 * jax.random.normal(k, shape, f32)

    L = DEPTH
    return {
        "x": jax.random.normal(ks[0], (BATCH, SEQ, D_MODEL), f32),
        "p": jax.random.normal(ks[1], (DEPTH, BATCH, SEQ, PLE_DIM), f32),
        "g_mix": gain(ks[2], (L, D_MODEL)),
        "w_in": nrm(ks[3], (L, D_MODEL, N_IN), D_MODEL),
        "cmp_pos_k": small(ks[4], (L, CMP_LEN, HEAD_DIM)),
        "cmp_pos_v": small(ks[5], (L, CMP_LEN, HEAD_DIM)),
        "w_cmp_k1": nrm(ks[6], (L, CMP_LEN * HEAD_DIM, CMP_HIDDEN), CMP_LEN * HEAD_DIM),
        "w_cmp_k2": nrm(ks[7], (L, CMP_HIDDEN, HEAD_DIM), CMP_HIDDEN),
        "w_cmp_v1": nrm(ks[8], (L, CMP_LEN * HEAD_DIM, CMP_HIDDEN), CMP_LEN * HEAD_DIM),
        "w_cmp_v2": nrm(ks[9], (L, CMP_HIDDEN, HEAD_DIM), CMP_HIDDEN),
        "w_o_nsa": nrm(ks[10], (L, NSA_Q, D_MODEL), NSA_Q),
        "conv_w": nrm(ks[11], (L, CONV_WIDTH, CONV_CH), CONV_WIDTH),
        "conv_b": small(ks[12], (L, CONV_CH)),
        "conv_ln_g": gain(ks[13], (L, CONV_CH)),
        "conv_ln_b": small(ks[14], (L, CONV_CH)),
        "w_conv_out": nrm(ks[15], (L, CONV_CH, D_MODEL), CONV_CH),
        "b_conv_out": small(ks[16], (L, D_MODEL)),
        "w_out": nrm(ks[17], (L, D_MODEL, D_MODEL), D_MODEL),
        "g_ffn": gain(ks[18], (L, D_MODEL)),
        "w_up": nrm(ks[19], (L, D_MODEL, 2 * D_FF), D_MODEL),
        "ffn_conv_w": nrm(ks[20], (L, FFN_CONV_WIDTH, 2 * D_FF), FFN_CONV_WIDTH),
        "ffn_conv_b": small(ks[21], (L, 2 * D_FF)),
        "w_down": nrm(ks[22], (L, D_FF, D_MODEL), D_FF),
        "g_ple": gain(ks[23], (L, D_MODEL)),
        "w_ple_gate": nrm(ks[24], (L, D_MODEL, D_MODEL), D_MODEL),
        "w_ple": nrm(ks[25], (L, PLE_DIM, D_MODEL), PLE_DIM),
        "g_final": gain(ks[26], (D_MODEL,)),
    }


def reference(x, p, g_mix, w_in, cmp_pos_k, cmp_pos_v, w_cmp_k1, w_cmp_k2, w_cmp_v1, w_cmp_v2,
              w_o_nsa, conv_w, conv_b, conv_ln_g, conv_ln_b, w_conv_out, b_conv_out, w_out,
              g_ffn, w_up, ffn_conv_w, ffn_conv_b, w_down, g_ple, w_ple_gate, w_ple, g_final):
    b, s, _ = x.shape
    h = x
    for i in range(DEPTH):
        u = rmsnorm(h, g_mix[i])
        z = u @ w_in[i]
        zq, zkv, zg, zc, zm = jnp.split(z, IN_SPLITS, axis=-1)

        q = jnp.transpose(zq.reshape(b, s, NSA_GROUPS, NSA_HPG, HEAD_DIM), (0, 2, 3, 1, 4))
        kv = jnp.transpose(zkv.reshape(b, s, 6, NSA_GROUPS, HEAD_DIM), (2, 0, 3, 1, 4))
        kc = compress(kv[0], cmp_pos_k[i], w_cmp_k1[i], w_cmp_k2[i])
        vc = compress(kv[1], cmp_pos_v[i], w_cmp_v1[i], w_cmp_v2[i])
        o_c, o_s, o_w = nsa_attention(q, kc, vc, kv[2], kv[3], kv[4], kv[5])
        gates = jnp.transpose(jax.nn.sigmoid(zg.reshape(b, s, 3, NSA_GROUPS, NSA_HPG)),
                              (2, 0, 3, 4, 1))[..., None]
        o = gates[0] * o_c + gates[1] * o_s + gates[2] * o_w
        o = jnp.transpose(o, (0, 3, 1, 2, 4)).reshape(b, s, NSA_Q)
        y_a = o @ w_o_nsa[i]

        c_val, c_gate = jnp.split(zc, 2, axis=-1)
        c = c_val * jax.nn.sigmoid(c_gate)
        c = causal_dwconv(c, conv_w[i], conv_b[i])
        c = jax.nn.silu(layernorm(c, conv_ln_g[i], conv_ln_b[i]))
        y_b = c @ w_conv_out[i] + b_conv_out[i]

        ga, gb = jnp.split(jax.nn.sigmoid(zm), 2, axis=-1)
        h = h + (ga * y_a + gb * y_b) @ w_out[i]

        v = rmsnorm(h, g_ffn[i]) @ w_up[i]
        v = causal_dwconv(v, ffn_conv_w[i], ffn_conv_b[i])
        v_gate, v_val = jnp.split(v, 2, axis=-1)
        h = h + (jax.nn.gelu(v_gate) * v_val) @ w_down[i]

        gate = jax.nn.sigmoid(rmsnorm(h, g_ple[i]) @ w_ple_gate[i])
        h = h + gate * (p[i] @ w_ple[i])
    return rmsnorm(h, g_final)
```

```python
import contextlib
import numpy as np
import ml_dtypes
import concourse.bass as bass
import concourse.mybir as mybir
from concourse.bass_utils import run_bass_kernel_spmd

F32 = mybir.dt.float32
BF16 = mybir.dt.bfloat16
AF = mybir.ActivationFunctionType
ALU = mybir.AluOpType
AX = mybir.AxisListType

D = 1024
S = 4096
NB = 4
NCORES = 8
SEGLEN = 1024
NSEG = 2
NCTX = 5
NEXT = NCTX + SEGLEN // 128
HALO = 32
OWNW = HALO + SEGLEN
OWN = NSEG * OWNW
SEG_STARTS = {0: [0, 3072], 1: [1024, 2048]}
BIG = 30000.0
EPS = 1e-6
DFF = 2816
NIN = 4376
SAME_SYNC = True
NOSAME = ()
CQ, CKV, CG, CCV, CM = 0, 512, 1280, 1304, 2328


class Buf:
    __slots__ = ("name", "last_w", "readers", "dma_sem")

    def __init__(self, name):
        self.name = name
        self.last_w = None
        self.readers = []
        self.dma_sem = None


class Prog:
    def __init__(self, nc, same_engine_sync=True):
        self.nc = nc
        self.es = contextlib.ExitStack()
        self.cnt = {}
        self.waited = {e: {} for e in ("pe", "act", "dve", "pool", "sp")}
        self.sem = {}
        self.semobj = {}
        self.semcnt = {}
        for e in ("pe", "act", "dve", "pool"):
            s = self.es.enter_context(nc.semaphore("s_" + e))
            self.sem[e] = s
            self.semobj[("eng", e)] = s
            self.semcnt[("eng", e)] = 0
        self.same = same_engine_sync
        self.E = {"pe": nc.tensor, "act": nc.scalar, "dve": nc.vector, "pool": nc.gpsimd,
                  "sp": nc.sync}
        self.ndma = 0
        self.nbuf = 0
        self.ninstr = {e: 0 for e in self.E}

    def sb(self, name, shape, dt, es=None, side=None):
        return (es or self.es).enter_context(self.nc.sbuf_tensor("sb_" + name, shape, dt, side=side))

    def ps(self, name, shape, dt, es=None):
        return (es or self.es).enter_context(self.nc.psum_tensor("pp_" + name, shape, dt))

    def buf(self, name=None):
        self.nbuf += 1
        return Buf(name or f"b{self.nbuf}")

    def bufs(self, n, name="b"):
        return [self.buf(f"{name}{i}") for i in range(n)]

    def new_dma_sem(self):
        self.ndma += 1
        s = self.es.enter_context(self.nc.semaphore(f"sd{self.ndma}"))
        key = ("dma", self.ndma)
        self.semobj[key] = s
        self.semcnt[key] = 0
        return key

    def _deps(self, reads, writes):
        deps = []
        for b in reads:
            if b.last_w is not None:
                deps.append(b.last_w)
        for b in writes:
            if b.last_w is not None:
                deps.append(b.last_w)
            deps.extend(b.readers)
        return deps

    def _emit_waits(self, eng, deps):
        w = self.waited[eng]
        need = {}
        for key, val in deps:
            if key == ("eng", eng) and (eng == "pe" or eng in NOSAME):
                continue
            if w.get(key, 0) < val:
                need[key] = max(need.get(key, 0), val)
        for key, val in need.items():
            w[key] = val
            self.E[eng].wait_ge(self.semobj[key], val)

    def _stamp(self, stamp, reads, writes):
        for b in reads:
            b.readers.append(stamp)
        for b in writes:
            b.last_w = stamp
            b.readers = []

    def op(self, eng, fn, reads=(), writes=()):
        self._emit_waits(eng, self._deps(reads, writes))
        key = ("eng", eng)
        self.semcnt[key] += 1
        stamp = (key, self.semcnt[key])
        fn(self.E[eng]).then_inc(self.sem[eng], 1)
        self.ninstr[eng] += 1
        self._stamp(stamp, reads, writes)

    def dma(self, queue, out, in_, reads=(), writes=(), sembuf=None, **kw):
        b0 = sembuf or (writes[0] if writes else reads[0])
        if b0.dma_sem is None:
            b0.dma_sem = self.new_dma_sem()
        key = b0.dma_sem
        deps = [d for d in self._deps(reads, writes)
                if not (d[0] == key and any(b.last_w == d for b in writes))]
        self._emit_waits(queue, deps)
        self.semcnt[key] += 16
        stamp = (key, self.semcnt[key])
        self.E[queue].dma_start(out, in_, **kw).then_inc(self.semobj[key], 16)
        self.ninstr[queue] += 1
        self._stamp(stamp, reads, writes)

    def barrier(self):
        for e in ("pe", "act", "dve", "pool", "sp"):
            deps = [(k, v) for k, v in self.semcnt.items() if v > 0 and k != ("eng", e)]
            self._emit_waits(e, deps)

    def wait_bufs(self, eng, bufs):
        deps = []
        for b in bufs:
            if b.last_w is not None:
                deps.append(b.last_w)
            deps.extend(b.readers)
        self._emit_waits(eng, deps)


def _bf(a):
    return np.ascontiguousarray(a).astype(ml_dtypes.bfloat16)


def own_token_positions(role):
    pos = np.zeros(OWN, np.int64)
    for s, s0 in enumerate(SEG_STARTS[role]):
        pos[s * OWNW:(s + 1) * OWNW] = np.arange(s0 - HALO, s0 + SEGLEN)
    return pos


def host_consts(role):
    c = {}
    starts = SEG_STARTS[role]
    c["ident"] = _bf(np.eye(128, dtype=np.float32))
    c["identf"] = np.eye(128, dtype=np.float32)
    E = np.zeros((64, S), np.float32)
    E[np.arange(S) // 64, np.arange(S)] = 1.0
    c["E"] = _bf(E)
    k = np.arange(128)[:, None]
    q = np.arange(128)[None, :]
    c["mcausal"] = _bf(np.where(k <= q, 0.0, -BIG))
    c["mfar"] = _bf(np.where(k > q, 0.0, -BIG))
    n_cmp = 255
    c0 = np.arange(n_cmp) * 16
    j0 = np.arange(64) * 64
    lo = np.maximum(c0[:, None], j0[None, :])
    hi = np.minimum(c0[:, None] + 32, j0[None, :] + 64)
    sm = np.zeros((256, 64), np.float32)
    sm[:255] = np.maximum(hi - lo, 0) / 32.0
    c["selmap"] = _bf(sm.reshape(2, 128, 64).transpose(1, 0, 2))
    pos = own_token_positions(role)
    cidx = np.arange(256)
    cend = cidx * 16 + 31
    allowed = (cend[:, None] <= pos[None, :]) & (cidx[:, None] < 255)
    cm = np.where(allowed, 0.0, -BIG).astype(np.float32)
    c["cmask"] = _bf(cm.reshape(2, 128, OWN).transpose(1, 0, 2))
    j = np.arange(64)[None, :]
    cur = (pos // 64)[:, None]
    real = (pos >= 0)[:, None]
    valid = (j <= cur) & real
    forced = ((j == 0) | (j == cur) | (j == cur - 1)) & valid
    mul = (valid & ~forced).astype(np.float32)
    add = np.where(forced, 1e4, np.where(valid, 0.0, -1.0)).astype(np.float32)
    add = np.where(real, add, 0.0)
    dt = np.zeros(OWN, np.int64)
    for s, s0 in enumerate(starts):
        dt[s * OWNW: s * OWNW + HALO] = s0 // 128 - 1
        dt[s * OWNW + HALO:(s + 1) * OWNW] = (s0 + np.arange(SEGLEN)) // 128
    pre2 = np.where(j >= 2 * dt[:, None], -2 * BIG, -BIG).astype(np.float32)
    c["selc"] = np.ascontiguousarray(np.stack([mul, add, pre2], axis=1))
    vf = np.zeros((128, NSEG, NEXT), np.float32)
    for s, s0 in enumerate(starts):
        tpos = s0 - NCTX * 128 + np.arange(NEXT * 128)
        vf[:, s, :] = (tpos >= 0).astype(np.float32).reshape(NEXT, 128).T
    c["vflag"] = vf
    hv = np.zeros((128, NSEG), np.float32)
    for s, s0 in enumerate(starts):
        hv[:, s] = 1.0 if s0 > 0 else 0.0
    c["hvalid"] = hv
    return c


def build(debug=None, stop_after=None):
    debug = debug or set()
    nc = bass.Bass("TRN2", target_bir_lowering=False)
    P = Prog(nc, same_engine_sync=SAME_SYNC)
    dbg_out = {}
    ES = contextlib.ExitStack

    def din(name, shape, dt=F32):
        return nc.dram_tensor(name, list(shape), dt, kind="ExternalInput").ap()

    xc = din("xc", [S, D])
    xe = din("xe", [NSEG, NEXT * 128, D])
    pown = din("pown", [NSEG, SEGLEN, 256])
    w_in = din("w_in", [D, NIN])
    g_mix = din("g_mix", [D])
    pos_kT = din("pos_kT", [64, 32])
    pos_vT = din("pos_vT", [64, 32])
    w_k1 = din("w_k1", [2048, 256])
    w_k2 = din("w_k2", [256, 64])
    w_v1 = din("w_v1", [2048, 256])
    w_v2 = din("w_v2", [256, 64])
    w_o = din("w_o", [512, D])
    conv_wT = din("conv_wT", [128, 4, 31])
    conv_b = din("conv_b", [128, 4])
    ln_g = din("ln_g", [128, 4])
    ln_b = din("ln_b", [128, 4])
    w_co = din("w_co", [512, D])
    b_co = din("b_co", [128, 8])
    w_out = din("w_out", [D, D])
    g_ffn = din("g_ffn", [D])
    w_up = din("w_up", [D, 2 * DFF])
    fcw = din("fcw", [128, 44, 3])
    fcb = din("fcb", [128, 44])
    w_down = din("w_down", [DFF, D])
    g_ple = din("g_ple", [D])
    w_pg = din("w_pg", [D, D])
    w_ple = din("w_ple", [256, D])
    g_fin = din("g_fin", [D])
    c_ident = din("ident", [128, 128], BF16)
    c_identf = din("identf", [128, 128], F32)
    c_E = din("E", [64, S], BF16)
    c_mcausal = din("mcausal", [128, 128], BF16)
    c_mfar = din("mfar", [128, 128], BF16)
    c_selmap = din("selmap", [128, 2, 64], BF16)
    c_cmask = din("cmask", [128, 2, OWN], BF16)
    c_selc = din("selc", [OWN, 3, 64], F32)
    c_vflag = din("vflag", [128, NSEG, NEXT], F32)
    c_hvalid = din("hvalid", [128, NSEG], F32)
    out = nc.dram_tensor("out", [NSEG, SEGLEN, D], F32, kind="ExternalOutput").ap()
    h1_scr = nc.dram_tensor("h1_scr", [NSEG, OWNW, D], F32, kind="Internal").ap()
    uT_scr = nc.dram_tensor("uT_scr", [128, 8, OWN], BF16, kind="Internal").ap()
    cT_scr = nc.dram_tensor("cT_scr", [128, 4, OWN], BF16, kind="Internal").ap()
    bscr_u = [P.buf(f"uTscr{s}") for s in range(NSEG)]
    bscr_c = [P.buf(f"cTscr{s}") for s in range(NSEG)]

    def dump(name, shape, src_ap, reads, dt=F32):
        if name not in debug:
            return
        t = nc.dram_tensor("dbg_" + name, list(shape), dt, kind="ExternalOutput").ap()
        b = P.buf("dbg_" + name)
        P.dma("sp", t, src_ap, reads=reads, writes=[b])
        dbg_out[name] = b

    psb = [P.ps(f"ps{i}", [128, 512], F32) for i in range(8)]
    psbuf = [P.buf(f"ps{i}") for i in range(8)]

    ident = P.sb("ident", [128, 128], BF16)
    identf = P.sb("identf", [128, 128], F32)
    eps_t = P.sb("eps_t", [128, 1], F32)
    vflag = P.sb("vflag", [128, NSEG, NEXT], F32)
    hvalid = P.sb("hvalid", [128, NSEG], F32)
    bconst = P.buf("consts")
    P.dma("sp", ident[:], c_ident, writes=[bconst])
    P.dma("sp", identf[:], c_identf, writes=[bconst])
    P.dma("sp", vflag[:], c_vflag, writes=[bconst])
    P.dma("sp", hvalid[:], c_hvalid, writes=[bconst])
    P.op("dve", lambda E: E.memset(eps_t[:], EPS), writes=[bconst])

    oT_scr = nc.dram_tensor("oT_scr", [128, 4, OWN], BF16, kind="Internal").ap()
    boT = P.buf("oTscr")
    esL2 = ES()
    KsAug = [P.sb(f"KsAug{g}", [128, S], BF16, esL2) for g in range(2)]
    VsAug = P.sb("VsAug", [128, 32, 2, 65], BF16, esL2)
    KcAug = [P.sb(f"KcAug{g}", [128, 256], BF16, esL2) for g in range(2)]
    VcAug = P.sb("VcAug", [128, 2, 2, 65], BF16, esL2)
    KwAug = [P.sb(f"KwAug{g}", [128, NSEG, NEXT * 128], BF16, esL2) for g in range(2)]
    KxAug = [P.sb(f"KxAug{g}", [128, NSEG, NEXT * 128], BF16, esL2) for g in range(2)]
    VwAug = P.sb("VwAug", [128, NSEG, NEXT, 2, 65], BF16, esL2)
    VxAug = P.sb("VxAug", [128, NSEG, NEXT, 2, 65], BF16, esL2)
    QT = P.sb("QT", [128, 4, OWN], BF16, esL2)
    NOT_ = NSEG * 9
    gates = P.sb("gates", [128, NOT_, 24], F32, esL2)
    for g in range(2):
        P.op("pool", lambda E, g=g: E.memset(KcAug[g][:], 0.0), writes=[bconst])
        P.op("pool", lambda E, g=g: E.memset(KwAug[g][:], 0.0), writes=[bconst])
        P.op("dve", lambda E, g=g: E.memset(KxAug[g][:], 0.0), writes=[bconst])
    P.op("dve", lambda E: E.memset(VsAug[:], 1.0), writes=[bconst])
    P.op("dve", lambda E: E.memset(VcAug[:], 1.0), writes=[bconst])
    P.op("pool", lambda E: E.memset(VxAug[:], 1.0), writes=[bconst])
    P.op("pool", lambda E: E.memset(VwAug[:], 1.0), writes=[bconst])
    P.dma("sp", KsAug[0][64:128, :], c_E, writes=[bconst])
    P.dma("sp", KsAug[1][0:64, :], c_E, writes=[bconst])
    P.barrier()
    for g in range(2):
        P.op("dve", lambda E, g=g: E.tensor_copy(VwAug[:, :, :, g, 64], vflag[:]), reads=[bconst],
             writes=[bconst])
        P.op("dve", lambda E, g=g: E.tensor_copy(VxAug[:, :, :, g, 64], vflag[:]), reads=[bconst],
             writes=[bconst])

    esCP = ES()
    cpads = [P.sb(f"cpad{s}", [128, 4, 32 + OWNW], BF16, esCP) for s in range(NSEG)]
    bcpad = P.bufs(NSEG, "cpad")
    esL3 = ES()
    gmix_bc = P.sb("gmix_bc", [128, D], F32, esL3)
    P.dma("sp", gmix_bc[:], g_mix.partition_broadcast(128), writes=[bconst])
    Wkv = P.sb("Wkv", [128, 8, 768], BF16, esL3)
    bW1 = P.buf("Wkv")
    w_in_v = w_in.rearrange("(k p) n -> p k n", p=128)
    P.dma("pool", Wkv[:], w_in_v[:, :, CKV:CKV + 768], writes=[bW1])
    PS_T = 7
    psT = psb[PS_T].bitcast(BF16)

    psTs = {b: psb[b].bitcast(BF16) for b in (6, 7)}

    def make_front(es_, tag, with_xs=True, banks=(6, 7), NXS=2, NUTM=2, depth=1):
        pend = []
        xs = [P.sb(f"xs{tag}{i}", [128, D], F32, es_) for i in range(NXS)] if with_xs else None
        xs_b = P.bufs(NXS, "xs")
        sq_junk = P.sb("sq_junk" + tag, [128, D], BF16, es_)
        sq_b = P.buf("sqj")
        ss = [P.sb(f"ss{tag}{i}", [128, 4], F32, es_) for i in range(NXS)]
        ss_b = P.bufs(NXS, "ss")
        utm = [P.sb(f"utm{tag}{i}", [128, D], BF16, es_) for i in range(NUTM)]
        utm_b = P.bufs(NUTM, "utm")
        front_ctr = [0]

        def front_tile(x_rows_ap, dst_ap, dst_buf, g_bc, nrows=128, xres=None, hold=False):
            i = front_ctr[0]
            front_ctr[0] += 1
            s3 = i % NXS
            s2 = i % NUTM
            if xres is None:
                P.dma("sp", xs[s3][0:nrows, :], x_rows_ap, writes=[xs_b[s3]])
                xin, xb = xs[s3], xs_b[s3]
            else:
                xin, xb = xres
            P.op("act", lambda E: E.activation(sq_junk[0:nrows, :], xin[0:nrows, :], AF.Square,
                                               accum_out=ss[s3][0:nrows, 0:1]),
                 reads=[xb], writes=[sq_b, ss_b[s3]])
            P.op("act", lambda E: E.activation(ss[s3][0:nrows, 1:2], ss[s3][0:nrows, 0:1], AF.Sqrt,
                                               bias=eps_t[0:nrows, 0:1], scale=1.0 / D),
                 reads=[ss_b[s3], bconst], writes=[ss_b[s3]])
            P.op("dve", lambda E: E.reciprocal(ss[s3][0:nrows, 2:3], ss[s3][0:nrows, 1:2]), reads=[ss_b[s3]],
                 writes=[ss_b[s3]])
            P.op("dve", lambda E: E.scalar_tensor_tensor(utm[s2][0:nrows, :], xin[0:nrows, :], ss[s3][0:nrows, 2:3],
                                                         g_bc[0:nrows, :], ALU.mult, ALU.mult),
                 reads=[xb, ss_b[s3], bconst], writes=[utm_b[s2]])
            bki = banks[i % len(banks)]
            psT_ = psTs[bki]

            def stage_b():
                for k in range(8):
                    P.op("pe", lambda E, k=k: E.transpose(psT_[:, k * 128:k * 128 + nrows],
                                                          utm[s2][0:nrows, k * 128:(k + 1) * 128],
                                                          ident[0:nrows, 0:nrows]),
                         reads=[utm_b[s2], bconst], writes=[psbuf[bki]])
                P.op("act", lambda E: E.copy(dst_ap, psT_[:, :].rearrange("p (k t) -> p k t", k=8)[:, :, 0:nrows]),
                     reads=[psbuf[bki]], writes=[dst_buf])

            if not hold:
                while len(pend) >= depth:
                    pend.pop(0)()
            pend.append(stage_b)
            return ss[s3], ss_b[s3]

        def flush(n=None):
            k = 0
            while pend and (n is None or k < n):
                pend.pop(0)()
                k += 1

        front_tile.flush = flush
        return front_tile

    front_tile = make_front(esL3, "a", NXS=4, NUTM=4, depth=3)
    def proj_fm(lhs_fn, rhs_fn, n, bank, rbufs, nk=8):
        for k in range(nk):
            P.op("pe", lambda E, k=k: E.matmul(psb[bank][:, 0:n], lhs_fn(k), rhs_fn(k),
                                               start=(k == 0), stop=(k == nk - 1)),
                 reads=rbufs, writes=[psbuf[bank]])

    esL4 = ES()
    kcmpT = P.sb("kcmpT", [128, S], BF16, esL4)
    vcmpT = P.sb("vcmpT", [128, S], BF16, esL4)
    w1s = P.sb("w1s", [128, 32, 256], BF16, esL4)
    bw1s = P.buf("w1s")

    def load_w1(wsrc):
        v = wsrc.rearrange("(l d) h -> d l h", d=64)
        P.dma("pool", w1s[0:64], v, writes=[bw1s])
        P.dma("pool", w1s[64:128], v, writes=[bw1s])

    load_w1(w_k1)
    esL4b = ES()
    uTg = [P.sb(f"uTg{i}", [128, 8, 512], BF16, esL4b) for i in range(2)]
    uTg_b = P.bufs(2, "uTg")
    bctx = P.bufs(8, "ctx")
    def ph1_front(grp):
        ub = grp % 2
        for j in range(4):
            t = grp * 4 + j
            front_tile(xc[t * 128:(t + 1) * 128, :], uTg[ub][:, :, j * 128:(j + 1) * 128], uTg_b[ub], gmix_bc)

    def ph1_proj(grp):
        ub = grp % 2
        for ch, bank in ((0, 0), (1, 1), (2, 2)):
            proj_fm(lambda k, ch=ch: Wkv[:, k, ch * 128:(ch + 1) * 128], lambda k: uTg[ub][:, k, :],
                    512, bank, [bW1, uTg_b[ub]])
        for j in range(4):
            for k in range(8):
                P.op("pe", lambda E, j=j, k=k: E.matmul(psb[3][:, j * 128:(j + 1) * 128],
                                                        uTg[ub][:, k, j * 128:(j + 1) * 128],
                                                        Wkv[:, k, 384:512], start=(k == 0),
                                                        stop=(k == 7)),
                     reads=[bW1, uTg_b[ub]], writes=[psbuf[3]])

    def ph1_evac(grp):
        cs = slice(grp * 512, (grp + 1) * 512)
        kd_ = kcmpT[:, :].rearrange("p (s c) -> p s c", s=16)[:, :, grp * 32:(grp + 1) * 32]
        vd_ = vcmpT[:, :].rearrange("p (s c) -> p s c", s=16)[:, :, grp * 32:(grp + 1) * 32]
        P.op("act", lambda E: E.copy(kd_, psb[0][:, :].rearrange("p (c s) -> p s c", s=16)), reads=[psbuf[0]],
             writes=[bctx[grp]])
        P.op("dve", lambda E: E.tensor_copy(vd_, psb[1][:, :].rearrange("p (c s) -> p s c", s=16)),
             reads=[psbuf[1]], writes=[bctx[grp]])
        P.op("act", lambda E: E.copy(KsAug[0][0:64, cs], psb[2][0:64, :]), reads=[psbuf[2]],
             writes=[bctx[grp]])
        P.op("dve", lambda E: E.tensor_copy(KsAug[1][64:128, cs], psb[2][64:128, :]), reads=[psbuf[2]],
             writes=[bctx[grp]])
        P.op("dve", lambda E: E.tensor_copy(
            VsAug[:, grp * 4:(grp + 1) * 4, :, 0:64],
            psb[3][:, :].rearrange("p (j g d) -> p j g d", j=4, g=2)), reads=[psbuf[3]],
            writes=[bctx[grp]])

    ph1_front(0)
    front_tile.flush()
    for grp in range(8):
        ph1_proj(grp)
        if grp + 1 < 8:
            ph1_front(grp + 1)
        ph1_evac(grp)
        front_tile.flush()
    dump("KsAug0", [128, S], KsAug[0][:], bctx, BF16)
    dump("VsAug", [128, 32, 2, 65], VsAug[:], bctx, BF16)
    P.barrier()
    esL4b.close()
    if stop_after == "ph1":
        return nc, P, dbg_out

    es2 = ES()
    bw2 = P.buf("w_cmp")
    w1 = {"k": w1s, "v": w1s}
    w2 = P.sb("w2", [128, 2, 2, 128], BF16, es2)
    for ki, wsrc in enumerate((w_k2, w_v2)):
        v = wsrc.rearrange("(c p) d -> p c d", p=128)
        P.dma("pool", w2[:, :, ki, 0:64], v, writes=[bw2])
        P.dma("pool", w2[:, :, ki, 64:128], v, writes=[bw2])
    posT = P.sb("posT", [64, 2, 32], BF16, es2)
    P.dma("pool", posT[:, 0, :], pos_kT, writes=[bw2])
    P.dma("pool", posT[:, 1, :], pos_vT, writes=[bw2])
    hb = P.sb("hb", [128, 4], F32, es2)
    bhb = P.buf("hb")
    srcs = {"k": kcmpT, "v": vcmpT}
    hg = {}
    bhg = P.buf("hg")
    for kind in ("k", "v"):
        for g in range(2):
            for hc in range(2):
                hg[(kind, g, hc)] = P.sb(f"hg{kind}{g}{hc}", [128, 256], BF16, es2)
                P.op("pool", lambda E, t=hg[(kind, g, hc)]: E.memset(t[:], 0.0), writes=[bhg])
    cnt = 0
    if stop_after == "ph2a":
        P.barrier()
        return nc, P, dbg_out
    for ki, (kind, wsrc) in enumerate((("k", w_k1), ("v", w_v1))):
        if stop_after == "ph2b" and ki == 1:
            P.barrier()
            return nc, P, dbg_out
        if ki == 1:
            load_w1(wsrc)
        for hc in range(2):
            bank = (ki * 2 + hc) % 4
            for l in range(32):
                P.op("pe", lambda E, l=l, kind=kind, hc=hc, bank=bank, ki=ki: E.matmul(
                    psb[bank][:, 0:1], w1[kind][0:64, l, hc * 128:(hc + 1) * 128], posT[0:64, ki, l:l + 1],
                    start=(l == 0), stop=(l == 31)), reads=[bw2, bw1s], writes=[psbuf[bank]])
            P.op("act", lambda E, bank=bank, ki=ki, hc=hc: E.copy(hb[:, ki * 2 + hc:ki * 2 + hc + 1],
                                                                   psb[bank][:, 0:1]),
                 reads=[psbuf[bank]], writes=[bhb])
        src16 = srcs[kind][:, :].rearrange("p (s c) -> p s c", s=16)
        for g in range(2):
            gh = slice(g * 64, (g + 1) * 64)
            for hc in range(2):
                bank = cnt % 4
                cnt += 1
                for l in range(32):
                    P.op("pe", lambda E, l=l, kind=kind, hc=hc, bank=bank, gh=gh: E.matmul(
                        psb[bank][:, 0:255], w1[kind][gh, l, hc * 128:(hc + 1) * 128],
                        src16[gh, l % 16, l // 16:l // 16 + 255], start=(l == 0), stop=(l == 31)),
                        reads=[bw2, bw1s], writes=[psbuf[bank]])
                P.op("act", lambda E, bank=bank, kind=kind, g=g, hc=hc, ki=ki: E.activation(
                    hg[(kind, g, hc)][:, 0:255], psb[bank][:, 0:255], AF.Gelu_apprx_tanh,
                    bias=hb[:, ki * 2 + hc:ki * 2 + hc + 1]), reads=[psbuf[bank], bhb, bhg], writes=[bhg])
    bkc = P.buf("kc")
    if stop_after == "ph2c":
        P.barrier()
        return nc, P, dbg_out
    for g in range(2):
        gh = slice(g * 64, (g + 1) * 64)
        bank = 4 + g
        for hc in range(2):
            P.op("pe", lambda E, hc=hc, g=g, bank=bank: E.matmul(
                psb[bank][:, 0:256], w2[:, hc, 0, :], hg[("k", g, hc)][:, :], start=(hc == 0), stop=(hc == 1)),
                reads=[bw2, bhg], writes=[psbuf[bank]])
        P.op("act", lambda E, g=g, gh=gh, bank=bank: E.copy(KcAug[g][gh, :], psb[bank][gh, 0:256]),
             reads=[psbuf[bank]], writes=[bkc])
        for ct in range(2):
            bank2 = 6 + ct
            for hc in range(2):
                P.op("pe", lambda E, hc=hc, g=g, ct=ct, bank2=bank2: E.matmul(
                    psb[bank2][:, 0:64], hg[("v", g, hc)][:, ct * 128:(ct + 1) * 128], w2[:, hc, 1, 0:64],
                    start=(hc == 0), stop=(hc == 1)), reads=[bw2, bhg], writes=[psbuf[bank2]])
            P.op("dve", lambda E, g=g, ct=ct, bank2=bank2: E.tensor_copy(VcAug[:, ct, g, 0:64],
                                                                        psb[bank2][:, 0:64]),
                 reads=[psbuf[bank2]], writes=[bkc])
    dump("KcAug0", [128, 256], KcAug[0][:], [bkc], BF16)
    dump("KcAug1", [128, 256], KcAug[1][:], [bkc], BF16)
    dump("VcAug", [128, 2, 2, 65], VcAug[:], [bkc], BF16)
    P.barrier()
    es2.close()
    esL4.close()
    if stop_after == "ph2":
        return nc, P, dbg_out
    es3 = ES()
    uTx = [P.sb("uTx0", [128, 8, 512], BF16, es3), P.sb("uTx1", [128, 8, 128], BF16, es3)]
    uTx_b = P.bufs(2, "uTx")
    uT_seg = P.sb("uT_seg", [128, 8, OWNW], BF16, es3)
    bown = P.buf("own")
    Wq = P.sb("Wq", [128, 8, 512], BF16, es3)
    Wc = P.sb("Wc", [128, 8, 1024], BF16, es3)
    Wg = P.sb("Wg", [128, 8, 24], BF16, es3)
    bW3 = P.buf("W3")
    P.dma("pool", Wq[:], w_in_v[:, :, CQ:CQ + 512], writes=[bW3])
    P.dma("pool", Wc[:], w_in_v[:, :, CCV:CCV + 1024], writes=[bW3])
    P.dma("pool", Wg[:], w_in_v[:, :, CG:CG + 24], writes=[bW3])
    sig = [P.sb(f"sig{i}", [128, 512], F32, es3) for i in range(2)]
    sig_b = P.bufs(2, "sig")
    bext = P.buf("ext")
    bq = P.buf("QT")
    bgates = P.buf("gates")
    sigc = [0]
    bankrr = [0]

    def nextbank(lo=0, hi=6):
        b = lo + bankrr[0] % (hi - lo)
        bankrr[0] += 1
        return b

    for seg in range(NSEG):
        base = seg * OWNW
        egroups = ((0, 4), (4, 1), (5, 4), (9, 4))

        def ext_dst(ft, ntl):
            n = ntl * 128
            if ft >= NCTX:
                c0 = HALO + (ft - NCTX) * 128
                return uT_seg[:, :, c0:c0 + n], bown
            ub = (0 if ft == 0 else 1)
            return uTx[ub][:, :, 0:n], uTx_b[ub]

        def ext_front(ft, ntl):
            dstT, dstb = ext_dst(ft, ntl)
            for j in range(ntl):
                front_tile(xe[seg, (ft + j) * 128:(ft + j + 1) * 128, :], dstT[:, :, j * 128:(j + 1) * 128], dstb,
                           gmix_bc)

        def ext_proj(ft, ntl):
            n = ntl * 128
            dstT, dstb = ext_dst(ft, ntl)
            ecs = slice(ft * 128, ft * 128 + n)
            evacs = []
            if ft == 4:
                evacs.append(lambda: P.op("pool", lambda E: E.tensor_copy(uT_seg[:, :, 0:HALO], uTx[1][:, :, 96:128]),
                                          reads=[uTx_b[1]], writes=[bown]))
            for ch, dst in ((2, KxAug), (4, KwAug)):
                bank = nextbank()
                proj_fm(lambda k, ch=ch: Wkv[:, k, ch * 128:(ch + 1) * 128], lambda k: dstT[:, k, :], n, bank,
                        [bW1, dstb])

                def ev(dst=dst, bank=bank):
                    P.op("act", lambda E: E.copy(dst[0][0:64, seg, ecs], psb[bank][0:64, 0:n]),
                         reads=[psbuf[bank]], writes=[bext])
                    P.op("dve", lambda E: E.tensor_copy(dst[1][64:128, seg, ecs], psb[bank][64:128, 0:n]),
                         reads=[psbuf[bank]], writes=[bext])
                evacs.append(ev)
            for ch, dst in ((3, VxAug), (5, VwAug)):
                bank = nextbank()
                for j in range(ntl):
                    for k in range(8):
                        P.op("pe", lambda E, j=j, k=k, ch=ch, bank=bank: E.matmul(
                            psb[bank][:, j * 128:(j + 1) * 128], dstT[:, k, j * 128:(j + 1) * 128],
                            Wkv[:, k, ch * 128:(ch + 1) * 128], start=(k == 0), stop=(k == 7)),
                            reads=[bW1, dstb], writes=[psbuf[bank]])

                def ev2(dst=dst, bank=bank):
                    P.op("dve", lambda E: E.tensor_copy(
                        dst[:, seg, ft:ft + ntl, :, 0:64],
                        psb[bank][:, 0:n].rearrange("p (j g d) -> p j g d", j=ntl, g=2)),
                        reads=[psbuf[bank]], writes=[bext])
                evacs.append(ev2)
            return evacs

        ext_front(*egroups[0])
        front_tile.flush()
        for gi_, (ft, ntl) in enumerate(egroups):
            evs = ext_proj(ft, ntl)
            if gi_ + 1 < len(egroups):
                ext_front(*egroups[gi_ + 1])
            for ev_ in evs:
                ev_()
            front_tile.flush()
        cpad = cpads[seg]
        P.op("pool", lambda E: E.memset(cpad[:, :, 0:32], 0.0), writes=[bcpad[seg]])
        for (o0, n) in ((0, 352), (352, 352), (704, 352)):
            cs = slice(o0, o0 + n)
            gcs = slice(base + o0, base + o0 + n)
            for hh in range(4):
                bank = nextbank()
                proj_fm(lambda k, hh=hh: Wq[:, k, hh * 128:(hh + 1) * 128], lambda k: uT_seg[:, k, cs], n, bank,
                        [bW3, bown])
                P.op("act", lambda E, bank=bank, hh=hh: E.activation(QT[:, hh, gcs], psb[bank][:, 0:n], AF.Copy,
                                                                      scale=0.125),
                     reads=[psbuf[bank]], writes=[bq])
            for i in range(4):
                bv = nextbank()
                bg_ = nextbank()
                proj_fm(lambda k, i=i: Wc[:, k, i * 128:(i + 1) * 128], lambda k: uT_seg[:, k, cs], n, bv,
                        [bW3, bown])
                proj_fm(lambda k, i=i: Wc[:, k, 512 + i * 128:512 + (i + 1) * 128], lambda k: uT_seg[:, k, cs],
                        n, bg_, [bW3, bown])
                si = sigc[0] % 2
                sigc[0] += 1
                P.op("act", lambda E, si=si, bg_=bg_: E.activation(sig[si][:, 0:n], psb[bg_][:, 0:n], AF.Sigmoid),
                     reads=[psbuf[bg_]], writes=[sig_b[si]])
                P.op("dve", lambda E, si=si, bv=bv, i=i: E.tensor_tensor(
                    cpad[:, i, 32 + o0:32 + o0 + n], psb[bv][:, 0:n], sig[si][:, 0:n], ALU.mult),
                    reads=[psbuf[bv], sig_b[si]], writes=[bcpad[seg]])
        for ti in range(9):
            qw = HALO if ti == 0 else 128
            c0 = 0 if ti == 0 else HALO + (ti - 1) * 128
            bank = nextbank()
            for k in range(8):
                P.op("pe", lambda E, k=k, bank=bank: E.matmul(psb[bank][0:qw, 0:24], uT_seg[:, k, c0:c0 + qw],
                                                              Wg[:, k, :], start=(k == 0), stop=(k == 7)),
                     reads=[bW3, bown], writes=[psbuf[bank]])
            P.op("act", lambda E, bank=bank, ti=ti: E.activation(gates[0:qw, seg * 9 + ti, :], psb[bank][0:qw, 0:24],
                                                                  AF.Sigmoid),
                 reads=[psbuf[bank]], writes=[bgates])
        P.dma("sp", uT_scr[:, :, base:base + OWNW], uT_seg[:], reads=[bown], writes=[bscr_u[seg]])
    dump("QT", [128, 4, OWN], QT[:], [bq], BF16)
    dump("gates", [128, NOT_, 24], gates[:], [bgates])
    dump("KwAug0", [128, NSEG, NEXT * 128], KwAug[0][:], [bext], BF16)
    dump("KxAug1", [128, NSEG, NEXT * 128], KxAug[1][:], [bext], BF16)
    dump("VwAug", [128, NSEG, NEXT, 2, 65], VwAug[:], [bext], BF16)
    dump("cpad0", [128, 4, 32 + OWNW], cpads[0][:], [bcpad[0]], BF16)
    P.barrier()
    es3.close()
    esL3.close()
    if stop_after == "ph3":
        return nc, P, dbg_out
    esCF = ES()
    cw = P.sb("cw", [128, 4, 31], F32, esCF)
    cb = P.sb("cb", [128, 4], F32, esCF)
    lng = P.sb("lng", [128, 4], F32, esCF)
    lnb = P.sb("lnb", [128, 4], F32, esCF)
    bWc = P.buf("Wcf")
    P.dma("sp", cw[:], conv_wT, writes=[bWc])
    with nc.allow_non_contiguous_dma(reason="tiny per-channel vectors"):
        P.dma("sp", cb[:], conv_b, writes=[bWc])
        P.dma("sp", lng[:], ln_g, writes=[bWc])
        P.dma("sp", lnb[:], ln_b, writes=[bWc])
    ones_bf = P.sb("ones_bf", [128, 128], BF16, esCF)
    P.op("dve", lambda E: E.memset(ones_bf[:], 1.0), writes=[bWc])
    cconv = P.sb("cconv", [128, 4, OWNW], F32, esCF)
    cc16 = P.sb("cc16", [128, 4, OWNW], BF16, esCF)
    csq16 = P.sb("csq16", [128, 4, OWNW], BF16, esCF)
    cT_seg = P.sb("cT_seg", [128, 4, OWNW], BF16, esCF)
    bcc = P.bufs(4, "cc")
    bcTs = P.buf("cTs")
    diag = P.sb("diag", [128, 4, 31, 128], BF16, esCF)
    diag_bs = P.bufs(4, "diag")
    lnm = P.sb("lnm", [128, 512], F32, esCF)
    lnr = P.sb("lnr", [128, 512], F32, esCF)
    lnt = [P.sb(f"lnt{i}", [128, 512], F32, esCF) for i in range(2)]
    bln = P.buf("ln")
    lnt_b = P.bufs(2, "lnt")
    diagc = [0]
    groups = ((0, 352), (352, 352), (704, 352))
    for i in range(4):
        for k in range(31):
            P.op("dve", lambda E, i=i, k=k: E.tensor_scalar(diag[:, i, k, :], ident[:], cw[:, i, 30 - k:31 - k], None,
                                                            ALU.mult), reads=[bWc, bconst], writes=[])
    P.op("dve", lambda E: E.memset(lnm[:, 0:8], 0.0), writes=diag_bs)
    for seg in range(NSEG):
        base = seg * OWNW
        cpad = cpads[seg]
        for i in range(4):
            banks = (0, 1, 2) if i % 2 == 0 else (3, 4, 5)
            for k in range(31):
                for gi, (o0, n) in enumerate(groups):
                    P.op("pe", lambda E, i=i, k=k, gi=gi, o0=o0, n=n: E.matmul(
                        psb[banks[gi]][:, 0:n], diag[:, i, k, :], cpad[:, i, 32 + o0 - k:32 + o0 - k + n],
                        start=(k == 0), stop=(k == 30)), reads=[diag_bs[i], bcpad[seg]],
                        writes=[psbuf[banks[gi]]])
            for gi, (o0, n) in enumerate(groups):
                bk = banks[gi]
                P.op("act", lambda E, bk=bk, i=i, o0=o0, n=n: E.activation(
                    cconv[:, i, o0:o0 + n], psb[bk][:, 0:n], AF.Identity, bias=cb[:, i:i + 1]),
                    reads=[psbuf[bk], bWc], writes=[bcc[i]])
                P.op("act", lambda E, bk=bk, i=i, o0=o0, n=n: E.activation(
                    csq16[:, i, o0:o0 + n], psb[bk][:, 0:n], AF.Square, bias=cb[:, i:i + 1]),
                    reads=[psbuf[bk], bWc], writes=[bcc[i]])
                P.op("dve", lambda E, i=i, o0=o0, n=n: E.tensor_copy(cc16[:, i, o0:o0 + n], cconv[:, i, o0:o0 + n]),
                     reads=[bcc[i]], writes=[bcc[i]])
        dump(f"cconv{seg}", [128, 4, OWNW], cconv[:], bcc)
        for (o0, n) in groups:
            for i in range(4):
                P.op("pe", lambda E, i=i: E.matmul(psb[6][:, 0:n], ones_bf[:], cc16[:, i, o0:o0 + n],
                                                   start=(i == 0), stop=(i == 3)),
                     reads=[bWc, bcc[i]], writes=[psbuf[6]])
            for i in range(4):
                P.op("pe", lambda E, i=i: E.matmul(psb[7][:, 0:n], ones_bf[:], csq16[:, i, o0:o0 + n],
                                                   start=(i == 0), stop=(i == 3)),
                     reads=[bWc, bcc[i]], writes=[psbuf[7]])
            P.op("dve", lambda E: E.tensor_scalar(lnm[:, 0:n], psb[6][:, 0:n], 1.0 / 512, None, ALU.mult),
                 reads=[psbuf[6]], writes=[bln])
            P.op("dve", lambda E: E.tensor_tensor(lnr[:, 0:n], lnm[:, 0:n], lnm[:, 0:n], ALU.mult),
                 reads=[bln], writes=[bln])
            P.op("dve", lambda E: E.scalar_tensor_tensor(lnr[:, 0:n], psb[7][:, 0:n], 1.0 / 512, lnr[:, 0:n],
                                                         ALU.mult, ALU.subtract),
                 reads=[psbuf[7], bln], writes=[bln])
            P.op("act", lambda E: E.activation(lnr[:, 0:n], lnr[:, 0:n], AF.Sqrt, bias=eps_t[:, 0:1]),
                 reads=[bln, bconst], writes=[bln])
            P.op("dve", lambda E: E.reciprocal(lnr[:, 0:n], lnr[:, 0:n]), reads=[bln], writes=[bln])
            for i in range(4):
                ti_ = i % 2
                P.op("dve", lambda E, i=i, ti_=ti_: E.tensor_tensor(lnt[ti_][:, 0:n], cconv[:, i, o0:o0 + n],
                                                                    lnm[:, 0:n], ALU.subtract),
                     reads=[bcc[i], bln], writes=[lnt_b[ti_]])
                P.op("dve", lambda E, ti_=ti_: E.tensor_tensor(lnt[ti_][:, 0:n], lnt[ti_][:, 0:n], lnr[:, 0:n],
                                                               ALU.mult),
                     reads=[bln, lnt_b[ti_]], writes=[lnt_b[ti_]])
                P.op("act", lambda E, i=i, ti_=ti_: E.activation(
                    cT_seg[:, i, o0:o0 + n], lnt[ti_][:, 0:n], AF.Silu, bias=lnb[:, i:i + 1],
                    scale=lng[:, i:i + 1]), reads=[lnt_b[ti_], bWc], writes=[bcTs])
        P.dma("sp", cT_scr[:, :, base:base + OWNW], cT_seg[:], reads=[bcTs], writes=[bscr_c[seg]])
        dump(f"cT{seg}", [128, 4, OWNW], cT_seg[:], [bcTs], BF16)
    P.barrier()
    esCF.close()
    esCP.close()
    if stop_after == "conf":
        return nc, P, dbg_out
    esP2c = ES()
    fcw_t = P.sb("fcw_t", [128, 44, 3], F32, esP2c, side="right")
    fcb_t = P.sb("fcb_t", [128, 44], F32, esP2c, side="right")
    gffn_bc = P.sb("gffn_bc", [128, D], F32, esP2c, side="right")
    gple_bc = P.sb("gple_bc", [128, D], F32, esP2c, side="right")
    gfin_bc = P.sb("gfin_bc", [128, D], F32, esP2c, side="right")
    bWp2 = P.buf("Wp2")
    P.dma("sp", fcw_t[:], fcw, writes=[bWp2])
    with nc.allow_non_contiguous_dma(reason="tiny per-channel vector"):
        P.dma("sp", fcb_t[:], fcb, writes=[bWp2])
    P.dma("sp", gffn_bc[:], g_ffn.partition_broadcast(128), writes=[bWp2])
    P.dma("sp", gple_bc[:], g_ple.partition_broadcast(128), writes=[bWp2])
    P.dma("sp", gfin_bc[:], g_fin.partition_broadcast(128), writes=[bWp2])
    esP1w = ES()
    Wo = P.sb("Wo", [128, 4, D], BF16, esP1w, side="right")
    Wco = P.sb("Wco", [128, 4, D], BF16, esP1w, side="right")
    Wm = P.sb("Wm", [128, 8, 2048], BF16, esP1w, side="right")
    Wout = P.sb("Wout", [128, 8, D], BF16, esP1w, side="right")
    bco_t = P.sb("bco_t", [128, 8], F32, esP1w, side="right")
    bWp1 = P.buf("Wp1")
    bWp1b = P.buf("Wp1b")
    P.dma("pool", Wo[:], w_o.rearrange("(k p) n -> p k n", p=128), writes=[bWp1])
    P.dma("pool", Wco[:], w_co.rearrange("(k p) n -> p k n", p=128), writes=[bWp1])
    P.dma("pool", Wm[:, :, 0:1024], w_in_v[:, :, CM:CM + 1024], writes=[bWp1])
    P.dma("pool", Wm[:, :, 1024:2048], w_in_v[:, :, CM + 1024:CM + 2048], writes=[bWp1])
    P.dma("pool", Wout[:], w_out.rearrange("(k p) n -> p k n", p=128), writes=[bWp1])
    with nc.allow_non_contiguous_dma(reason="tiny per-channel vector"):
        P.dma("sp", bco_t[:], b_co, writes=[bWp1b])
    esT = ES()
    bA = P.buf("attc")
    Qaug = [[P.sb(f"Qaug{g}_{i}", [128, 4, 128], BF16, esT) for i in range(2)] for g in range(2)]
    Qq_b = [P.bufs(2, f"Qq{g}") for g in range(2)]
    Qm_b = [P.bufs(2, f"Qm{g}") for g in range(2)]
    for g in range(2):
        for i in range(2):
            P.op("pool", lambda E, g=g, i=i: E.memset(Qaug[g][i][:], 0.0), writes=[Qq_b[g][i], Qm_b[g][i]])
    NPT = 4
    PT = [P.sb(f"PT{i}", [128, 512], BF16, esT) for i in range(NPT)]
    PT_b = P.bufs(NPT, "PT")
    cmk = [P.sb(f"cmk{i}", [128, 2, 128], BF16, esT) for i in range(2)]
    cmk_b = P.bufs(2, "cmk")
    slc = [P.sb(f"slc{i}", [128, 3, 64], F32, esT) for i in range(2)]
    slc_b = P.bufs(2, "slc")
    mcaus = P.sb("mcaus", [128, 128], BF16, esT)
    mfar = P.sb("mfar", [128, 128], BF16, esT)
    selmap = P.sb("selmap", [128, 2, 64], BF16, esT)
    zeros_bf = P.sb("zeros_bf", [128, 320], BF16, esT)
    P.dma("sp", mcaus[:], c_mcausal, writes=[bA])
    P.dma("sp", mfar[:], c_mfar, writes=[bA])
    P.dma("sp", selmap[:], c_selmap, writes=[bA])
    P.op("pool", lambda E: E.memset(zeros_bf[:], 0.0), writes=[bA])
    obf = [P.sb(f"obf{i}", [128, 512], BF16, esT) for i in range(2)]
    obf_b = P.bufs(2, "obf")
    oacc = P.sb("oacc", [128, 2, 256], F32, esT)
    otmp = P.sb("otmp", [128, 2, 256], F32, esT)
    rden = P.sb("rden", [128, 2, 3, 4], F32, esT)
    scl = P.sb("scl", [128, 2, 3, 4], F32, esT)
    imps = P.sb("imps", [128, 2, 64], F32, esT)
    impadj = P.sb("impadj", [128, 2, 64], F32, esT)
    tmpm = P.sb("tmpm", [128, 2, 64], F32, esT)
    self_ = P.sb("self_", [128, 2, 64], F32, esT)
    m8a = P.sb("m8a", [128, 2, 8], F32, esT)
    m8b = P.sb("m8b", [128, 2, 8], F32, esT)
    mnegw = P.sb("mnegw", [128, 2, 128], F32, esT)
    bsel = P.bufs(2, "sel")
    bcomb = P.bufs(2, "comb")
    oTt = [P.sb(f"oTt{i}", [128, 4, 128], BF16, esT) for i in range(2)]
    oTt_b = P.bufs(2, "oTt")
    otc = [0]
    S_BANKS = (0, 1, 2)
    OC, OS, OW, MISC, TR = 3, 4, 5, 6, 7
    sctr = [0]
    pctr = [0]
    psT7 = psb[TR].bitcast(BF16)
    attn_tiles = []
    for seg in range(NSEG):
        for ti in range(9):
            attn_tiles.append((seg, ti))
    if stop_after == "att1":
        attn_tiles = attn_tiles[:3]
    tcount = 0
    units_meta = []
    LA = 3
    pending = []
    deferred = []

    def push(s_fn, pv_fn):
        pi = s_fn()
        pending.append((pv_fn, pi))
        while len(pending) > LA:
            f, p_ = pending.pop(0)
            f(p_)

    def flush():
        while pending:
            f, p_ = pending.pop(0)
            f(p_)

    for (seg, ti) in attn_tiles:
        qw = HALO if ti == 0 else 128
        qc0 = seg * OWNW + (0 if ti == 0 else HALO + (ti - 1) * 128)
        T = NCTX - 1 if ti == 0 else NCTX + ti - 1
        qoff = 128 - HALO if ti == 0 else 0
        n_main = max(max(SEG_STARTS[r][seg] // 128 + (ti - 1 if ti >= 1 else -1), 0) for r in (0, 1))
        tidx = seg * 9 + ti
        sl = tcount % 2
        tcount += 1
        units_meta.append((seg, ti, qw, qc0, T, qoff, n_main, tidx, sl))

    def make_unit(seg, ti, qw, qc0, T, qoff, n_main, tidx, sl, g):
        gh = slice(g * 64, (g + 1) * 64)
        oh = slice((1 - g) * 64, (2 - g) * 64)
        Qa = Qaug[g][sl]
        qb, mb = Qq_b[g][sl], Qm_b[g][sl]
        rhsQ = Qa[:, :, 0:qw]
        bs_ = bsel[g]
        bc_ = bcomb[g]

        def s_step(lhsT, qbufs, mask=None, mbufs=()):
            sbk = S_BANKS[sctr[0] % len(S_BANKS)]
            sctr[0] += 1
            o3 = psb[sbk][:, 0:4 * qw].rearrange("p (h q) -> p h q", h=4)
            P.op("pe", lambda E: E.matmul(o3, lhsT, rhsQ, start=True, stop=(mask is None)),
                 reads=list(qbufs), writes=[psbuf[sbk]])
            if mask is not None:
                P.op("pe", lambda E: E.matmul(o3, ident[:], mask.unsqueeze(1).broadcast_to([128, 4, qw]),
                                              start=False, stop=True),
                     reads=[bconst, bA] + list(mbufs), writes=[psbuf[sbk]])
            pi = pctr[0] % NPT
            pctr[0] += 1
            P.op("act", lambda E: E.activation(PT[pi][:, 0:4 * qw], psb[sbk][:, 0:4 * qw], AF.Exp),
                 reads=[psbuf[sbk]], writes=[PT_b[pi]])
            return pi

        def zero_bank(bank, ncols):
            P.op("pe", lambda E: E.matmul(psb[bank][0:qw, 0:ncols], zeros_bf[:, 0:qw], zeros_bf[:, 0:ncols],
                                          start=True, stop=False, skip_group_check=True),
                 reads=[bA], writes=[psbuf[bank]])

        def pv(bank, pi, V_ap, last, w=65):
            for h in range(4):
                P.op("pe", lambda E, h=h: E.matmul(psb[bank][0:qw, h * w:(h + 1) * w],
                                                   PT[pi][:, h * qw:(h + 1) * qw], V_ap,
                                                   start=False, stop=last, skip_group_check=True),
                     reads=[PT_b[pi]], writes=[psbuf[bank]])

        def bview(bank):
            return psb[bank][0:qw, 0:260].rearrange("p (h e) -> p h e", e=65)[:, :, 0:64]

        def sview(br):
            return scl[0:qw, g, br, :].unsqueeze(2).broadcast_to([qw, 4, 64])

        oa = oacc[0:qw, g, :].rearrange("p (h d) -> p h d", h=4)
        ot = otmp[0:qw, g, :].rearrange("p (h d) -> p h d", h=4)
        ob = obf[sl][0:qw, g * 256:(g + 1) * 256].rearrange("p (h d) -> p h d", h=4)

        def den_scale(br, bank):
            bv = psb[bank][0:qw, 0:260].rearrange("p (h e) -> p h e", e=65)
            if br != 0:
                P.op("dve", lambda E: E.tensor_scalar(rden[0:qw, g, br, :], bv[:, :, 64], 1e-30, None, ALU.add),
                     reads=[psbuf[bank], bs_], writes=[bs_])
                P.op("dve", lambda E: E.reciprocal(rden[0:qw, g, br, :], rden[0:qw, g, br, :]),
                     reads=[bs_], writes=[bs_])
            P.op("dve", lambda E: E.tensor_tensor(
                scl[0:qw, g, br, :], rden[0:qw, g, br, :],
                gates[0:qw, tidx, br * 8 + g * 4:br * 8 + g * 4 + 4], ALU.mult),
                reads=[bs_], writes=[bs_])

        def cmp_pv(ct):
            def f(pi):
                if ct == 0:
                    zero_bank(OC, 260)
                    zero_bank(MISC, 256)
                pv(OC, pi, VcAug[:, ct, g, :], ct == 1)
                pv(MISC, pi, selmap[:, ct, :], ct == 1, w=64)
                if ct == 1:
                    selection()
            return f

        def selection():
            ocv = psb[OC][0:qw, 0:260].rearrange("p (h e) -> p h e", e=65)
            P.op("dve", lambda E: E.tensor_scalar(rden[0:qw, g, 0, :], ocv[:, :, 64], 1e-30, None, ALU.add),
                 reads=[psbuf[OC]], writes=[bs_])
            P.op("dve", lambda E: E.reciprocal(rden[0:qw, g, 0, :], rden[0:qw, g, 0, :]), reads=[bs_],
                 writes=[bs_])
            P.op("dve", lambda E: E.tensor_scalar(imps[0:qw, g, :], psb[MISC][0:qw, 0:64], rden[0:qw, g, 0, 0:1],
                                                  None, ALU.mult), reads=[psbuf[MISC], bs_], writes=[bs_])
            for h in range(1, 4):
                P.op("dve", lambda E, h=h: E.scalar_tensor_tensor(
                    imps[0:qw, g, :], psb[MISC][0:qw, h * 64:(h + 1) * 64], rden[0:qw, g, 0, h:h + 1],
                    imps[0:qw, g, :], ALU.mult, ALU.add), reads=[psbuf[MISC], bs_], writes=[bs_])
            P.op("dve", lambda E: E.tensor_tensor(impadj[0:qw, g, :], imps[0:qw, g, :], slc[sl][0:qw, 0, :],
                                                  ALU.mult), reads=[bs_, slc_b[sl]], writes=[bs_])
            P.op("dve", lambda E: E.tensor_tensor(impadj[0:qw, g, :], impadj[0:qw, g, :], slc[sl][0:qw, 1, :],
                                                  ALU.add), reads=[bs_, slc_b[sl]], writes=[bs_])
            P.op("dve", lambda E: E.max(m8a[0:qw, g, :], impadj[0:qw, g, :]), reads=[bs_], writes=[bs_])
            P.op("dve", lambda E: E.match_replace(tmpm[0:qw, g, :], m8a[0:qw, g, :], impadj[0:qw, g, :], -1e9),
                 reads=[bs_], writes=[bs_])
            P.op("dve", lambda E: E.max(m8b[0:qw, g, :], tmpm[0:qw, g, :]), reads=[bs_], writes=[bs_])
            P.op("dve", lambda E: E.tensor_scalar(self_[0:qw, g, :], impadj[0:qw, g, :], m8b[0:qw, g, 7:8], None,
                                                  ALU.is_ge), reads=[bs_], writes=[bs_])
            for half in range(2):
                P.op("dve", lambda E, half=half: E.scalar_tensor_tensor(
                    mnegw[0:qw, g, half * 64:(half + 1) * 64], self_[0:qw, g, :], BIG, slc[sl][0:qw, 2, :],
                    ALU.mult, ALU.add), reads=[bs_, slc_b[sl]], writes=[bs_])
            den_scale(0, OC)
            P.op("dve", lambda E: E.tensor_tensor(oa, bview(OC), sview(0), ALU.mult),
                 reads=[psbuf[OC], bs_], writes=[bc_])


        def c_part():
            if g == 0:
                P.dma("sp", cmk[sl][:, :, 0:qw], c_cmask[:, :, qc0:qc0 + qw], writes=[cmk_b[sl]])
                P.dma("sp", slc[sl][0:qw], c_selc[qc0:qc0 + qw], writes=[slc_b[sl]])
            P.op("pool", lambda E: E.tensor_copy(Qa[gh, :, 0:qw], QT[gh, :, qc0:qc0 + qw]), writes=[qb])
            for ct in range(2):
                push(lambda ct=ct: s_step(KcAug[g][:, ct * 128:(ct + 1) * 128], [qb], cmk[sl][:, ct, 0:qw],
                                          [cmk_b[sl]]), cmp_pv(ct))
            while deferred:
                deferred.pop(0)()

        def w_part():
            wsteps = [(T - 4, mfar)] + [(T - j, None) for j in (3, 2, 1)] + [(T, mcaus)]

            def win_pv(wi, kt):
                def f(pi):
                    if wi == 0:
                        zero_bank(OW, 260)
                    last = wi == len(wsteps) - 1
                    pv(OW, pi, VwAug[:, seg, kt, g, :], last)
                    if last:
                        den_scale(2, OW)
                        P.op("dve", lambda E: E.tensor_tensor(ot, bview(OW), sview(2), ALU.mult),
                             reads=[psbuf[OW], bs_], writes=[bc_])
                        P.op("dve", lambda E: E.tensor_tensor(oa, oa, ot, ALU.add), reads=[bc_], writes=[bc_])
                return f

            for wi, (kt, msk) in enumerate(wsteps):
                m_ap = None if msk is None else msk[:, qoff:qoff + qw]
                push(lambda kt=kt, m_ap=m_ap: s_step(KwAug[g][:, seg, kt * 128:(kt + 1) * 128], [qb], m_ap),
                     win_pv(wi, kt))

        def s_part():
            P.op("pe", lambda E: E.transpose(psb[MISC][:, 256:256 + qw], mnegw[0:qw, g, :], identf[0:qw, 0:qw]),
                 reads=[bs_, bconst], writes=[psbuf[MISC]])
            P.op("act", lambda E: E.copy(Qa[oh, :, 0:qw],
                                         psb[MISC][oh, 256:256 + qw].unsqueeze(1).broadcast_to([64, 4, qw])),
                 reads=[psbuf[MISC]], writes=[mb])
            def sel_pv(idx, V_ap):
                def f(pi):
                    if idx == 0:
                        zero_bank(OS, 260)
                    last = idx == n_main
                    pv(OS, pi, V_ap, last)
                    if last:
                        den_scale(1, OS)
                        P.op("dve", lambda E: E.tensor_tensor(ot, bview(OS), sview(1), ALU.mult),
                             reads=[psbuf[OS], bs_], writes=[bc_])
                        P.op("dve", lambda E: E.tensor_tensor(ob, oa, ot, ALU.add), reads=[bc_],
                             writes=[bc_, obf_b[sl]])
                        if g == 1:
                            deferred.append(o_transposes)
                return f

            def o_transposes():
                for c in range(4):
                    P.op("pe", lambda E, c=c: E.transpose(psT7[:, 512 + c * 128:512 + c * 128 + qw],
                                                          obf[sl][0:qw, c * 128:(c + 1) * 128], ident[0:qw, 0:qw]),
                         reads=[obf_b[sl], bconst], writes=[psbuf[TR]])
                oi = otc[0] % 2
                otc[0] += 1
                P.op("act", lambda E: E.copy(oTt[oi][:, :, 0:qw],
                                             psT7[:, 512:1024].rearrange("p (c t) -> p c t", c=4)[:, :, 0:qw]),
                     reads=[psbuf[TR]], writes=[oTt_b[oi]])
                P.dma("sp", oT_scr[:, :, qc0:qc0 + qw], oTt[oi][:, :, 0:qw], reads=[oTt_b[oi]], writes=[boT],
                      sembuf=oTt_b[oi])

            for kt in range(n_main):
                push(lambda kt=kt: s_step(KsAug[g][:, kt * 128:(kt + 1) * 128], [qb, mb]),
                     sel_pv(kt, VsAug[:, kt, g, :]))
            push(lambda: s_step(KxAug[g][:, seg, T * 128:(T + 1) * 128], [qb, mb], mcaus[:, qoff:qoff + qw]),
                 sel_pv(n_main, VxAug[:, seg, T, g, :]))


        return c_part, w_part, s_part

    units = [make_unit(*m, g) for m in units_meta for g in range(2)]
    for ui, (cp, wp, sp_) in enumerate(units):
        if ui == 0:
            cp()
        wp()
        if ui + 1 < len(units):
            units[ui + 1][0]()
        sp_()
    flush()
    while deferred:
        deferred.pop(0)()
    if "oT" in debug:
        t_ = nc.dram_tensor("dbg_oT", [128, 4, OWN], BF16, kind="ExternalOutput").ap()
        P.dma("sp", t_, oT_scr, reads=[boT], writes=[P.buf("dbg_oT")])
    P.barrier()
    esT.close()
    esL2.close()
    if stop_after in ("att", "att1"):
        return nc, P, dbg_out
    esP2w = ES()
    NJ = 11
    WPIECES = ((0, 4), (4, 4), (8, 3))
    Wup = P.sb("Wup", [128, 8, 2 * NJ * 128], BF16, esP2w)
    bWup = P.bufs(len(WPIECES), "Wup")
    w_up_v = w_up.rearrange("(k p) n -> p k n", p=128)

    def load_wup(half):
        j0_ = half * NJ
        for pi_, (pj, pn) in enumerate(WPIECES):
            P.dma("pool", Wup[:, :, pj * 128:(pj + pn) * 128],
                  w_up_v[:, :, (j0_ + pj) * 128:(j0_ + pj + pn) * 128], writes=[bWup[pi_]])
            P.dma("pool", Wup[:, :, (NJ + pj) * 128:(NJ + pj + pn) * 128],
                  w_up_v[:, :, DFF + (j0_ + pj) * 128:DFF + (j0_ + pj + pn) * 128], writes=[bWup[pi_]])

    esP1 = ES()
    load_wup(0)
    oTr = [P.sb(f"oTr{i}", [128, 4, 512], BF16, esP1) for i in range(2)]
    oTr_b = P.bufs(2, "oTr")
    uTt = [P.sb(f"uTt{i}", [128, 8, 512], BF16, esP1) for i in range(2)]
    uTt_b = P.bufs(2, "uTt")
    cTt = [P.sb(f"cTt{i}", [128, 4, 512], BF16, esP1) for i in range(2)]
    cTt_b = P.bufs(2, "cTt")
    mT = P.sb("mT", [128, 8, 512], BF16, esP1)
    mT_b = P.bufs(8, "mT")
    gab = [P.sb(f"gab{i}", [128, 2, 512], F32, esP1) for i in range(2)]
    gab_b = P.bufs(2, "gab")
    t12 = [P.sb(f"t12{i}", [128, 2, 512], F32, esP1) for i in range(2)]
    t12_b = P.bufs(2, "t12")
    xres = [P.sb(f"xres{i}", [128, D], F32, esP1) for i in range(2)]
    xres_b = P.bufs(2, "xres")
    h1t = [P.sb(f"h1t{i}", [128, D], F32, esP1) for i in range(2)]
    h1t_b = P.bufs(2, "h1t")
    bh1s = [P.buf(f"h1scr{s}") for s in range(NSEG)]
    brr = [0]

    def nb():
        b = brr[0] % 8
        brr[0] += 1
        return b

    gcnt = 0
    fcnt = 0
    tcnt = 0
    p1groups = [(seg, o0, n) for seg in range(NSEG) for (o0, n) in ((0, HALO + 256), (HALO + 256, 384), (HALO + 640, 384))]

    def p1_tiles(o0, n):
        tl = []
        c = 0
        if o0 == 0:
            tl.append((0, HALO))
            c = HALO
        while c < n:
            tl.append((c, 128))
            c += 128
        return tl

    def p1_loads(gidx):
        seg_, o0_, n_ = p1groups[gidx]
        base_ = seg_ * OWNW
        gi_ = gidx % 2
        P.dma("sp", uTt[gi_][:, :, 0:n_], uT_scr[:, :, base_ + o0_:base_ + o0_ + n_], reads=[bscr_u[seg_]],
              writes=[uTt_b[gi_]])
        P.dma("sp", cTt[gi_][:, :, 0:n_], cT_scr[:, :, base_ + o0_:base_ + o0_ + n_], reads=[bscr_c[seg_]],
              writes=[cTt_b[gi_]])
        P.dma("sp", oTr[gi_][:, :, 0:n_], oT_scr[:, :, base_ + o0_:base_ + o0_ + n_], reads=[boT],
              writes=[oTr_b[gi_]])

    p1_loads(0)
    for gidx, (seg, o0, n) in enumerate(p1groups):
        base = seg * OWNW
        if True:
            gi = gcnt % 2
            gcnt += 1
            if gidx + 1 < len(p1groups):
                p1_loads(gidx + 1)
            for f in range(8):
                fs = slice(f * 128, (f + 1) * 128)
                bA_, bB_, bC_, bD_ = nb(), nb(), nb(), nb()
                proj_fm(lambda k: Wo[:, k, fs], lambda k: oTr[gi][:, k, 0:n], n, bA_, [bWp1, oTr_b[gi]], nk=4)
                proj_fm(lambda k: Wco[:, k, fs], lambda k: cTt[gi][:, k, 0:n], n, bB_, [bWp1, cTt_b[gi]], nk=4)
                proj_fm(lambda k: Wm[:, k, fs], lambda k: uTt[gi][:, k, 0:n], n, bC_, [bWp1, uTt_b[gi]])
                proj_fm(lambda k: Wm[:, k, 1024 + f * 128:1024 + (f + 1) * 128], lambda k: uTt[gi][:, k, 0:n], n, bD_,
                        [bWp1, uTt_b[gi]])
                fi = fcnt % 2
                fcnt += 1
                P.op("act", lambda E: E.activation(gab[fi][:, 0, 0:n], psb[bC_][:, 0:n], AF.Sigmoid),
                     reads=[psbuf[bC_]], writes=[gab_b[fi]])
                P.op("act", lambda E: E.activation(gab[fi][:, 1, 0:n], psb[bD_][:, 0:n], AF.Sigmoid),
                     reads=[psbuf[bD_]], writes=[gab_b[fi]])
                P.op("dve", lambda E: E.tensor_tensor(t12[fi][:, 0, 0:n], psb[bA_][:, 0:n], gab[fi][:, 0, 0:n], ALU.mult),
                     reads=[psbuf[bA_], gab_b[fi]], writes=[t12_b[fi]])
                P.op("dve", lambda E: E.scalar_tensor_tensor(t12[fi][:, 1, 0:n], psb[bB_][:, 0:n], bco_t[:, f:f + 1],
                                                             gab[fi][:, 1, 0:n], ALU.add, ALU.mult),
                     reads=[psbuf[bB_], gab_b[fi], bWp1b], writes=[t12_b[fi]])
                P.op("pool", lambda E: E.tensor_tensor(mT[:, f, 0:n], t12[fi][:, 0, 0:n], t12[fi][:, 1, 0:n], ALU.add),
                     reads=[t12_b[fi]], writes=[mT_b[f]])
            tiles = p1_tiles(o0, n)
            erow0 = NCTX * 128 - HALO + o0
            t0_, w0_ = tiles[0]
            P.dma("sp", xres[tcnt % 2][0:w0_, :], xe[seg, erow0 + t0_:erow0 + t0_ + w0_, :], writes=[xres_b[tcnt % 2]])
            for j, (toff, tw) in enumerate(tiles):
                xi = tcnt % 2
                tcnt += 1
                if j + 1 < len(tiles):
                    t1_, w1_ = tiles[j + 1]
                    P.dma("sp", xres[tcnt % 2][0:w1_, :], xe[seg, erow0 + t1_:erow0 + t1_ + w1_, :],
                          writes=[xres_b[tcnt % 2]])
                for hf in range(2):
                    bk = nb()
                    for f in range(8):
                        P.op("pe", lambda E, f=f: E.matmul(psb[bk][0:tw, :], mT[:, f, toff:toff + tw],
                                                           Wout[:, f, hf * 512:(hf + 1) * 512],
                                                           start=(f == 0), stop=(f == 7)),
                             reads=[mT_b[f], bWp1], writes=[psbuf[bk]])
                    P.op("dve", lambda E: E.tensor_tensor(h1t[xi][0:tw, hf * 512:(hf + 1) * 512], psb[bk][0:tw, :],
                                                          xres[xi][0:tw, hf * 512:(hf + 1) * 512], ALU.add),
                         reads=[psbuf[bk], xres_b[xi]], writes=[h1t_b[xi]])
                P.dma("sp", h1_scr[seg, o0 + toff:o0 + toff + tw, :], h1t[xi][0:tw, :], reads=[h1t_b[xi]],
                      writes=[bh1s[seg]], sembuf=h1t_b[xi])
    if "h1" in debug:
        t = nc.dram_tensor("dbg_h1", [NSEG, OWNW, D], F32, kind="ExternalOutput").ap()
        bd = P.buf("dbg_h1")
        P.dma("sp", t, h1_scr, reads=bh1s, writes=[bd])
    P.barrier()
    esP1.close()
    esP1w.close()
    if stop_after == "p1":
        return nc, P, dbg_out
    esP2 = ES()
    Wdn = P.sb("Wdn", [128, NJ, D], BF16, esP2)
    bWdn = P.bufs(len(WPIECES), "Wdn")
    w_dn_v = w_down.rearrange("(j p) n -> p j n", p=128)

    def load_wdn(half):
        j0_ = half * NJ
        for pi_, (pj, pn) in enumerate(WPIECES):
            P.dma("pool", Wdn[:, pj:pj + pn, :], w_dn_v[:, j0_ + pj:j0_ + pj + pn, :], writes=[bWdn[pi_]])

    def piece_of(jj):
        for pi_, (pj, pn) in enumerate(WPIECES):
            if pj <= jj < pj + pn:
                return pi_

    load_wdn(0)
    Wpg = P.sb("Wpg", [128, 8, D], BF16, esP2)
    Wpl = P.sb("Wpl", [128, 2, D], BF16, esP2)
    bWp3 = P.buf("Wp3")
    P.dma("pool", Wpg[:], w_pg.rearrange("(k p) n -> p k n", p=128), writes=[bWp3])
    P.dma("pool", Wpl[:], w_ple.rearrange("(k p) n -> p k n", p=128), writes=[bWp3])
    front2 = make_front(esP2, "b", with_xs=False, NUTM=4)
    front3 = make_front(esP2, "c", with_xs=False, NUTM=2)
    tails = [P.sb(f"tail{i}", [128, 2 * NJ, 2], F32, esP2) for i in range(2)]
    btails = [P.bufs(2 * NJ, f"tail{i}") for i in range(2)]
    h1f = [P.sb(f"h1f{i}", [128, D], F32, esP2) for i in range(2)]
    h1f_b = P.bufs(2, "h1f")
    u2T = [P.sb(f"u2T{i}", [128, 8, 512], BF16, esP2) for i in range(2)]
    u2T_b = P.bufs(2, "u2T")
    u2h = P.sb("u2h", [128, 8, HALO], BF16, esP2)
    u2h_b = P.buf("u2h")
    aT = P.sb("aT", [128, NJ, 512], BF16, esP2)
    aT_b = P.bufs(NJ, "aT")
    vbuf = [[P.sb(f"vbuf{kd}{i}", [128, 516], F32, esP2) for i in range(2)] for kd in range(2)]
    vbuf_b = [P.bufs(2, f"vbuf{kd}") for kd in range(2)]
    cgv = [[P.sb(f"cgv{kd}{i}", [128, 512], F32, esP2) for i in range(2)] for kd in range(2)]
    cgv_b = [P.bufs(2, f"cgv{kd}") for kd in range(2)]
    NH2 = 4
    h2t = [P.sb(f"h2t{i}", [128, D], F32, esP2) for i in range(NH2)]
    h2t_b = P.bufs(NH2, "h2t")
    u3T = [P.sb(f"u3T{i}", [128, 8, 128], BF16, esP2) for i in range(2)]
    u3T_b = P.bufs(2, "u3T")
    gate_s = [P.sb("gate_s0", [128, D], F32, esP2)] * 2
    gate_b = [P.buf("gate_s")] * 2
    pt = [P.sb(f"pt{i}", [128, 256], F32, esP2) for i in range(4)]
    pt_b = P.bufs(4, "pt")
    pbf = [P.sb(f"pbf{i}", [128, 256], BF16, esP2) for i in range(2)]
    pbf_b = P.bufs(2, "pbf")
    pT = [P.sb(f"pT{i}", [128, 2, 128], BF16, esP2) for i in range(2)]
    pT_b = P.bufs(2, "pT")
    fss = [P.sb(f"fss{i}", [128, 4], F32, esP2) for i in range(2)]
    fss_b = P.bufs(2, "fss")
    outt = [P.sb(f"outt{i}", [128, D], F32, esP2) for i in range(2)]
    outt_b = P.bufs(2, "outt")
    h2_scr = nc.dram_tensor("h2_scr", [NSEG, SEGLEN, D], F32, kind="Internal").ap()
    u2_scr = nc.dram_tensor("u2_scr", [2 * NSEG, 128, 8, 512], BF16, kind="Internal").ap()
    bu2s = P.bufs(2 * NSEG, "u2s")
    u2h_scr = nc.dram_tensor("u2h_scr", [NSEG, 128, 8, HALO], BF16, kind="Internal").ap()
    bu2h = P.bufs(NSEG, "u2h")
    bh2s = [[P.buf(f"h2s{s}_{t}") for t in range(8)] for s in range(NSEG)]
    bout = P.buf("out")
    brr2 = [0]

    reserved = set()

    def nb2(lo=0, hi=5):
        while True:
            hi_ = hi + (1 if (hi == 5 and cur_half[0] == 0) else 0)
            b = lo + brr2[0] % (hi_ - lo)
            brr2[0] += 1
            if b not in reserved:
                return b

    PBANK = 5
    psP = psb[PBANK].bitcast(BF16)
    cctr = [0]
    fctr = [0]
    groups = [(seg, grp) for seg in range(NSEG) for grp in range(2)]

    cur_half = [0]
    for half in range(2):
        j0 = half * NJ
        cur_half[0] = half

        def halo_stage(seg):
            tail, btail = tails[seg % 2], btails[seg % 2]
            if half == 0:
                P.dma("sp", h1f[0][0:HALO, :], h1_scr[seg, 0:HALO, :], reads=[bh1s[seg]], writes=[h1f_b[0]])
                front2(None, u2h[:, :, :], u2h_b, gffn_bc, nrows=HALO, xres=(h1f[0], h1f_b[0]))
                front2.flush()
                P.dma("sp", u2h_scr[seg], u2h[:], reads=[u2h_b], writes=[bu2h[seg]])
            else:
                P.dma("sp", u2h[:], u2h_scr[seg], reads=[bu2h[seg]], writes=[u2h_b])
            bk = nb2()
            for kd in range(2):
                for jj in range(NJ):
                    cidx = kd * NJ + jj
                    for k in range(8):
                        P.op("pe", lambda E, k=k: E.matmul(psb[bk][:, 2 * cidx:2 * cidx + 2],
                                                           Wup[:, k, cidx * 128:(cidx + 1) * 128],
                                                           u2h[:, k, HALO - 2:HALO], start=(k == 0), stop=(k == 7)),
                             reads=[bWup[piece_of(jj)], u2h_b], writes=[psbuf[bk]])
            P.op("act", lambda E: E.activation(tail[:, :, :],
                                               psb[bk][:, 0:4 * NJ].rearrange("p (c t) -> p c t", t=2),
                                               AF.Identity, scale=hvalid[:, seg:seg + 1]),
                 reads=[psbuf[bk], bconst], writes=btail)

        def f_stage(gidx):
            seg, grp = groups[gidx]
            ub = gidx % 2
            if half == 1:
                P.dma("sp", u2T[ub][:], u2_scr[gidx], reads=[bu2s[gidx]], writes=[u2T_b[ub]])
                return
            for j in range(4):
                r0 = HALO + grp * 512 + j * 128
                fi = fctr[0] % 2
                fctr[0] += 1
                P.dma("sp", h1f[fi][:], h1_scr[seg, r0:r0 + 128, :], reads=[bh1s[seg]], writes=[h1f_b[fi]])
                front2(None, u2T[ub][:, :, j * 128:(j + 1) * 128], u2T_b[ub], gffn_bc, xres=(h1f[fi], h1f_b[fi]),
                       hold=True)

        def f_done(gidx):
            if half == 0:
                ub = gidx % 2
                P.dma("sp", u2_scr[gidx], u2T[ub][:], reads=[u2T_b[ub]], writes=[bu2s[gidx]])

        pre = {}

        def j_proj(gidx, jj):
            ub = gidx % 2
            banks = []
            for kd in range(2):
                cidx = kd * NJ + jj
                bk = nb2()
                proj_fm(lambda k: Wup[:, k, cidx * 128:(cidx + 1) * 128], lambda k: u2T[ub][:, k, :], 512, bk,
                        [bWup[piece_of(jj)], u2T_b[ub]])
                banks.append(bk)
            return banks

        def j_chain(gidx, jj, banks):
            tail, btail = tails[groups[gidx][0] % 2], btails[groups[gidx][0] % 2]
            res = []
            for kd in range(2):
                cidx = kd * NJ + jj
                ch = kd * 22 + j0 + jj
                bk = banks[kd]
                vi = cctr[0] % 2
                vb, vbb = vbuf[kd][vi], vbuf_b[kd][vi]
                cg, cgb = cgv[kd][vi], cgv_b[kd][vi]
                P.op("pool", lambda E: E.tensor_copy(vb[:, 0:2], tail[:, cidx, :]), reads=[btail[cidx]],
                     writes=[vbb])
                P.op("act", lambda E: E.copy(vb[:, 2:514], psb[bk][:, :]), reads=[psbuf[bk]], writes=[vbb])
                P.op("pool", lambda E: E.tensor_copy(tail[:, cidx, :], vb[:, 512:514]), reads=[vbb],
                     writes=[btail[cidx]])
                P.op("act", lambda E: E.activation(cg[:, :], psb[bk][:, :], AF.Identity,
                                                   bias=fcb_t[:, ch:ch + 1], scale=fcw_t[:, ch, 2:3]),
                     reads=[psbuf[bk], bWp2], writes=[cgb])
                P.op("dve", lambda E: E.scalar_tensor_tensor(cg[:, :], vb[:, 1:513], fcw_t[:, ch, 1:2], cg[:, :],
                                                             ALU.mult, ALU.add),
                     reads=[vbb, cgb, bWp2], writes=[cgb])
                P.op("dve", lambda E: E.scalar_tensor_tensor(cg[:, :], vb[:, 0:512], fcw_t[:, ch, 0:1], cg[:, :],
                                                             ALU.mult, ALU.add),
                     reads=[vbb, cgb, bWp2], writes=[cgb])
                res.append((cg, cgb))
            cctr[0] += 1
            (cgg, cggb), (cgvv, cgvb) = res
            P.op("act", lambda E: E.activation(cgg[:, :], cgg[:, :], AF.Gelu_apprx_tanh), reads=[cggb],
                 writes=[cggb])
            P.op("dve", lambda E: E.tensor_tensor(aT[:, jj, :], cgg[:, :], cgvv[:, :], ALU.mult),
                 reads=[cggb, cgvb], writes=[aT_b[jj]])

        def j_stage(gidx):
            for jj in range(NJ):
                if (gidx, jj) in pre:
                    banks = pre.pop((gidx, jj))
                    for b_ in banks:
                        reserved.discard(b_)
                else:
                    banks = j_proj(gidx, jj)
                j_chain(gidx, jj, banks)

        def j_preissue(gidx, jj):
            banks = j_proj(gidx, jj)
            pre[(gidx, jj)] = banks
            reserved.update(banks)

        def down_mm(j, hf, bk):
            for jj in range(NJ):
                P.op("pe", lambda E, jj=jj: E.matmul(psb[bk][:, :], aT[:, jj, j * 128:(j + 1) * 128],
                                                     Wdn[:, jj, hf * 512:(hf + 1) * 512],
                                                     start=(jj == 0), stop=(jj == NJ - 1)),
                     reads=[aT_b[jj], bWdn[piece_of(jj)]], writes=[psbuf[bk]])

        def w_stage_half0(gidx):
            seg, grp = groups[gidx]
            for j in range(4):
                r0 = grp * 512 + j * 128
                P.dma("sp", h2t[j % NH2][:], h1_scr[seg, HALO + r0:HALO + r0 + 128, :], reads=[bh1s[seg]],
                      writes=[h2t_b[j % NH2]])
            for j in range(4):
                r0 = grp * 512 + j * 128
                si = j % NH2
                for hf in range(2):
                    bk = nb2()
                    down_mm(j, hf, bk)
                    hs = slice(hf * 512, (hf + 1) * 512)
                    P.op("dve", lambda E: E.tensor_tensor(h2t[si][:, hs], psb[bk][:, :], h2t[si][:, hs], ALU.add),
                         reads=[psbuf[bk], h2t_b[si]], writes=[h2t_b[si]])
                P.dma("sp", h2_scr[seg, r0:r0 + 128, :], h2t[si][:], reads=[h2t_b[si]],
                      writes=[bh2s[seg][grp * 4 + j]], sembuf=h2t_b[si])

        def w_stage_half1(gidx):
            seg, grp = groups[gidx]

            for j in range(4):
                r0 = grp * 512 + j * 128
                P.dma("sp", h2t[j % NH2][:], h2_scr[seg, r0:r0 + 128, :], reads=[bh2s[seg][grp * 4 + j]],
                      writes=[h2t_b[j % NH2]])
                P.dma("sp", pt[j][:], pown[seg, r0:r0 + 128, :], writes=[pt_b[j]])

            def s0(j):
                r0 = grp * 512 + j * 128
                si = j % NH2
                for hf in range(2):
                    bk = nb2()
                    down_mm(j, hf, bk)
                    hs = slice(hf * 512, (hf + 1) * 512)
                    P.op("dve", lambda E: E.tensor_tensor(h2t[si][:, hs], psb[bk][:, :], h2t[si][:, hs], ALU.add),
                         reads=[psbuf[bk], h2t_b[si]], writes=[h2t_b[si]])

            def s1(j):
                r0 = grp * 512 + j * 128
                si, s2_ = j % NH2, j % 2
                front3(None, u3T[s2_][:, :, :], u3T_b[s2_], gple_bc, xres=(h2t[si], h2t_b[si]), hold=True)
                P.op("pool", lambda E: E.tensor_copy(pbf[s2_][:], pt[j][:]), reads=[pt_b[j]], writes=[pbf_b[s2_]])

            def s2(j):
                s2_ = j % 2
                front3.flush(1)
                for kp in range(2):
                    P.op("pe", lambda E, kp=kp: E.transpose(psP[:, kp * 128:(kp + 1) * 128],
                                                            pbf[s2_][:, kp * 128:(kp + 1) * 128], ident[:]),
                         reads=[pbf_b[s2_], bconst], writes=[psbuf[PBANK]])
                P.op("act", lambda E: E.copy(pT[s2_][:], psP[:, 0:256].rearrange("p (k t) -> p k t", k=2)),
                     reads=[psbuf[PBANK]], writes=[pT_b[s2_]])

            def s3(j):
                si, s2_ = j % NH2, j % 2
                for hf in range(2):
                    hs = slice(hf * 512, (hf + 1) * 512)
                    bkg = nb2()
                    for k in range(8):
                        P.op("pe", lambda E, k=k: E.matmul(psb[bkg][:, :], u3T[s2_][:, k, :], Wpg[:, k, hs],
                                                           start=(k == 0), stop=(k == 7)),
                             reads=[u3T_b[s2_], bWp3], writes=[psbuf[bkg]])
                    P.op("act", lambda E: E.activation(gate_s[s2_][:, hs], psb[bkg][:, :], AF.Sigmoid),
                         reads=[psbuf[bkg]], writes=[gate_b[s2_]])
                    bkp = nb2()
                    for kp in range(2):
                        P.op("pe", lambda E, kp=kp: E.matmul(psb[bkp][:, :], pT[s2_][:, kp, :], Wpl[:, kp, hs],
                                                             start=(kp == 0), stop=(kp == 1)),
                             reads=[pT_b[s2_], bWp3], writes=[psbuf[bkp]])
                    P.op("dve", lambda E: E.tensor_tensor(gate_s[s2_][:, hs], psb[bkp][:, :], gate_s[s2_][:, hs],
                                                          ALU.mult),
                         reads=[psbuf[bkp], gate_b[s2_]], writes=[gate_b[s2_]])
                    P.op("pool", lambda E: E.tensor_tensor(h2t[si][:, hs], h2t[si][:, hs], gate_s[s2_][:, hs], ALU.add),
                         reads=[gate_b[s2_], h2t_b[si]], writes=[h2t_b[si]])

            def s4(j):
                r0 = grp * 512 + j * 128
                si, s2_ = j % NH2, j % 2
                P.op("act", lambda E: E.activation(outt[s2_][:], h2t[si][:], AF.Square, accum_out=fss[s2_][:, 0:1]),
                     reads=[h2t_b[si]], writes=[fss_b[s2_], outt_b[s2_]])
                P.op("act", lambda E: E.activation(fss[s2_][:, 1:2], fss[s2_][:, 0:1], AF.Sqrt, bias=eps_t[:, 0:1],
                                                   scale=1.0 / D), reads=[fss_b[s2_], bconst], writes=[fss_b[s2_]])
                P.op("dve", lambda E: E.reciprocal(fss[s2_][:, 2:3], fss[s2_][:, 1:2]), reads=[fss_b[s2_]],
                     writes=[fss_b[s2_]])
                P.op("dve", lambda E: E.scalar_tensor_tensor(outt[s2_][:], h2t[si][:], fss[s2_][:, 2:3], gfin_bc[:],
                                                             ALU.mult, ALU.mult),
                     reads=[h2t_b[si], fss_b[s2_], bconst], writes=[outt_b[s2_]])
                P.dma("sp", out[seg, r0:r0 + 128, :], outt[s2_][:], reads=[outt_b[s2_]], writes=[bout],
                      sembuf=outt_b[s2_])

            stages = (s0, s1, s2, s3, s4)
            for t in range(4 + len(stages) - 1):
                for si_, fn in enumerate(stages):
                    j = t - si_
                    if 0 <= j < 4:
                        fn(j)

        ng = len(groups)
        f_stage(0)
        front2.flush()
        f_done(0)
        halo_stage(groups[0][0])
        if ng > 1:
            if groups[1][0] != groups[0][0]:
                halo_stage(groups[1][0])
            f_stage(1)
        for gidx in range(ng):
            j_stage(gidx)
            front2.flush()
            if gidx + 1 < ng:
                f_done(gidx + 1)
            if gidx + 2 < ng:
                if groups[gidx + 2][0] != groups[gidx + 1][0]:
                    halo_stage(groups[gidx + 2][0])
                f_stage(gidx + 2)
            if gidx == ng - 1 and half == 0:
                load_wup(1)
            if half == 0:
                w_stage_half0(gidx)
            else:
                w_stage_half1(gidx)
        if half == 0:
            load_wdn(1)
    P.barrier()
    esP2.close()
    esP2w.close()
    esP2c.close()

    return nc, P, dbg_out


def make_in_maps(inputs, cores=range(NCORES)):
    x = np.asarray(inputs["x"], np.float32)
    p = np.asarray(inputs["p"], np.float32)
    f = lambda k: np.ascontiguousarray(np.asarray(inputs[k], np.float32)[0])
    pc = lambda v: np.ascontiguousarray(v.reshape(-1, 128).T)
    shared = {
        "w_in": f("w_in"), "g_mix": f("g_mix"),
        "pos_kT": np.ascontiguousarray(f("cmp_pos_k").T), "pos_vT": np.ascontiguousarray(f("cmp_pos_v").T),
        "w_k1": f("w_cmp_k1"), "w_k2": f("w_cmp_k2"), "w_v1": f("w_cmp_v1"), "w_v2": f("w_cmp_v2"),
        "w_o": f("w_o_nsa"),
        "conv_wT": np.ascontiguousarray(f("conv_w").T.reshape(4, 128, 31).transpose(1, 0, 2)),
        "conv_b": pc(f("conv_b")), "ln_g": pc(f("conv_ln_g")), "ln_b": pc(f("conv_ln_b")),
        "w_co": f("w_conv_out"), "b_co": pc(f("b_conv_out")),
        "w_out": f("w_out"), "g_ffn": f("g_ffn"), "w_up": f("w_up"),
        "fcw": np.ascontiguousarray(f("ffn_conv_w").T.reshape(44, 128, 3).transpose(1, 0, 2)),
        "fcb": pc(f("ffn_conv_b")), "w_down": f("w_down"),
        "g_ple": f("g_ple"), "w_pg": f("w_ple_gate"), "w_ple": f("w_ple"),
        "g_fin": np.ascontiguousarray(np.asarray(inputs["g_final"], np.float32)),
    }
    w = shared["w_in"].copy()
    wq = w[:, 0:512].reshape(D, 2, 4, 64).transpose(0, 2, 1, 3).reshape(D, 512)
    w[:, 0:512] = wq
    shared["w_in"] = w
    consts = {r: host_consts(r) for r in (0, 1)}
    maps = []
    for c in cores:
        b, r = c // 2, c % 2
        m = dict(shared)
        m.update(consts[r])
        m["xc"] = np.ascontiguousarray(x[b])
        xe = np.zeros((NSEG, NEXT * 128, D), np.float32)
        po = np.zeros((NSEG, SEGLEN, 256), np.float32)
        for s, s0 in enumerate(SEG_STARTS[r]):
            lo = s0 - NCTX * 128
            src_lo = max(lo, 0)
            xe[s, src_lo - lo:] = x[b, src_lo:s0 + SEGLEN]
            po[s] = p[0, b, s0:s0 + SEGLEN]
        m["xe"] = xe
        m["pown"] = po
        maps.append(m)
    return maps


_NC_CACHE = {}


def kernel(**inputs):
    if "nc" not in _NC_CACHE:
        _NC_CACHE["nc"] = build()[0]
    nc = _NC_CACHE["nc"]
    maps = make_in_maps(inputs)
    res = run_bass_kernel_spmd(nc, maps, core_ids=list(range(NCORES)))
    outp = np.zeros((NB, S, D), np.float32)
    for c in range(NCORES):
        b, r = c // 2, c % 2
        o = res.results[c]["out"]
        for s, s0 in enumerate(SEG_STARTS[r]):
            outp[b, s0:s0 + SEGLEN] = o[s]
    return outp
```

```python
import contextlib
import numpy as np
import ml_dtypes
import concourse.bass as bass
import concourse.mybir as mybir
from concourse.bass_utils import run_bass_kernel_spmd

F32 = mybir.dt.float32
BF16 = mybir.dt.bfloat16
AF = mybir.ActivationFunctionType
ALU = mybir.AluOpType
AX = mybir.AxisListType

D = 1024
S = 4096
NB = 4
NCORES = 8
SEGLEN = 1024
NSEG = 2
NCTX = 5
NEXT = NCTX + SEGLEN // 128
HALO = 32
OWNW = HALO + SEGLEN
OWN = NSEG * OWNW
SEG_STARTS = {0: [0, 3072], 1: [1024, 2048]}
BIG = 30000.0
EPS = 1e-6
DFF = 2816
NIN = 4376
SAME_SYNC = True
NOSAME = ()
CQ, CKV, CG, CCV, CM = 0, 512, 1280, 1304, 2328


class Buf:
    __slots__ = ("name", "last_w", "readers", "dma_sem")

    def __init__(self, name):
        self.name = name
        self.last_w = None
        self.readers = []
        self.dma_sem = None


class Prog:
    def __init__(self, nc, same_engine_sync=True):
        self.nc = nc
        self.es = contextlib.ExitStack()
        self.cnt = {}
        self.waited = {e: {} for e in ("pe", "act", "dve", "pool", "sp")}
        self.sem = {}
        self.semobj = {}
        self.semcnt = {}
        for e in ("pe", "act", "dve", "pool"):
            s = self.es.enter_context(nc.semaphore("s_" + e))
            self.sem[e] = s
            self.semobj[("eng", e)] = s
            self.semcnt[("eng", e)] = 0
        self.same = same_engine_sync
        self.E = {"pe": nc.tensor, "act": nc.scalar, "dve": nc.vector, "pool": nc.gpsimd,
                  "sp": nc.sync}
        self.ndma = 0
        self.nbuf = 0
        self.ninstr = {e: 0 for e in self.E}

    def sb(self, name, shape, dt, es=None, side=None):
        return (es or self.es).enter_context(self.nc.sbuf_tensor("sb_" + name, shape, dt, side=side))

    def ps(self, name, shape, dt, es=None):
        return (es or self.es).enter_context(self.nc.psum_tensor("pp_" + name, shape, dt))

    def buf(self, name=None):
        self.nbuf += 1
        return Buf(name or f"b{self.nbuf}")

    def bufs(self, n, name="b"):
        return [self.buf(f"{name}{i}") for i in range(n)]

    def new_dma_sem(self):
        self.ndma += 1
        s = self.es.enter_context(self.nc.semaphore(f"sd{self.ndma}"))
        key = ("dma", self.ndma)
        self.semobj[key] = s
        self.semcnt[key] = 0
        return key

    def _deps(self, reads, writes):
        deps = []
        for b in reads:
            if b.last_w is not None:
                deps.append(b.last_w)
        for b in writes:
            if b.last_w is not None:
                deps.append(b.last_w)
            deps.extend(b.readers)
        return deps

    def _emit_waits(self, eng, deps):
        w = self.waited[eng]
        need = {}
        for key, val in deps:
            if key == ("eng", eng) and (eng == "pe" or eng in NOSAME):
                continue
            if w.get(key, 0) < val:
                need[key] = max(need.get(key, 0), val)
        for key, val in need.items():
            w[key] = val
            self.E[eng].wait_ge(self.semobj[key], val)

    def _stamp(self, stamp, reads, writes):
        for b in reads:
            b.readers.append(stamp)
        for b in writes:
            b.last_w = stamp
            b.readers = []

    def op(self, eng, fn, reads=(), writes=()):
        self._emit_waits(eng, self._deps(reads, writes))
        key = ("eng", eng)
        self.semcnt[key] += 1
        stamp = (key, self.semcnt[key])
        fn(self.E[eng]).then_inc(self.sem[eng], 1)
        self.ninstr[eng] += 1
        self._stamp(stamp, reads, writes)

    def dma(self, queue, out, in_, reads=(), writes=(), sembuf=None, **kw):
        b0 = sembuf or (writes[0] if writes else reads[0])
        if b0.dma_sem is None:
            b0.dma_sem = self.new_dma_sem()
        key = b0.dma_sem
        deps = [d for d in self._deps(reads, writes)
                if not (d[0] == key and any(b.last_w == d for b in writes))]
        self._emit_waits(queue, deps)
        self.semcnt[key] += 16
        stamp = (key, self.semcnt[key])
        self.E[queue].dma_start(out, in_, **kw).then_inc(self.semobj[key], 16)
        self.ninstr[queue] += 1
        self._stamp(stamp, reads, writes)

    def barrier(self):
        for e in ("pe", "act", "dve", "pool", "sp"):
            deps = [(k, v) for k, v in self.semcnt.items() if v > 0 and k != ("eng", e)]
            self._emit_waits(e, deps)

    def wait_bufs(self, eng, bufs):
        deps = []
        for b in bufs:
            if b.last_w is not None:
                deps.append(b.last_w)
            deps.extend(b.readers)
        self._emit_waits(eng, deps)


def _bf(a):
    return np.ascontiguousarray(a).astype(ml_dtypes.bfloat16)


def own_token_positions(role):
    pos = np.zeros(OWN, np.int64)
    for s, s0 in enumerate(SEG_STARTS[role]):
        pos[s * OWNW:(s + 1) * OWNW] = np.arange(s0 - HALO, s0 + SEGLEN)
    return pos


def host_consts(role):
    c = {}
    starts = SEG_STARTS[role]
    c["ident"] = _bf(np.eye(128, dtype=np.float32))
    c["identf"] = np.eye(128, dtype=np.float32)
    E = np.zeros((64, S), np.float32)
    E[np.arange(S) // 64, np.arange(S)] = 1.0
    c["E"] = _bf(E)
    k = np.arange(128)[:, None]
    q = np.arange(128)[None, :]
    c["mcausal"] = _bf(np.where(k <= q, 0.0, -BIG))
    c["mfar"] = _bf(np.where(k > q, 0.0, -BIG))
    n_cmp = 255
    c0 = np.arange(n_cmp) * 16
    j0 = np.arange(64) * 64
    lo = np.maximum(c0[:, None], j0[None, :])
    hi = np.minimum(c0[:, None] + 32, j0[None, :] + 64)
    sm = np.zeros((256, 64), np.float32)
    sm[:255] = np.maximum(hi - lo, 0) / 32.0
    c["selmap"] = _bf(sm.reshape(2, 128, 64).transpose(1, 0, 2))
    pos = own_token_positions(role)
    cidx = np.arange(256)
    cend = cidx * 16 + 31
    allowed = (cend[:, None] <= pos[None, :]) & (cidx[:, None] < 255)
    cm = np.where(allowed, 0.0, -BIG).astype(np.float32)
    c["cmask"] = _bf(cm.reshape(2, 128, OWN).transpose(1, 0, 2))
    j = np.arange(64)[None, :]
    cur = (pos // 64)[:, None]
    real = (pos >= 0)[:, None]
    valid = (j <= cur) & real
    forced = ((j == 0) | (j == cur) | (j == cur - 1)) & valid
    mul = (valid & ~forced).astype(np.float32)
    add = np.where(forced, 1e4, np.where(valid, 0.0, -1.0)).astype(np.float32)
    add = np.where(real, add, 0.0)
    dt = np.zeros(OWN, np.int64)
    for s, s0 in enumerate(starts):
        dt[s * OWNW: s * OWNW + HALO] = s0 // 128 - 1
        dt[s * OWNW + HALO:(s + 1) * OWNW] = (s0 + np.arange(SEGLEN)) // 128
    pre2 = np.where(j >= 2 * dt[:, None], -2 * BIG, -BIG).astype(np.float32)
    c["selc"] = np.ascontiguousarray(np.stack([mul, add, pre2], axis=1))
    vf = np.zeros((128, NSEG, NEXT), np.float32)
    for s, s0 in enumerate(starts):
        tpos = s0 - NCTX * 128 + np.arange(NEXT * 128)
        vf[:, s, :] = (tpos >= 0).astype(np.float32).reshape(NEXT, 128).T
    c["vflag"] = vf
    hv = np.zeros((128, NSEG), np.float32)
    for s, s0 in enumerate(starts):
        hv[:, s] = 1.0 if s0 > 0 else 0.0
    c["hvalid"] = hv
    return c


def build(debug=None, stop_after=None):
    debug = debug or set()
    nc = bass.Bass("TRN2", target_bir_lowering=False)
    P = Prog(nc, same_engine_sync=SAME_SYNC)
    dbg_out = {}
    ES = contextlib.ExitStack

    def din(name, shape, dt=F32):
        return nc.dram_tensor(name, list(shape), dt, kind="ExternalInput").ap()

    xc = din("xc", [S, D])
    xe = din("xe", [NSEG, NEXT * 128, D])
    pown = din("pown", [NSEG, SEGLEN, 256])
    w_in = din("w_in", [D, NIN])
    g_mix = din("g_mix", [D])
    pos_kT = din("pos_kT", [64, 32])
    pos_vT = din("pos_vT", [64, 32])
    w_k1 = din("w_k1", [2048, 256])
    w_k2 = din("w_k2", [256, 64])
    w_v1 = din("w_v1", [2048, 256])
    w_v2 = din("w_v2", [256, 64])
    w_o = din("w_o", [512, D])
    conv_wT = din("conv_wT", [128, 4, 31])
    conv_b = din("conv_b", [128, 4])
    ln_g = din("ln_g", [128, 4])
    ln_b = din("ln_b", [128, 4])
    w_co = din("w_co", [512, D])
    b_co = din("b_co", [128, 8])
    w_out = din("w_out", [D, D])
    g_ffn = din("g_ffn", [D])
    w_up = din("w_up", [D, 2 * DFF])
    fcw = din("fcw", [128, 44, 3])
    fcb = din("fcb", [128, 44])
    w_down = din("w_down", [DFF, D])
    g_ple = din("g_ple", [D])
    w_pg = din("w_pg", [D, D])
    w_ple = din("w_ple", [256, D])
    g_fin = din("g_fin", [D])
    c_ident = din("ident", [128, 128], BF16)
    c_identf = din("identf", [128, 128], F32)
    c_E = din("E", [64, S], BF16)
    c_mcausal = din("mcausal", [128, 128], BF16)
    c_mfar = din("mfar", [128, 128], BF16)
    c_selmap = din("selmap", [128, 2, 64], BF16)
    c_cmask = din("cmask", [128, 2, OWN], BF16)
    c_selc = din("selc", [OWN, 3, 64], F32)
    c_vflag = din("vflag", [128, NSEG, NEXT], F32)
    c_hvalid = din("hvalid", [128, NSEG], F32)
    out = nc.dram_tensor("out", [NSEG, SEGLEN, D], F32, kind="ExternalOutput").ap()
    h1_scr = nc.dram_tensor("h1_scr", [NSEG, OWNW, D], F32, kind="Internal").ap()
    uT_scr = nc.dram_tensor("uT_scr", [128, 8, OWN], BF16, kind="Internal").ap()
    cT_scr = nc.dram_tensor("cT_scr", [128, 4, OWN], BF16, kind="Internal").ap()
    bscr_u = [P.buf(f"uTscr{s}") for s in range(NSEG)]
    bscr_c = [P.buf(f"cTscr{s}") for s in range(NSEG)]

    def dump(name, shape, src_ap, reads, dt=F32):
        if name not in debug:
            return
        t = nc.dram_tensor("dbg_" + name, list(shape), dt, kind="ExternalOutput").ap()
        b = P.buf("dbg_" + name)
        P.dma("sp", t, src_ap, reads=reads, writes=[b])
        dbg_out[name] = b

    psb = [P.ps(f"ps{i}", [128, 512], F32) for i in range(8)]
    psbuf = [P.buf(f"ps{i}") for i in range(8)]

    ident = P.sb("ident", [128, 128], BF16)
    identf = P.sb("identf", [128, 128], F32)
    eps_t = P.sb("eps_t", [128, 1], F32)
    vflag = P.sb("vflag", [128, NSEG, NEXT], F32)
    hvalid = P.sb("hvalid", [128, NSEG], F32)
    bconst = P.buf("consts")
    P.dma("sp", ident[:], c_ident, writes=[bconst])
    P.dma("sp", identf[:], c_identf, writes=[bconst])
    P.dma("sp", vflag[:], c_vflag, writes=[bconst])
    P.dma("sp", hvalid[:], c_hvalid, writes=[bconst])
    P.op("dve", lambda E: E.memset(eps_t[:], EPS), writes=[bconst])

    oT_scr = nc.dram_tensor("oT_scr", [128, 4, OWN], BF16, kind="Internal").ap()
    boT = P.buf("oTscr")
    esL2 = ES()
    KsAug = [P.sb(f"KsAug{g}", [128, S], BF16, esL2) for g in range(2)]
    VsAug = P.sb("VsAug", [128, 32, 2, 65], BF16, esL2)
    KcAug = [P.sb(f"KcAug{g}", [128, 256], BF16, esL2) for g in range(2)]
    VcAug = P.sb("VcAug", [128, 2, 2, 65], BF16, esL2)
    KwAug = [P.sb(f"KwAug{g}", [128, NSEG, NEXT * 128], BF16, esL2) for g in range(2)]
    KxAug = [P.sb(f"KxAug{g}", [128, NSEG, NEXT * 128], BF16, esL2) for g in range(2)]
    VwAug = P.sb("VwAug", [128, NSEG, NEXT, 2, 65], BF16, esL2)
    VxAug = P.sb("VxAug", [128, NSEG, NEXT, 2, 65], BF16, esL2)
    QT = P.sb("QT", [128, 4, OWN], BF16, esL2)
    NOT_ = NSEG * 9
    gates = P.sb("gates", [128, NOT_, 24], F32, esL2)
    for g in range(2):
        P.op("pool", lambda E, g=g: E.memset(KcAug[g][:], 0.0), writes=[bconst])
        P.op("pool", lambda E, g=g: E.memset(KwAug[g][:], 0.0), writes=[bconst])
        P.op("dve", lambda E, g=g: E.memset(KxAug[g][:], 0.0), writes=[bconst])
    P.op("dve", lambda E: E.memset(VsAug[:], 1.0), writes=[bconst])
    P.op("dve", lambda E: E.memset(VcAug[:], 1.0), writes=[bconst])
    P.op("pool", lambda E: E.memset(VxAug[:], 1.0), writes=[bconst])
    P.op("pool", lambda E: E.memset(VwAug[:], 1.0), writes=[bconst])
    P.dma("sp", KsAug[0][64:128, :], c_E, writes=[bconst])
    P.dma("sp", KsAug[1][0:64, :], c_E, writes=[bconst])
    P.barrier()
    for g in range(2):
        P.op("dve", lambda E, g=g: E.tensor_copy(VwAug[:, :, :, g, 64], vflag[:]), reads=[bconst],
             writes=[bconst])
        P.op("dve", lambda E, g=g: E.tensor_copy(VxAug[:, :, :, g, 64], vflag[:]), reads=[bconst],
             writes=[bconst])

    esCP = ES()
    cpads = [P.sb(f"cpad{s}", [128, 4, 32 + OWNW], BF16, esCP) for s in range(NSEG)]
    bcpad = P.bufs(NSEG, "cpad")
    esL3 = ES()
    gmix_bc = P.sb("gmix_bc", [128, D], F32, esL3)
    P.dma("sp", gmix_bc[:], g_mix.partition_broadcast(128), writes=[bconst])
    Wkv = P.sb("Wkv", [128, 8, 768], BF16, esL3)
    bW1 = P.buf("Wkv")
    w_in_v = w_in.rearrange("(k p) n -> p k n", p=128)
    P.dma("pool", Wkv[:], w_in_v[:, :, CKV:CKV + 768], writes=[bW1])
    PS_T = 7
    psT = psb[PS_T].bitcast(BF16)

    psTs = {b: psb[b].bitcast(BF16) for b in (6, 7)}

    def make_front(es_, tag, with_xs=True, banks=(6, 7), NXS=2, NUTM=2, depth=1):
        pend = []
        xs = [P.sb(f"xs{tag}{i}", [128, D], F32, es_) for i in range(NXS)] if with_xs else None
        xs_b = P.bufs(NXS, "xs")
        sq_junk = P.sb("sq_junk" + tag, [128, D], BF16, es_)
        sq_b = P.buf("sqj")
        ss = [P.sb(f"ss{tag}{i}", [128, 4], F32, es_) for i in range(NXS)]
        ss_b = P.bufs(NXS, "ss")
        utm = [P.sb(f"utm{tag}{i}", [128, D], BF16, es_) for i in range(NUTM)]
        utm_b = P.bufs(NUTM, "utm")
        front_ctr = [0]

        def front_tile(x_rows_ap, dst_ap, dst_buf, g_bc, nrows=128, xres=None, hold=False):
            i = front_ctr[0]
            front_ctr[0] += 1
            s3 = i % NXS
            s2 = i % NUTM
            if xres is None:
                P.dma("sp", xs[s3][0:nrows, :], x_rows_ap, writes=[xs_b[s3]])
                xin, xb = xs[s3], xs_b[s3]
            else:
                xin, xb = xres
            P.op("act", lambda E: E.activation(sq_junk[0:nrows, :], xin[0:nrows, :], AF.Square,
                                               accum_out=ss[s3][0:nrows, 0:1]),
                 reads=[xb], writes=[sq_b, ss_b[s3]])
            P.op("act", lambda E: E.activation(ss[s3][0:nrows, 1:2], ss[s3][0:nrows, 0:1], AF.Sqrt,
                                               bias=eps_t[0:nrows, 0:1], scale=1.0 / D),
                 reads=[ss_b[s3], bconst], writes=[ss_b[s3]])
            P.op("dve", lambda E: E.reciprocal(ss[s3][0:nrows, 2:3], ss[s3][0:nrows, 1:2]), reads=[ss_b[s3]],
                 writes=[ss_b[s3]])
            P.op("dve", lambda E: E.scalar_tensor_tensor(utm[s2][0:nrows, :], xin[0:nrows, :], ss[s3][0:nrows, 2:3],
                                                         g_bc[0:nrows, :], ALU.mult, ALU.mult),
                 reads=[xb, ss_b[s3], bconst], writes=[utm_b[s2]])
            bki = banks[i % len(banks)]
            psT_ = psTs[bki]

            def stage_b():
                for k in range(8):
                    P.op("pe", lambda E, k=k: E.transpose(psT_[:, k * 128:k * 128 + nrows],
                                                          utm[s2][0:nrows, k * 128:(k + 1) * 128],
                                                          ident[0:nrows, 0:nrows]),
                         reads=[utm_b[s2], bconst], writes=[psbuf[bki]])
                P.op("act", lambda E: E.copy(dst_ap, psT_[:, :].rearrange("p (k t) -> p k t", k=8)[:, :, 0:nrows]),
                     reads=[psbuf[bki]], writes=[dst_buf])

            if not hold:
                while len(pend) >= depth:
                    pend.pop(0)()
            pend.append(stage_b)
            return ss[s3], ss_b[s3]

        def flush(n=None):
            k = 0
            while pend and (n is None or k < n):
                pend.pop(0)()
                k += 1

        front_tile.flush = flush
        return front_tile

    front_tile = make_front(esL3, "a", NXS=4, NUTM=4, depth=3)
    def proj_fm(lhs_fn, rhs_fn, n, bank, rbufs, nk=8):
        for k in range(nk):
            P.op("pe", lambda E, k=k: E.matmul(psb[bank][:, 0:n], lhs_fn(k), rhs_fn(k),
                                               start=(k == 0), stop=(k == nk - 1)),
                 reads=rbufs, writes=[psbuf[bank]])

    esL4 = ES()
    kcmpT = P.sb("kcmpT", [128, S], BF16, esL4)
    vcmpT = P.sb("vcmpT", [128, S], BF16, esL4)
    w1s = P.sb("w1s", [128, 32, 256], BF16, esL4)
    bw1s = P.buf("w1s")

    def load_w1(wsrc):
        v = wsrc.rearrange("(l d) h -> d l h", d=64)
        P.dma("pool", w1s[0:64], v, writes=[bw1s])
        P.dma("pool", w1s[64:128], v, writes=[bw1s])

    load_w1(w_k1)
    esL4b = ES()
    uTg = [P.sb(f"uTg{i}", [128, 8, 512], BF16, esL4b) for i in range(2)]
    uTg_b = P.bufs(2, "uTg")
    bctx = P.bufs(8, "ctx")
    def ph1_front(grp):
        ub = grp % 2
        for j in range(4):
            t = grp * 4 + j
            front_tile(xc[t * 128:(t + 1) * 128, :], uTg[ub][:, :, j * 128:(j + 1) * 128], uTg_b[ub], gmix_bc)

    def ph1_proj(grp):
        ub = grp % 2
        for ch, bank in ((0, 0), (1, 1), (2, 2)):
            proj_fm(lambda k, ch=ch: Wkv[:, k, ch * 128:(ch + 1) * 128], lambda k: uTg[ub][:, k, :],
                    512, bank, [bW1, uTg_b[ub]])
        for j in range(4):
            for k in range(8):
                P.op("pe", lambda E, j=j, k=k: E.matmul(psb[3][:, j * 128:(j + 1) * 128],
                                                        uTg[ub][:, k, j * 128:(j + 1) * 128],
                                                        Wkv[:, k, 384:512], start=(k == 0),
                                                        stop=(k == 7)),
                     reads=[bW1, uTg_b[ub]], writes=[psbuf[3]])

    def ph1_evac(grp):
        cs = slice(grp * 512, (grp + 1) * 512)
        kd_ = kcmpT[:, :].rearrange("p (s c) -> p s c", s=16)[:, :, grp * 32:(grp + 1) * 32]
        vd_ = vcmpT[:, :].rearrange("p (s c) -> p s c", s=16)[:, :, grp * 32:(grp + 1) * 32]
        P.op("act", lambda E: E.copy(kd_, psb[0][:, :].rearrange("p (c s) -> p s c", s=16)), reads=[psbuf[0]],
             writes=[bctx[grp]])
        P.op("dve", lambda E: E.tensor_copy(vd_, psb[1][:, :].rearrange("p (c s) -> p s c", s=16)),
             reads=[psbuf[1]], writes=[bctx[grp]])
        P.op("act", lambda E: E.copy(KsAug[0][0:64, cs], psb[2][0:64, :]), reads=[psbuf[2]],
             writes=[bctx[grp]])
        P.op("dve", lambda E: E.tensor_copy(KsAug[1][64:128, cs], psb[2][64:128, :]), reads=[psbuf[2]],
             writes=[bctx[grp]])
        P.op("dve", lambda E: E.tensor_copy(
            VsAug[:, grp * 4:(grp + 1) * 4, :, 0:64],
            psb[3][:, :].rearrange("p (j g d) -> p j g d", j=4, g=2)), reads=[psbuf[3]],
            writes=[bctx[grp]])

    ph1_front(0)
    front_tile.flush()
    for grp in range(8):
        ph1_proj(grp)
        if grp + 1 < 8:
            ph1_front(grp + 1)
        ph1_evac(grp)
        front_tile.flush()
    dump("KsAug0", [128, S], KsAug[0][:], bctx, BF16)
    dump("VsAug", [128, 32, 2, 65], VsAug[:], bctx, BF16)
    P.barrier()
    esL4b.close()
    if stop_after == "ph1":
        return nc, P, dbg_out

    es2 = ES()
    bw2 = P.buf("w_cmp")
    w1 = {"k": w1s, "v": w1s}
    w2 = P.sb("w2", [128, 2, 2, 128], BF16, es2)
    for ki, wsrc in enumerate((w_k2, w_v2)):
        v = wsrc.rearrange("(c p) d -> p c d", p=128)
        P.dma("pool", w2[:, :, ki, 0:64], v, writes=[bw2])
        P.dma("pool", w2[:, :, ki, 64:128], v, writes=[bw2])
    posT = P.sb("posT", [64, 2, 32], BF16, es2)
    P.dma("pool", posT[:, 0, :], pos_kT, writes=[bw2])
    P.dma("pool", posT[:, 1, :], pos_vT, writes=[bw2])
    hb = P.sb("hb", [128, 4], F32, es2)
    bhb = P.buf("hb")
    srcs = {"k": kcmpT, "v": vcmpT}
    hg = {}
    bhg = P.buf("hg")
    for kind in ("k", "v"):
        for g in range(2):
            for hc in range(2):
                hg[(kind, g, hc)] = P.sb(f"hg{kind}{g}{hc}", [128, 256], BF16, es2)
                P.op("pool", lambda E, t=hg[(kind, g, hc)]: E.memset(t[:], 0.0), writes=[bhg])
    cnt = 0
    if stop_after == "ph2a":
        P.barrier()
        return nc, P, dbg_out
    for ki, (kind, wsrc) in enumerate((("k", w_k1), ("v", w_v1))):
        if stop_after == "ph2b" and ki == 1:
            P.barrier()
            return nc, P, dbg_out
        if ki == 1:
            load_w1(wsrc)
        for hc in range(2):
            bank = (ki * 2 + hc) % 4
            for l in range(32):
                P.op("pe", lambda E, l=l, kind=kind, hc=hc, bank=bank, ki=ki: E.matmul(
                    psb[bank][:, 0:1], w1[kind][0:64, l, hc * 128:(hc + 1) * 128], posT[0:64, ki, l:l + 1],
                    start=(l == 0), stop=(l == 31)), reads=[bw2, bw1s], writes=[psbuf[bank]])
            P.op("act", lambda E, bank=bank, ki=ki, hc=hc: E.copy(hb[:, ki * 2 + hc:ki * 2 + hc + 1],
                                                                   psb[bank][:, 0:1]),
                 reads=[psbuf[bank]], writes=[bhb])
        src16 = srcs[kind][:, :].rearrange("p (s c) -> p s c", s=16)
        for g in range(2):
            gh = slice(g * 64, (g + 1) * 64)
            for hc in range(2):
                bank = cnt % 4
                cnt += 1
                for l in range(32):
                    P.op("pe", lambda E, l=l, kind=kind, hc=hc, bank=bank, gh=gh: E.matmul(
                        psb[bank][:, 0:255], w1[kind][gh, l, hc * 128:(hc + 1) * 128],
                        src16[gh, l % 16, l // 16:l // 16 + 255], start=(l == 0), stop=(l == 31)),
                        reads=[bw2, bw1s], writes=[psbuf[bank]])
                P.op("act", lambda E, bank=bank, kind=kind, g=g, hc=hc, ki=ki: E.activation(
                    hg[(kind, g, hc)][:, 0:255], psb[bank][:, 0:255], AF.Gelu_apprx_tanh,
                    bias=hb[:, ki * 2 + hc:ki * 2 + hc + 1]), reads=[psbuf[bank], bhb, bhg], writes=[bhg])
    bkc = P.buf("kc")
    if stop_after == "ph2c":
        P.barrier()
        return nc, P, dbg_out
    for g in range(2):
        gh = slice(g * 64, (g + 1) * 64)
        bank = 4 + g
        for hc in range(2):
            P.op("pe", lambda E, hc=hc, g=g, bank=bank: E.matmul(
                psb[bank][:, 0:256], w2[:, hc, 0, :], hg[("k", g, hc)][:, :], start=(hc == 0), stop=(hc == 1)),
                reads=[bw2, bhg], writes=[psbuf[bank]])
        P.op("act", lambda E, g=g, gh=gh, bank=bank: E.copy(KcAug[g][gh, :], psb[bank][gh, 0:256]),
             reads=[psbuf[bank]], writes=[bkc])
        for ct in range(2):
            bank2 = 6 + ct
            for hc in range(2):
                P.op("pe", lambda E, hc=hc, g=g, ct=ct, bank2=bank2: E.matmul(
                    psb[bank2][:, 0:64], hg[("v", g, hc)][:, ct * 128:(ct + 1) * 128], w2[:, hc, 1, 0:64],
                    start=(hc == 0), stop=(hc == 1)), reads=[bw2, bhg], writes=[psbuf[bank2]])
            P.op("dve", lambda E, g=g, ct=ct, bank2=bank2: E.tensor_copy(VcAug[:, ct, g, 0:64],
                                                                        psb[bank2][:, 0:64]),
                 reads=[psbuf[bank2]], writes=[bkc])
    dump("KcAug0", [128, 256], KcAug[0][:], [bkc], BF16)
    dump("KcAug1", [128, 256], KcAug[1][:], [bkc], BF16)
    dump("VcAug", [128, 2, 2, 65], VcAug[:], [bkc], BF16)
    P.barrier()
    es2.close()
    esL4.close()
    if stop_after == "ph2":
        return nc, P, dbg_out
    es3 = ES()
    uTx = [P.sb("uTx0", [128, 8, 512], BF16, es3), P.sb("uTx1", [128, 8, 128], BF16, es3)]
    uTx_b = P.bufs(2, "uTx")
    uT_seg = P.sb("uT_seg", [128, 8, OWNW], BF16, es3)
    bown = P.buf("own")
    Wq = P.sb("Wq", [128, 8, 512], BF16, es3)
    Wc = P.sb("Wc", [128, 8, 1024], BF16, es3)
    Wg = P.sb("Wg", [128, 8, 24], BF16, es3)
    bW3 = P.buf("W3")
    P.dma("pool", Wq[:], w_in_v[:, :, CQ:CQ + 512], writes=[bW3])
    P.dma("pool", Wc[:], w_in_v[:, :, CCV:CCV + 1024], writes=[bW3])
    P.dma("pool", Wg[:], w_in_v[:, :, CG:CG + 24], writes=[bW3])
    sig = [P.sb(f"sig{i}", [128, 512], F32, es3) for i in range(2)]
    sig_b = P.bufs(2, "sig")
    bext = P.buf("ext")
    bq = P.buf("QT")
    bgates = P.buf("gates")
    sigc = [0]
    bankrr = [0]

    def nextbank(lo=0, hi=6):
        b = lo + bankrr[0] % (hi - lo)
        bankrr[0] += 1
        return b

    for seg in range(NSEG):
        base = seg * OWNW
        egroups = ((0, 4), (4, 1), (5, 4), (9, 4))

        def ext_dst(ft, ntl):
            n = ntl * 128
            if ft >= NCTX:
                c0 = HALO + (ft - NCTX) * 128
                return uT_seg[:, :, c0:c0 + n], bown
            ub = (0 if ft == 0 else 1)
            return uTx[ub][:, :, 0:n], uTx_b[ub]

        def ext_front(ft, ntl):
            dstT, dstb = ext_dst(ft, ntl)
            for j in range(ntl):
                front_tile(xe[seg, (ft + j) * 128:(ft + j + 1) * 128, :], dstT[:, :, j * 128:(j + 1) * 128], dstb,
                           gmix_bc)

        def ext_proj(ft, ntl):
            n = ntl * 128
            dstT, dstb = ext_dst(ft, ntl)
            ecs = slice(ft * 128, ft * 128 + n)
            evacs = []
            if ft == 4:
                evacs.append(lambda: P.op("pool", lambda E: E.tensor_copy(uT_seg[:, :, 0:HALO], uTx[1][:, :, 96:128]),
                                          reads=[uTx_b[1]], writes=[bown]))
            for ch, dst in ((2, KxAug), (4, KwAug)):
                bank = nextbank()
                proj_fm(lambda k, ch=ch: Wkv[:, k, ch * 128:(ch + 1) * 128], lambda k: dstT[:, k, :], n, bank,
                        [bW1, dstb])

                def ev(dst=dst, bank=bank):
                    P.op("act", lambda E: E.copy(dst[0][0:64, seg, ecs], psb[bank][0:64, 0:n]),
                         reads=[psbuf[bank]], writes=[bext])
                    P.op("dve", lambda E: E.tensor_copy(dst[1][64:128, seg, ecs], psb[bank][64:128, 0:n]),
                         reads=[psbuf[bank]], writes=[bext])
                evacs.append(ev)
            for ch, dst in ((3, VxAug), (5, VwAug)):
                bank = nextbank()
                for j in range(ntl):
                    for k in range(8):
                        P.op("pe", lambda E, j=j, k=k, ch=ch, bank=bank: E.matmul(
                            psb[bank][:, j * 128:(j + 1) * 128], dstT[:, k, j * 128:(j + 1) * 128],
                            Wkv[:, k, ch * 128:(ch + 1) * 128], start=(k == 0), stop=(k == 7)),
                            reads=[bW1, dstb], writes=[psbuf[bank]])

                def ev2(dst=dst, bank=bank):
                    P.op("dve", lambda E: E.tensor_copy(
                        dst[:, seg, ft:ft + ntl, :, 0:64],
                        psb[bank][:, 0:n].rearrange("p (j g d) -> p j g d", j=ntl, g=2)),
                        reads=[psbuf[bank]], writes=[bext])
                evacs.append(ev2)
            return evacs

        ext_front(*egroups[0])
        front_tile.flush()
        for gi_, (ft, ntl) in enumerate(egroups):
            evs = ext_proj(ft, ntl)
            if gi_ + 1 < len(egroups):
                ext_front(*egroups[gi_ + 1])
            for ev_ in evs:
                ev_()
            front_tile.flush()
        cpad = cpads[seg]
        P.op("pool", lambda E: E.memset(cpad[:, :, 0:32], 0.0), writes=[bcpad[seg]])
        for (o0, n) in ((0, 352), (352, 352), (704, 352)):
            cs = slice(o0, o0 + n)
            gcs = slice(base + o0, base + o0 + n)
            for hh in range(4):
                bank = nextbank()
                proj_fm(lambda k, hh=hh: Wq[:, k, hh * 128:(hh + 1) * 128], lambda k: uT_seg[:, k, cs], n, bank,
                        [bW3, bown])
                P.op("act", lambda E, bank=bank, hh=hh: E.activation(QT[:, hh, gcs], psb[bank][:, 0:n], AF.Copy,
                                                                      scale=0.125),
                     reads=[psbuf[bank]], writes=[bq])
            for i in range(4):
                bv = nextbank()
                bg_ = nextbank()
                proj_fm(lambda k, i=i: Wc[:, k, i * 128:(i + 1) * 128], lambda k: uT_seg[:, k, cs], n, bv,
                        [bW3, bown])
                proj_fm(lambda k, i=i: Wc[:, k, 512 + i * 128:512 + (i + 1) * 128], lambda k: uT_seg[:, k, cs],
                        n, bg_, [bW3, bown])
                si = sigc[0] % 2
                sigc[0] += 1
                P.op("act", lambda E, si=si, bg_=bg_: E.activation(sig[si][:, 0:n], psb[bg_][:, 0:n], AF.Sigmoid),
                     reads=[psbuf[bg_]], writes=[sig_b[si]])
                P.op("dve", lambda E, si=si, bv=bv, i=i: E.tensor_tensor(
                    cpad[:, i, 32 + o0:32 + o0 + n], psb[bv][:, 0:n], sig[si][:, 0:n], ALU.mult),
                    reads=[psbuf[bv], sig_b[si]], writes=[bcpad[seg]])
        for ti in range(9):
            qw = HALO if ti == 0 else 128
            c0 = 0 if ti == 0 else HALO + (ti - 1) * 128
            bank = nextbank()
            for k in range(8):
                P.op("pe", lambda E, k=k, bank=bank: E.matmul(psb[bank][0:qw, 0:24], uT_seg[:, k, c0:c0 + qw],
                                                              Wg[:, k, :], start=(k == 0), stop=(k == 7)),
                     reads=[bW3, bown], writes=[psbuf[bank]])
            P.op("act", lambda E, bank=bank, ti=ti: E.activation(gates[0:qw, seg * 9 + ti, :], psb[bank][0:qw, 0:24],
                                                                  AF.Sigmoid),
                 reads=[psbuf[bank]], writes=[bgates])
        P.dma("sp", uT_scr[:, :, base:base + OWNW], uT_seg[:], reads=[bown], writes=[bscr_u[seg]])
    dump("QT", [128, 4, OWN], QT[:], [bq], BF16)
    dump("gates", [128, NOT_, 24], gates[:], [bgates])
    dump("KwAug0", [128, NSEG, NEXT * 128], KwAug[0][:], [bext], BF16)
    dump("KxAug1", [128, NSEG, NEXT * 128], KxAug[1][:], [bext], BF16)
    dump("VwAug", [128, NSEG, NEXT, 2, 65], VwAug[:], [bext], BF16)
    dump("cpad0", [128, 4, 32 + OWNW], cpads[0][:], [bcpad[0]], BF16)
    P.barrier()
    es3.close()
    esL3.close()
    if stop_after == "ph3":
        return nc, P, dbg_out
    esCF = ES()
    cw = P.sb("cw", [128, 4, 31], F32, esCF)
    cb = P.sb("cb", [128, 4], F32, esCF)
    lng = P.sb("lng", [128, 4], F32, esCF)
    lnb = P.sb("lnb", [128, 4], F32, esCF)
    bWc = P.buf("Wcf")
    P.dma("sp", cw[:], conv_wT, writes=[bWc])
    with nc.allow_non_contiguous_dma(reason="tiny per-channel vectors"):
        P.dma("sp", cb[:], conv_b, writes=[bWc])
        P.dma("sp", lng[:], ln_g, writes=[bWc])
        P.dma("sp", lnb[:], ln_b, writes=[bWc])
    ones_bf = P.sb("ones_bf", [128, 128], BF16, esCF)
    P.op("dve", lambda E: E.memset(ones_bf[:], 1.0), writes=[bWc])
    cconv = P.sb("cconv", [128, 4, OWNW], F32, esCF)
    cc16 = P.sb("cc16", [128, 4, OWNW], BF16, esCF)
    csq16 = P.sb("csq16", [128, 4, OWNW], BF16, esCF)
    cT_seg = P.sb("cT_seg", [128, 4, OWNW], BF16, esCF)
    bcc = P.bufs(4, "cc")
    bcTs = P.buf("cTs")
    diag = P.sb("diag", [128, 4, 31, 128], BF16, esCF)
    diag_bs = P.bufs(4, "diag")
    lnm = P.sb("lnm", [128, 512], F32, esCF)
    lnr = P.sb("lnr", [128, 512], F32, esCF)
    lnt = [P.sb(f"lnt{i}", [128, 512], F32, esCF) for i in range(2)]
    bln = P.buf("ln")
    lnt_b = P.bufs(2, "lnt")
    diagc = [0]
    groups = ((0, 352), (352, 352), (704, 352))
    for i in range(4):
        for k in range(31):
            P.op("dve", lambda E, i=i, k=k: E.tensor_scalar(diag[:, i, k, :], ident[:], cw[:, i, 30 - k:31 - k], None,
                                                            ALU.mult), reads=[bWc, bconst], writes=[])
    P.op("dve", lambda E: E.memset(lnm[:, 0:8], 0.0), writes=diag_bs)
    for seg in range(NSEG):
        base = seg * OWNW
        cpad = cpads[seg]
        for i in range(4):
            banks = (0, 1, 2) if i % 2 == 0 else (3, 4, 5)
            for k in range(31):
                for gi, (o0, n) in enumerate(groups):
                    P.op("pe", lambda E, i=i, k=k, gi=gi, o0=o0, n=n: E.matmul(
                        psb[banks[gi]][:, 0:n], diag[:, i, k, :], cpad[:, i, 32 + o0 - k:32 + o0 - k + n],
                        start=(k == 0), stop=(k == 30)), reads=[diag_bs[i], bcpad[seg]],
                        writes=[psbuf[banks[gi]]])
            for gi, (o0, n) in enumerate(groups):
                bk = banks[gi]
                P.op("act", lambda E, bk=bk, i=i, o0=o0, n=n: E.activation(
                    cconv[:, i, o0:o0 + n], psb[bk][:, 0:n], AF.Identity, bias=cb[:, i:i + 1]),
                    reads=[psbuf[bk], bWc], writes=[bcc[i]])
                P.op("act", lambda E, bk=bk, i=i, o0=o0, n=n: E.activation(
                    csq16[:, i, o0:o0 + n], psb[bk][:, 0:n], AF.Square, bias=cb[:, i:i + 1]),
                    reads=[psbuf[bk], bWc], writes=[bcc[i]])
                P.op("dve", lambda E, i=i, o0=o0, n=n: E.tensor_copy(cc16[:, i, o0:o0 + n], cconv[:, i, o0:o0 + n]),
                     reads=[bcc[i]], writes=[bcc[i]])
        dump(f"cconv{seg}", [128, 4, OWNW], cconv[:], bcc)
        for (o0, n) in groups:
            for i in range(4):
                P.op("pe", lambda E, i=i: E.matmul(psb[6][:, 0:n], ones_bf[:], cc16[:, i, o0:o0 + n],
                                                   start=(i == 0), stop=(i == 3)),
                     reads=[bWc, bcc[i]], writes=[psbuf[6]])
            for i in range(4):
                P.op("pe", lambda E, i=i: E.matmul(psb[7][:, 0:n], ones_bf[:], csq16[:, i, o0:o0 + n],
                                                   start=(i == 0), stop=(i == 3)),
                     reads=[bWc, bcc[i]], writes=[psbuf[7]])
            P.op("dve", lambda E: E.tensor_scalar(lnm[:, 0:n], psb[6][:, 0:n], 1.0 / 512, None, ALU.mult),
                 reads=[psbuf[6]], writes=[bln])
            P.op("dve", lambda E: E.tensor_tensor(lnr[:, 0:n], lnm[:, 0:n], lnm[:, 0:n], ALU.mult),
                 reads=[bln], writes=[bln])
            P.op("dve", lambda E: E.scalar_tensor_tensor(lnr[:, 0:n], psb[7][:, 0:n], 1.0 / 512, lnr[:, 0:n],
                                                         ALU.mult, ALU.subtract),
                 reads=[psbuf[7], bln], writes=[bln])
            P.op("act", lambda E: E.activation(lnr[:, 0:n], lnr[:, 0:n], AF.Sqrt, bias=eps_t[:, 0:1]),
                 reads=[bln, bconst], writes=[bln])
            P.op("dve", lambda E: E.reciprocal(lnr[:, 0:n], lnr[:, 0:n]), reads=[bln], writes=[bln])
            for i in range(4):
                ti_ = i % 2
                P.op("dve", lambda E, i=i, ti_=ti_: E.tensor_tensor(lnt[ti_][:, 0:n], cconv[:, i, o0:o0 + n],
                                                                    lnm[:, 0:n], ALU.subtract),
                     reads=[bcc[i], bln], writes=[lnt_b[ti_]])
                P.op("dve", lambda E, ti_=ti_: E.tensor_tensor(lnt[ti_][:, 0:n], lnt[ti_][:, 0:n], lnr[:, 0:n],
                                                               ALU.mult),
                     reads=[bln, lnt_b[ti_]], writes=[lnt_b[ti_]])
                P.op("act", lambda E, i=i, ti_=ti_: E.activation(
                    cT_seg[:, i, o0:o0 + n], lnt[ti_][:, 0:n], AF.Silu, bias=lnb[:, i:i + 1],
                    scale=lng[:, i:i + 1]), reads=[lnt_b[ti_], bWc], writes=[bcTs])
        P.dma("sp", cT_scr[:, :, base:base + OWNW], cT_seg[:], reads=[bcTs], writes=[bscr_c[seg]])
        dump(f"cT{seg}", [128, 4, OWNW], cT_seg[:], [bcTs], BF16)
    P.barrier()
    esCF.close()
    esCP.close()
    if stop_after == "conf":
        return nc, P, dbg_out
    esP2c = ES()
    fcw_t = P.sb("fcw_t", [128, 44, 3], F32, esP2c, side="right")
    fcb_t = P.sb("fcb_t", [128, 44], F32, esP2c, side="right")
    gffn_bc = P.sb("gffn_bc", [128, D], F32, esP2c, side="right")
    gple_bc = P.sb("gple_bc", [128, D], F32, esP2c, side="right")
    gfin_bc = P.sb("gfin_bc", [128, D], F32, esP2c, side="right")
    bWp2 = P.buf("Wp2")
    P.dma("sp", fcw_t[:], fcw, writes=[bWp2])
    with nc.allow_non_contiguous_dma(reason="tiny per-channel vector"):
        P.dma("sp", fcb_t[:], fcb, writes=[bWp2])
    P.dma("sp", gffn_bc[:], g_ffn.partition_broadcast(128), writes=[bWp2])
    P.dma("sp", gple_bc[:], g_ple.partition_broadcast(128), writes=[bWp2])
    P.dma("sp", gfin_bc[:], g_fin.partition_broadcast(128), writes=[bWp2])
    esP1w = ES()
    Wo = P.sb("Wo", [128, 4, D], BF16, esP1w, side="right")
    Wco = P.sb("Wco", [128, 4, D], BF16, esP1w, side="right")
    Wm = P.sb("Wm", [128, 8, 2048], BF16, esP1w, side="right")
    Wout = P.sb("Wout", [128, 8, D], BF16, esP1w, side="right")
    bco_t = P.sb("bco_t", [128, 8], F32, esP1w, side="right")
    bWp1 = P.buf("Wp1")
    bWp1b = P.buf("Wp1b")
    P.dma("pool", Wo[:], w_o.rearrange("(k p) n -> p k n", p=128), writes=[bWp1])
    P.dma("pool", Wco[:], w_co.rearrange("(k p) n -> p k n", p=128), writes=[bWp1])
    P.dma("pool", Wm[:, :, 0:1024], w_in_v[:, :, CM:CM + 1024], writes=[bWp1])
    P.dma("pool", Wm[:, :, 1024:2048], w_in_v[:, :, CM + 1024:CM + 2048], writes=[bWp1])
    P.dma("pool", Wout[:], w_out.rearrange("(k p) n -> p k n", p=128), writes=[bWp1])
    with nc.allow_non_contiguous_dma(reason="tiny per-channel vector"):
        P.dma("sp", bco_t[:], b_co, writes=[bWp1b])
    esT = ES()
    bA = P.buf("attc")
    Qaug = [[P.sb(f"Qaug{g}_{i}", [128, 4, 128], BF16, esT) for i in range(2)] for g in range(2)]
    Qq_b = [P.bufs(2, f"Qq{g}") for g in range(2)]
    Qm_b = [P.bufs(2, f"Qm{g}") for g in range(2)]
    for g in range(2):
        for i in range(2):
            P.op("pool", lambda E, g=g, i=i: E.memset(Qaug[g][i][:], 0.0), writes=[Qq_b[g][i], Qm_b[g][i]])
    NPT = 4
    PT = [P.sb(f"PT{i}", [128, 512], BF16, esT) for i in range(NPT)]
    PT_b = P.bufs(NPT, "PT")
    cmk = [P.sb(f"cmk{i}", [128, 2, 128], BF16, esT) for i in range(2)]
    cmk_b = P.bufs(2, "cmk")
    slc = [P.sb(f"slc{i}", [128, 3, 64], F32, esT) for i in range(2)]
    slc_b = P.bufs(2, "slc")
    mcaus = P.sb("mcaus", [128, 128], BF16, esT)
    mfar = P.sb("mfar", [128, 128], BF16, esT)
    selmap = P.sb("selmap", [128, 2, 64], BF16, esT)
    zeros_bf = P.sb("zeros_bf", [128, 320], BF16, esT)
    P.dma("sp", mcaus[:], c_mcausal, writes=[bA])
    P.dma("sp", mfar[:], c_mfar, writes=[bA])
    P.dma("sp", selmap[:], c_selmap, writes=[bA])
    P.op("pool", lambda E: E.memset(zeros_bf[:], 0.0), writes=[bA])
    obf = [P.sb(f"obf{i}", [128, 512], BF16, esT) for i in range(2)]
    obf_b = P.bufs(2, "obf")
    oacc = P.sb("oacc", [128, 2, 256], F32, esT)
    otmp = P.sb("otmp", [128, 2, 256], F32, esT)
    rden = P.sb("rden", [128, 2, 3, 4], F32, esT)
    scl = P.sb("scl", [128, 2, 3, 4], F32, esT)
    imps = P.sb("imps", [128, 2, 64], F32, esT)
    impadj = P.sb("impadj", [128, 2, 64], F32, esT)
    tmpm = P.sb("tmpm", [128, 2, 64], F32, esT)
    self_ = P.sb("self_", [128, 2, 64], F32, esT)
    m8a = P.sb("m8a", [128, 2, 8], F32, esT)
    m8b = P.sb("m8b", [128, 2, 8], F32, esT)
    mnegw = P.sb("mnegw", [128, 2, 128], F32, esT)
    bsel = P.bufs(2, "sel")
    bcomb = P.bufs(2, "comb")
    oTt = [P.sb(f"oTt{i}", [128, 4, 128], BF16, esT) for i in range(2)]
    oTt_b = P.bufs(2, "oTt")
    otc = [0]
    S_BANKS = (0, 1, 2)
    OC, OS, OW, MISC, TR = 3, 4, 5, 6, 7
    sctr = [0]
    pctr = [0]
    psT7 = psb[TR].bitcast(BF16)
    attn_tiles = []
    for seg in range(NSEG):
        for ti in range(9):
            attn_tiles.append((seg, ti))
    if stop_after == "att1":
        attn_tiles = attn_tiles[:3]
    tcount = 0
    units_meta = []
    LA = 3
    pending = []
    deferred = []

    def push(s_fn, pv_fn):
        pi = s_fn()
        pending.append((pv_fn, pi))
        while len(pending) > LA:
            f, p_ = pending.pop(0)
            f(p_)

    def flush():
        while pending:
            f, p_ = pending.pop(0)
            f(p_)

    for (seg, ti) in attn_tiles:
        qw = HALO if ti == 0 else 128
        qc0 = seg * OWNW + (0 if ti == 0 else HALO + (ti - 1) * 128)
        T = NCTX - 1 if ti == 0 else NCTX + ti - 1
        qoff = 128 - HALO if ti == 0 else 0
        n_main = max(max(SEG_STARTS[r][seg] // 128 + (ti - 1 if ti >= 1 else -1), 0) for r in (0, 1))
        tidx = seg * 9 + ti
        sl = tcount % 2
        tcount += 1
        units_meta.append((seg, ti, qw, qc0, T, qoff, n_main, tidx, sl))

    def make_unit(seg, ti, qw, qc0, T, qoff, n_main, tidx, sl, g):
        gh = slice(g * 64, (g + 1) * 64)
        oh = slice((1 - g) * 64, (2 - g) * 64)
        Qa = Qaug[g][sl]
        qb, mb = Qq_b[g][sl], Qm_b[g][sl]
        rhsQ = Qa[:, :, 0:qw]
        bs_ = bsel[g]
        bc_ = bcomb[g]

        def s_step(lhsT, qbufs, mask=None, mbufs=()):
            sbk = S_BANKS[sctr[0] % len(S_BANKS)]
            sctr[0] += 1
            o3 = psb[sbk][:, 0:4 * qw].rearrange("p (h q) -> p h q", h=4)
            P.op("pe", lambda E: E.matmul(o3, lhsT, rhsQ, start=True, stop=(mask is None)),
                 reads=list(qbufs), writes=[psbuf[sbk]])
            if mask is not None:
                P.op("pe", lambda E: E.matmul(o3, ident[:], mask.unsqueeze(1).broadcast_to([128, 4, qw]),
                                              start=False, stop=True),
                     reads=[bconst, bA] + list(mbufs), writes=[psbuf[sbk]])
            pi = pctr[0] % NPT
            pctr[0] += 1
            P.op("act", lambda E: E.activation(PT[pi][:, 0:4 * qw], psb[sbk][:, 0:4 * qw], AF.Exp),
                 reads=[psbuf[sbk]], writes=[PT_b[pi]])
            return pi

        def zero_bank(bank, ncols):
            P.op("pe", lambda E: E.matmul(psb[bank][0:qw, 0:ncols], zeros_bf[:, 0:qw], zeros_bf[:, 0:ncols],
                                          start=True, stop=False, skip_group_check=True),
                 reads=[bA], writes=[psbuf[bank]])

        def pv(bank, pi, V_ap, last, w=65):
            for h in range(4):
                P.op("pe", lambda E, h=h: E.matmul(psb[bank][0:qw, h * w:(h + 1) * w],
                                                   PT[pi][:, h * qw:(h + 1) * qw], V_ap,
                                                   start=False, stop=last, skip_group_check=True),
                     reads=[PT_b[pi]], writes=[psbuf[bank]])

        def bview(bank):
            return psb[bank][0:qw, 0:260].rearrange("p (h e) -> p h e", e=65)[:, :, 0:64]

        def sview(br):
            return scl[0:qw, g, br, :].unsqueeze(2).broadcast_to([qw, 4, 64])

        oa = oacc[0:qw, g, :].rearrange("p (h d) -> p h d", h=4)
        ot = otmp[0:qw, g, :].rearrange("p (h d) -> p h d", h=4)
        ob = obf[sl][0:qw, g * 256:(g + 1) * 256].rearrange("p (h d) -> p h d", h=4)

        def den_scale(br, bank):
            bv = psb[bank][0:qw, 0:260].rearrange("p (h e) -> p h e", e=65)
            if br != 0:
                P.op("dve", lambda E: E.tensor_scalar(rden[0:qw, g, br, :], bv[:, :, 64], 1e-30, None, ALU.add),
                     reads=[psbuf[bank], bs_], writes=[bs_])
                P.op("dve", lambda E: E.reciprocal(rden[0:qw, g, br, :], rden[0:qw, g, br, :]),
                     reads=[bs_], writes=[bs_])
            P.op("dve", lambda E: E.tensor_tensor(
                scl[0:qw, g, br, :], rden[0:qw, g, br, :],
                gates[0:qw, tidx, br * 8 + g * 4:br * 8 + g * 4 + 4], ALU.mult),
                reads=[bs_], writes=[bs_])

        def cmp_pv(ct):
            def f(pi):
                if ct == 0:
                    zero_bank(OC, 260)
                    zero_bank(MISC, 256)
                pv(OC, pi, VcAug[:, ct, g, :], ct == 1)
                pv(MISC, pi, selmap[:, ct, :], ct == 1, w=64)
                if ct == 1:
                    selection()
            return f

        def selection():
            ocv = psb[OC][0:qw, 0:260].rearrange("p (h e) -> p h e", e=65)
            P.op("dve", lambda E: E.tensor_scalar(rden[0:qw, g, 0, :], ocv[:, :, 64], 1e-30, None, ALU.add),
                 reads=[psbuf[OC]], writes=[bs_])
            P.op("dve", lambda E: E.reciprocal(rden[0:qw, g, 0, :], rden[0:qw, g, 0, :]), reads=[bs_],
                 writes=[bs_])
            P.op("dve", lambda E: E.tensor_scalar(imps[0:qw, g, :], psb[MISC][0:qw, 0:64], rden[0:qw, g, 0, 0:1],
                                                  None, ALU.mult), reads=[psbuf[MISC], bs_], writes=[bs_])
            for h in range(1, 4):
                P.op("dve", lambda E, h=h: E.scalar_tensor_tensor(
                    imps[0:qw, g, :], psb[MISC][0:qw, h * 64:(h + 1) * 64], rden[0:qw, g, 0, h:h + 1],
                    imps[0:qw, g, :], ALU.mult, ALU.add), reads=[psbuf[MISC], bs_], writes=[bs_])
            P.op("dve", lambda E: E.tensor_tensor(impadj[0:qw, g, :], imps[0:qw, g, :], slc[sl][0:qw, 0, :],
                                                  ALU.mult), reads=[bs_, slc_b[sl]], writes=[bs_])
            P.op("dve", lambda E: E.tensor_tensor(impadj[0:qw, g, :], impadj[0:qw, g, :], slc[sl][0:qw, 1, :],
                                                  ALU.add), reads=[bs_, slc_b[sl]], writes=[bs_])
            P.op("dve", lambda E: E.max(m8a[0:qw, g, :], impadj[0:qw, g, :]), reads=[bs_], writes=[bs_])
            P.op("dve", lambda E: E.match_replace(tmpm[0:qw, g, :], m8a[0:qw, g, :], impadj[0:qw, g, :], -1e9),
                 reads=[bs_], writes=[bs_])
            P.op("dve", lambda E: E.max(m8b[0:qw, g, :], tmpm[0:qw, g, :]), reads=[bs_], writes=[bs_])
            P.op("dve", lambda E: E.tensor_scalar(self_[0:qw, g, :], impadj[0:qw, g, :], m8b[0:qw, g, 7:8], None,
                                                  ALU.is_ge), reads=[bs_], writes=[bs_])
            for half in range(2):
                P.op("dve", lambda E, half=half: E.scalar_tensor_tensor(
                    mnegw[0:qw, g, half * 64:(half + 1) * 64], self_[0:qw, g, :], BIG, slc[sl][0:qw, 2, :],
                    ALU.mult, ALU.add), reads=[bs_, slc_b[sl]], writes=[bs_])
            den_scale(0, OC)
            P.op("dve", lambda E: E.tensor_tensor(oa, bview(OC), sview(0), ALU.mult),
                 reads=[psbuf[OC], bs_], writes=[bc_])


        def c_part():
            if g == 0:
                P.dma("sp", cmk[sl][:, :, 0:qw], c_cmask[:, :, qc0:qc0 + qw], writes=[cmk_b[sl]])
                P.dma("sp", slc[sl][0:qw], c_selc[qc0:qc0 + qw], writes=[slc_b[sl]])
            P.op("pool", lambda E: E.tensor_copy(Qa[gh, :, 0:qw], QT[gh, :, qc0:qc0 + qw]), writes=[qb])
            for ct in range(2):
                push(lambda ct=ct: s_step(KcAug[g][:, ct * 128:(ct + 1) * 128], [qb], cmk[sl][:, ct, 0:qw],
                                          [cmk_b[sl]]), cmp_pv(ct))
            while deferred:
                deferred.pop(0)()

        def w_part():
            wsteps = [(T - 4, mfar)] + [(T - j, None) for j in (3, 2, 1)] + [(T, mcaus)]

            def win_pv(wi, kt):
                def f(pi):
                    if wi == 0:
                        zero_bank(OW, 260)
                    last = wi == len(wsteps) - 1
                    pv(OW, pi, VwAug[:, seg, kt, g, :], last)
                    if last:
                        den_scale(2, OW)
                        P.op("dve", lambda E: E.tensor_tensor(ot, bview(OW), sview(2), ALU.mult),
                             reads=[psbuf[OW], bs_], writes=[bc_])
                        P.op("dve", lambda E: E.tensor_tensor(oa, oa, ot, ALU.add), reads=[bc_], writes=[bc_])
                return f

            for wi, (kt, msk) in enumerate(wsteps):
                m_ap = None if msk is None else msk[:, qoff:qoff + qw]
                push(lambda kt=kt, m_ap=m_ap: s_step(KwAug[g][:, seg, kt * 128:(kt + 1) * 128], [qb], m_ap),
                     win_pv(wi, kt))

        def s_part():
            P.op("pe", lambda E: E.transpose(psb[MISC][:, 256:256 + qw], mnegw[0:qw, g, :], identf[0:qw, 0:qw]),
                 reads=[bs_, bconst], writes=[psbuf[MISC]])
            P.op("act", lambda E: E.copy(Qa[oh, :, 0:qw],
                                         psb[MISC][oh, 256:256 + qw].unsqueeze(1).broadcast_to([64, 4, qw])),
                 reads=[psbuf[MISC]], writes=[mb])
            def sel_pv(idx, V_ap):
                def f(pi):
                    if idx == 0:
                        zero_bank(OS, 260)
                    last = idx == n_main
                    pv(OS, pi, V_ap, last)
                    if last:
                        den_scale(1, OS)
                        P.op("dve", lambda E: E.tensor_tensor(ot, bview(OS), sview(1), ALU.mult),
                             reads=[psbuf[OS], bs_], writes=[bc_])
                        P.op("dve", lambda E: E.tensor_tensor(ob, oa, ot, ALU.add), reads=[bc_],
                             writes=[bc_, obf_b[sl]])
                        if g == 1:
                            deferred.append(o_transposes)
                return f

            def o_transposes():
                for c in range(4):
                    P.op("pe", lambda E, c=c: E.transpose(psT7[:, 512 + c * 128:512 + c * 128 + qw],
                                                          obf[sl][0:qw, c * 128:(c + 1) * 128], ident[0:qw, 0:qw]),
                         reads=[obf_b[sl], bconst], writes=[psbuf[TR]])
                oi = otc[0] % 2
                otc[0] += 1
                P.op("dve", lambda E: E.tensor_copy(oTt[oi][:, :, 0:qw],
                                             psT7[:, 512:1024].rearrange("p (c t) -> p c t", c=4)[:, :, 0:qw]),
                     reads=[psbuf[TR]], writes=[oTt_b[oi]])
                P.dma("sp", oT_scr[:, :, qc0:qc0 + qw], oTt[oi][:, :, 0:qw], reads=[oTt_b[oi]], writes=[boT],
                      sembuf=oTt_b[oi])

            for kt in range(n_main):
                push(lambda kt=kt: s_step(KsAug[g][:, kt * 128:(kt + 1) * 128], [qb, mb]),
                     sel_pv(kt, VsAug[:, kt, g, :]))
            push(lambda: s_step(KxAug[g][:, seg, T * 128:(T + 1) * 128], [qb, mb], mcaus[:, qoff:qoff + qw]),
                 sel_pv(n_main, VxAug[:, seg, T, g, :]))


        return c_part, w_part, s_part

    units = [make_unit(*m, g) for m in units_meta for g in range(2)]
    for ui, (cp, wp, sp_) in enumerate(units):
        if ui == 0:
            cp()
        wp()
        if ui + 1 < len(units):
            units[ui + 1][0]()
        sp_()
    flush()
    while deferred:
        deferred.pop(0)()
    if "oT" in debug:
        t_ = nc.dram_tensor("dbg_oT", [128, 4, OWN], BF16, kind="ExternalOutput").ap()
        P.dma("sp", t_, oT_scr, reads=[boT], writes=[P.buf("dbg_oT")])
    P.barrier()
    esT.close()
    esL2.close()
    if stop_after in ("att", "att1"):
        return nc, P, dbg_out
    esP2w = ES()
    NJ = 11
    WPIECES = ((0, 4), (4, 4), (8, 3))
    Wup = P.sb("Wup", [128, 8, 2 * NJ * 128], BF16, esP2w)
    bWup = P.bufs(len(WPIECES), "Wup")
    w_up_v = w_up.rearrange("(k p) n -> p k n", p=128)

    def load_wup(half):
        j0_ = half * NJ
        for pi_, (pj, pn) in enumerate(WPIECES):
            P.dma("pool", Wup[:, :, pj * 128:(pj + pn) * 128],
                  w_up_v[:, :, (j0_ + pj) * 128:(j0_ + pj + pn) * 128], writes=[bWup[pi_]])
            P.dma("pool", Wup[:, :, (NJ + pj) * 128:(NJ + pj + pn) * 128],
                  w_up_v[:, :, DFF + (j0_ + pj) * 128:DFF + (j0_ + pj + pn) * 128], writes=[bWup[pi_]])

    esP1 = ES()
    load_wup(0)
    oTr = [P.sb(f"oTr{i}", [128, 4, 512], BF16, esP1) for i in range(2)]
    oTr_b = P.bufs(2, "oTr")
    uTt = [P.sb(f"uTt{i}", [128, 8, 512], BF16, esP1) for i in range(2)]
    uTt_b = P.bufs(2, "uTt")
    cTt = [P.sb(f"cTt{i}", [128, 4, 512], BF16, esP1) for i in range(2)]
    cTt_b = P.bufs(2, "cTt")
    mT = P.sb("mT", [128, 8, 512], BF16, esP1)
    mT_b = P.bufs(8, "mT")
    gab = [P.sb(f"gab{i}", [128, 2, 512], F32, esP1) for i in range(2)]
    gab_b = P.bufs(2, "gab")
    t12 = [P.sb(f"t12{i}", [128, 2, 512], F32, esP1) for i in range(2)]
    t12_b = P.bufs(2, "t12")
    xres = [P.sb(f"xres{i}", [128, D], F32, esP1) for i in range(2)]
    xres_b = P.bufs(2, "xres")
    h1t = [P.sb(f"h1t{i}", [128, D], F32, esP1) for i in range(2)]
    h1t_b = P.bufs(2, "h1t")
    bh1s = [P.buf(f"h1scr{s}") for s in range(NSEG)]
    brr = [0]

    def nb():
        b = brr[0] % 8
        brr[0] += 1
        return b

    gcnt = 0
    fcnt = 0
    tcnt = 0
    p1groups = [(seg, o0, n) for seg in range(NSEG) for (o0, n) in ((0, HALO + 256), (HALO + 256, 384), (HALO + 640, 384))]

    def p1_tiles(o0, n):
        tl = []
        c = 0
        if o0 == 0:
            tl.append((0, HALO))
            c = HALO
        while c < n:
            tl.append((c, 128))
            c += 128
        return tl

    def p1_loads(gidx):
        seg_, o0_, n_ = p1groups[gidx]
        base_ = seg_ * OWNW
        gi_ = gidx % 2
        P.dma("sp", uTt[gi_][:, :, 0:n_], uT_scr[:, :, base_ + o0_:base_ + o0_ + n_], reads=[bscr_u[seg_]],
              writes=[uTt_b[gi_]])
        P.dma("sp", cTt[gi_][:, :, 0:n_], cT_scr[:, :, base_ + o0_:base_ + o0_ + n_], reads=[bscr_c[seg_]],
              writes=[cTt_b[gi_]])
        P.dma("sp", oTr[gi_][:, :, 0:n_], oT_scr[:, :, base_ + o0_:base_ + o0_ + n_], reads=[boT],
              writes=[oTr_b[gi_]])

    p1_loads(0)
    for gidx, (seg, o0, n) in enumerate(p1groups):
        base = seg * OWNW
        if True:
            gi = gcnt % 2
            gcnt += 1
            if gidx + 1 < len(p1groups):
                p1_loads(gidx + 1)
            for f in range(8):
                fs = slice(f * 128, (f + 1) * 128)
                bA_, bB_, bC_, bD_ = nb(), nb(), nb(), nb()
                proj_fm(lambda k: Wo[:, k, fs], lambda k: oTr[gi][:, k, 0:n], n, bA_, [bWp1, oTr_b[gi]], nk=4)
                proj_fm(lambda k: Wco[:, k, fs], lambda k: cTt[gi][:, k, 0:n], n, bB_, [bWp1, cTt_b[gi]], nk=4)
                proj_fm(lambda k: Wm[:, k, fs], lambda k: uTt[gi][:, k, 0:n], n, bC_, [bWp1, uTt_b[gi]])
                proj_fm(lambda k: Wm[:, k, 1024 + f * 128:1024 + (f + 1) * 128], lambda k: uTt[gi][:, k, 0:n], n, bD_,
                        [bWp1, uTt_b[gi]])
                fi = fcnt % 2
                fcnt += 1
                P.op("act", lambda E: E.activation(gab[fi][:, 0, 0:n], psb[bC_][:, 0:n], AF.Sigmoid),
                     reads=[psbuf[bC_]], writes=[gab_b[fi]])
                P.op("act", lambda E: E.activation(gab[fi][:, 1, 0:n], psb[bD_][:, 0:n], AF.Sigmoid),
                     reads=[psbuf[bD_]], writes=[gab_b[fi]])
                P.op("dve", lambda E: E.tensor_tensor(t12[fi][:, 0, 0:n], psb[bA_][:, 0:n], gab[fi][:, 0, 0:n], ALU.mult),
                     reads=[psbuf[bA_], gab_b[fi]], writes=[t12_b[fi]])
                P.op("dve", lambda E: E.scalar_tensor_tensor(t12[fi][:, 1, 0:n], psb[bB_][:, 0:n], bco_t[:, f:f + 1],
                                                             gab[fi][:, 1, 0:n], ALU.add, ALU.mult),
                     reads=[psbuf[bB_], gab_b[fi], bWp1b], writes=[t12_b[fi]])
                P.op("pool", lambda E: E.tensor_tensor(mT[:, f, 0:n], t12[fi][:, 0, 0:n], t12[fi][:, 1, 0:n], ALU.add),
                     reads=[t12_b[fi]], writes=[mT_b[f]])
            tiles = p1_tiles(o0, n)
            erow0 = NCTX * 128 - HALO + o0
            t0_, w0_ = tiles[0]
            P.dma("sp", xres[tcnt % 2][0:w0_, :], xe[seg, erow0 + t0_:erow0 + t0_ + w0_, :], writes=[xres_b[tcnt % 2]])
            for j, (toff, tw) in enumerate(tiles):
                xi = tcnt % 2
                tcnt += 1
                if j + 1 < len(tiles):
                    t1_, w1_ = tiles[j + 1]
                    P.dma("sp", xres[tcnt % 2][0:w1_, :], xe[seg, erow0 + t1_:erow0 + t1_ + w1_, :],
                          writes=[xres_b[tcnt % 2]])
                for hf in range(2):
                    bk = nb()
                    for f in range(8):
                        P.op("pe", lambda E, f=f: E.matmul(psb[bk][0:tw, :], mT[:, f, toff:toff + tw],
                                                           Wout[:, f, hf * 512:(hf + 1) * 512],
                                                           start=(f == 0), stop=(f == 7)),
                             reads=[mT_b[f], bWp1], writes=[psbuf[bk]])
                    P.op("dve", lambda E: E.tensor_tensor(h1t[xi][0:tw, hf * 512:(hf + 1) * 512], psb[bk][0:tw, :],
                                                          xres[xi][0:tw, hf * 512:(hf + 1) * 512], ALU.add),
                         reads=[psbuf[bk], xres_b[xi]], writes=[h1t_b[xi]])
                P.dma("sp", h1_scr[seg, o0 + toff:o0 + toff + tw, :], h1t[xi][0:tw, :], reads=[h1t_b[xi]],
                      writes=[bh1s[seg]], sembuf=h1t_b[xi])
    if "h1" in debug:
        t = nc.dram_tensor("dbg_h1", [NSEG, OWNW, D], F32, kind="ExternalOutput").ap()
        bd = P.buf("dbg_h1")
        P.dma("sp", t, h1_scr, reads=bh1s, writes=[bd])
    P.barrier()
    esP1.close()
    esP1w.close()
    if stop_after == "p1":
        return nc, P, dbg_out
    esP2 = ES()
    Wdn = P.sb("Wdn", [128, NJ, D], BF16, esP2)
    bWdn = P.bufs(len(WPIECES), "Wdn")
    w_dn_v = w_down.rearrange("(j p) n -> p j n", p=128)

    def load_wdn(half):
        j0_ = half * NJ
        for pi_, (pj, pn) in enumerate(WPIECES):
            P.dma("pool", Wdn[:, pj:pj + pn, :], w_dn_v[:, j0_ + pj:j0_ + pj + pn, :], writes=[bWdn[pi_]])

    def piece_of(jj):
        for pi_, (pj, pn) in enumerate(WPIECES):
            if pj <= jj < pj + pn:
                return pi_

    load_wdn(0)
    Wpg = P.sb("Wpg", [128, 8, D], BF16, esP2)
    Wpl = P.sb("Wpl", [128, 2, D], BF16, esP2)
    bWp3 = P.buf("Wp3")
    P.dma("pool", Wpg[:], w_pg.rearrange("(k p) n -> p k n", p=128), writes=[bWp3])
    P.dma("pool", Wpl[:], w_ple.rearrange("(k p) n -> p k n", p=128), writes=[bWp3])
    front2 = make_front(esP2, "b", with_xs=False, NUTM=4)
    front3 = make_front(esP2, "c", with_xs=False, NUTM=2)
    tails = [P.sb(f"tail{i}", [128, 2 * NJ, 2], F32, esP2) for i in range(2)]
    btails = [P.bufs(2 * NJ, f"tail{i}") for i in range(2)]
    h1f = [P.sb(f"h1f{i}", [128, D], F32, esP2) for i in range(2)]
    h1f_b = P.bufs(2, "h1f")
    u2T = [P.sb(f"u2T{i}", [128, 8, 512], BF16, esP2) for i in range(2)]
    u2T_b = P.bufs(2, "u2T")
    u2h = P.sb("u2h", [128, 8, HALO], BF16, esP2)
    u2h_b = P.buf("u2h")
    aT = P.sb("aT", [128, NJ, 512], BF16, esP2)
    aT_b = P.bufs(NJ, "aT")
    vbuf = [[P.sb(f"vbuf{kd}{i}", [128, 516], F32, esP2) for i in range(2)] for kd in range(2)]
    vbuf_b = [P.bufs(2, f"vbuf{kd}") for kd in range(2)]
    cgv = [[P.sb(f"cgv{kd}{i}", [128, 512], F32, esP2) for i in range(2)] for kd in range(2)]
    cgv_b = [P.bufs(2, f"cgv{kd}") for kd in range(2)]
    NH2 = 4
    h2t = [P.sb(f"h2t{i}", [128, D], F32, esP2) for i in range(NH2)]
    h2t_b = P.bufs(NH2, "h2t")
    u3T = [P.sb(f"u3T{i}", [128, 8, 128], BF16, esP2) for i in range(2)]
    u3T_b = P.bufs(2, "u3T")
    gate_s = [P.sb("gate_s0", [128, D], F32, esP2)] * 2
    gate_b = [P.buf("gate_s")] * 2
    pt = [P.sb(f"pt{i}", [128, 256], F32, esP2) for i in range(4)]
    pt_b = P.bufs(4, "pt")
    pbf = [P.sb(f"pbf{i}", [128, 256], BF16, esP2) for i in range(2)]
    pbf_b = P.bufs(2, "pbf")
    pT = [P.sb(f"pT{i}", [128, 2, 128], BF16, esP2) for i in range(2)]
    pT_b = P.bufs(2, "pT")
    fss = [P.sb(f"fss{i}", [128, 4], F32, esP2) for i in range(2)]
    fss_b = P.bufs(2, "fss")
    outt = [P.sb(f"outt{i}", [128, D], F32, esP2) for i in range(2)]
    outt_b = P.bufs(2, "outt")
    h2_scr = nc.dram_tensor("h2_scr", [NSEG, SEGLEN, D], F32, kind="Internal").ap()
    u2_scr = nc.dram_tensor("u2_scr", [2 * NSEG, 128, 8, 512], BF16, kind="Internal").ap()
    bu2s = P.bufs(2 * NSEG, "u2s")
    u2h_scr = nc.dram_tensor("u2h_scr", [NSEG, 128, 8, HALO], BF16, kind="Internal").ap()
    bu2h = P.bufs(NSEG, "u2h")
    bh2s = [[P.buf(f"h2s{s}_{t}") for t in range(8)] for s in range(NSEG)]
    bout = P.buf("out")
    brr2 = [0]

    reserved = set()

    def nb2(lo=0, hi=5):
        while True:
            hi_ = hi + (1 if (hi == 5 and cur_half[0] == 0) else 0)
            b = lo + brr2[0] % (hi_ - lo)
            brr2[0] += 1
            if b not in reserved:
                return b

    PBANK = 5
    psP = psb[PBANK].bitcast(BF16)
    cctr = [0]
    fctr = [0]
    groups = [(seg, grp) for seg in range(NSEG) for grp in range(2)]

    cur_half = [0]
    for half in range(2):
        j0 = half * NJ
        cur_half[0] = half

        def halo_stage(seg):
            tail, btail = tails[seg % 2], btails[seg % 2]
            if half == 0:
                P.dma("sp", h1f[0][0:HALO, :], h1_scr[seg, 0:HALO, :], reads=[bh1s[seg]], writes=[h1f_b[0]])
                front2(None, u2h[:, :, :], u2h_b, gffn_bc, nrows=HALO, xres=(h1f[0], h1f_b[0]))
                front2.flush()
                P.dma("sp", u2h_scr[seg], u2h[:], reads=[u2h_b], writes=[bu2h[seg]])
            else:
                P.dma("sp", u2h[:], u2h_scr[seg], reads=[bu2h[seg]], writes=[u2h_b])
            bk = nb2()
            for kd in range(2):
                for jj in range(NJ):
                    cidx = kd * NJ + jj
                    for k in range(8):
                        P.op("pe", lambda E, k=k: E.matmul(psb[bk][:, 2 * cidx:2 * cidx + 2],
                                                           Wup[:, k, cidx * 128:(cidx + 1) * 128],
                                                           u2h[:, k, HALO - 2:HALO], start=(k == 0), stop=(k == 7)),
                             reads=[bWup[piece_of(jj)], u2h_b], writes=[psbuf[bk]])
            P.op("act", lambda E: E.activation(tail[:, :, :],
                                               psb[bk][:, 0:4 * NJ].rearrange("p (c t) -> p c t", t=2),
                                               AF.Identity, scale=hvalid[:, seg:seg + 1]),
                 reads=[psbuf[bk], bconst], writes=btail)

        def f_stage(gidx):
            seg, grp = groups[gidx]
            ub = gidx % 2
            if half == 1:
                P.dma("sp", u2T[ub][:], u2_scr[gidx], reads=[bu2s[gidx]], writes=[u2T_b[ub]])
                return
            for j in range(4):
                r0 = HALO + grp * 512 + j * 128
                fi = fctr[0] % 2
                fctr[0] += 1
                P.dma("sp", h1f[fi][:], h1_scr[seg, r0:r0 + 128, :], reads=[bh1s[seg]], writes=[h1f_b[fi]])
                front2(None, u2T[ub][:, :, j * 128:(j + 1) * 128], u2T_b[ub], gffn_bc, xres=(h1f[fi], h1f_b[fi]),
                       hold=True)

        def f_done(gidx):
            if half == 0:
                ub = gidx % 2
                P.dma("sp", u2_scr[gidx], u2T[ub][:], reads=[u2T_b[ub]], writes=[bu2s[gidx]])

        pre = {}

        def j_proj(gidx, jj):
            ub = gidx % 2
            banks = []
            for kd in range(2):
                cidx = kd * NJ + jj
                bk = nb2()
                proj_fm(lambda k: Wup[:, k, cidx * 128:(cidx + 1) * 128], lambda k: u2T[ub][:, k, :], 512, bk,
                        [bWup[piece_of(jj)], u2T_b[ub]])
                banks.append(bk)
            return banks

        def j_chain(gidx, jj, banks):
            tail, btail = tails[groups[gidx][0] % 2], btails[groups[gidx][0] % 2]
            res = []
            for kd in range(2):
                cidx = kd * NJ + jj
                ch = kd * 22 + j0 + jj
                bk = banks[kd]
                vi = cctr[0] % 2
                vb, vbb = vbuf[kd][vi], vbuf_b[kd][vi]
                cg, cgb = cgv[kd][vi], cgv_b[kd][vi]
                P.op("pool", lambda E: E.tensor_copy(vb[:, 0:2], tail[:, cidx, :]), reads=[btail[cidx]],
                     writes=[vbb])
                P.op("act", lambda E: E.copy(vb[:, 2:514], psb[bk][:, :]), reads=[psbuf[bk]], writes=[vbb])
                P.op("pool", lambda E: E.tensor_copy(tail[:, cidx, :], vb[:, 512:514]), reads=[vbb],
                     writes=[btail[cidx]])
                P.op("act", lambda E: E.activation(cg[:, :], psb[bk][:, :], AF.Identity,
                                                   bias=fcb_t[:, ch:ch + 1], scale=fcw_t[:, ch, 2:3]),
                     reads=[psbuf[bk], bWp2], writes=[cgb])
                P.op("dve", lambda E: E.scalar_tensor_tensor(cg[:, :], vb[:, 1:513], fcw_t[:, ch, 1:2], cg[:, :],
                                                             ALU.mult, ALU.add),
                     reads=[vbb, cgb, bWp2], writes=[cgb])
                P.op("dve", lambda E: E.scalar_tensor_tensor(cg[:, :], vb[:, 0:512], fcw_t[:, ch, 0:1], cg[:, :],
                                                             ALU.mult, ALU.add),
                     reads=[vbb, cgb, bWp2], writes=[cgb])
                res.append((cg, cgb))
            cctr[0] += 1
            (cgg, cggb), (cgvv, cgvb) = res
            P.op("act", lambda E: E.activation(cgg[:, :], cgg[:, :], AF.Gelu_apprx_tanh), reads=[cggb],
                 writes=[cggb])
            P.op("dve", lambda E: E.tensor_tensor(aT[:, jj, :], cgg[:, :], cgvv[:, :], ALU.mult),
                 reads=[cggb, cgvb], writes=[aT_b[jj]])

        def j_stage(gidx):
            for jj in range(NJ):
                if (gidx, jj) in pre:
                    banks = pre.pop((gidx, jj))
                    for b_ in banks:
                        reserved.discard(b_)
                else:
                    banks = j_proj(gidx, jj)
                j_chain(gidx, jj, banks)

        def j_preissue(gidx, jj):
            banks = j_proj(gidx, jj)
            pre[(gidx, jj)] = banks
            reserved.update(banks)

        def down_mm(j, hf, bk):
            for jj in range(NJ):
                P.op("pe", lambda E, jj=jj: E.matmul(psb[bk][:, :], aT[:, jj, j * 128:(j + 1) * 128],
                                                     Wdn[:, jj, hf * 512:(hf + 1) * 512],
                                                     start=(jj == 0), stop=(jj == NJ - 1)),
                     reads=[aT_b[jj], bWdn[piece_of(jj)]], writes=[psbuf[bk]])

        def w_stage_half0(gidx):
            seg, grp = groups[gidx]
            for j in range(4):
                r0 = grp * 512 + j * 128
                P.dma("sp", h2t[j % NH2][:], h1_scr[seg, HALO + r0:HALO + r0 + 128, :], reads=[bh1s[seg]],
                      writes=[h2t_b[j % NH2]])
            for j in range(4):
                r0 = grp * 512 + j * 128
                si = j % NH2
                for hf in range(2):
                    bk = nb2()
                    down_mm(j, hf, bk)
                    hs = slice(hf * 512, (hf + 1) * 512)
                    P.op("dve", lambda E: E.tensor_tensor(h2t[si][:, hs], psb[bk][:, :], h2t[si][:, hs], ALU.add),
                         reads=[psbuf[bk], h2t_b[si]], writes=[h2t_b[si]])
                P.dma("sp", h2_scr[seg, r0:r0 + 128, :], h2t[si][:], reads=[h2t_b[si]],
                      writes=[bh2s[seg][grp * 4 + j]], sembuf=h2t_b[si])

        def w_stage_half1(gidx):
            seg, grp = groups[gidx]

            for j in range(4):
                r0 = grp * 512 + j * 128
                P.dma("sp", h2t[j % NH2][:], h2_scr[seg, r0:r0 + 128, :], reads=[bh2s[seg][grp * 4 + j]],
                      writes=[h2t_b[j % NH2]])
                P.dma("sp", pt[j][:], pown[seg, r0:r0 + 128, :], writes=[pt_b[j]])

            def s0(j):
                r0 = grp * 512 + j * 128
                si = j % NH2
                for hf in range(2):
                    bk = nb2()
                    down_mm(j, hf, bk)
                    hs = slice(hf * 512, (hf + 1) * 512)
                    P.op("dve", lambda E: E.tensor_tensor(h2t[si][:, hs], psb[bk][:, :], h2t[si][:, hs], ALU.add),
                         reads=[psbuf[bk], h2t_b[si]], writes=[h2t_b[si]])

            def s1(j):
                r0 = grp * 512 + j * 128
                si, s2_ = j % NH2, j % 2
                front3(None, u3T[s2_][:, :, :], u3T_b[s2_], gple_bc, xres=(h2t[si], h2t_b[si]), hold=True)
                P.op("pool", lambda E: E.tensor_copy(pbf[s2_][:], pt[j][:]), reads=[pt_b[j]], writes=[pbf_b[s2_]])

            def s2(j):
                s2_ = j % 2
                front3.flush(1)
                for kp in range(2):
                    P.op("pe", lambda E, kp=kp: E.transpose(psP[:, kp * 128:(kp + 1) * 128],
                                                            pbf[s2_][:, kp * 128:(kp + 1) * 128], ident[:]),
                         reads=[pbf_b[s2_], bconst], writes=[psbuf[PBANK]])
                P.op("act", lambda E: E.copy(pT[s2_][:], psP[:, 0:256].rearrange("p (k t) -> p k t", k=2)),
                     reads=[psbuf[PBANK]], writes=[pT_b[s2_]])

            def s3(j):
                si, s2_ = j % NH2, j % 2
                for hf in range(2):
                    hs = slice(hf * 512, (hf + 1) * 512)
                    bkg = nb2()
                    for k in range(8):
                        P.op("pe", lambda E, k=k: E.matmul(psb[bkg][:, :], u3T[s2_][:, k, :], Wpg[:, k, hs],
                                                           start=(k == 0), stop=(k == 7)),
                             reads=[u3T_b[s2_], bWp3], writes=[psbuf[bkg]])
                    P.op("act", lambda E: E.activation(gate_s[s2_][:, hs], psb[bkg][:, :], AF.Sigmoid),
                         reads=[psbuf[bkg]], writes=[gate_b[s2_]])
                    bkp = nb2()
                    for kp in range(2):
                        P.op("pe", lambda E, kp=kp: E.matmul(psb[bkp][:, :], pT[s2_][:, kp, :], Wpl[:, kp, hs],
                                                             start=(kp == 0), stop=(kp == 1)),
                             reads=[pT_b[s2_], bWp3], writes=[psbuf[bkp]])
                    P.op("dve", lambda E: E.tensor_tensor(gate_s[s2_][:, hs], psb[bkp][:, :], gate_s[s2_][:, hs],
                                                          ALU.mult),
                         reads=[psbuf[bkp], gate_b[s2_]], writes=[gate_b[s2_]])
                    P.op("pool", lambda E: E.tensor_tensor(h2t[si][:, hs], h2t[si][:, hs], gate_s[s2_][:, hs], ALU.add),
                         reads=[gate_b[s2_], h2t_b[si]], writes=[h2t_b[si]])

            def s4(j):
                r0 = grp * 512 + j * 128
                si, s2_ = j % NH2, j % 2
                P.op("act", lambda E: E.activation(outt[s2_][:], h2t[si][:], AF.Square, accum_out=fss[s2_][:, 0:1]),
                     reads=[h2t_b[si]], writes=[fss_b[s2_], outt_b[s2_]])
                P.op("act", lambda E: E.activation(fss[s2_][:, 1:2], fss[s2_][:, 0:1], AF.Sqrt, bias=eps_t[:, 0:1],
                                                   scale=1.0 / D), reads=[fss_b[s2_], bconst], writes=[fss_b[s2_]])
                P.op("dve", lambda E: E.reciprocal(fss[s2_][:, 2:3], fss[s2_][:, 1:2]), reads=[fss_b[s2_]],
                     writes=[fss_b[s2_]])
                P.op("dve", lambda E: E.scalar_tensor_tensor(outt[s2_][:], h2t[si][:], fss[s2_][:, 2:3], gfin_bc[:],
                                                             ALU.mult, ALU.mult),
                     reads=[h2t_b[si], fss_b[s2_], bconst], writes=[outt_b[s2_]])
                P.dma("sp", out[seg, r0:r0 + 128, :], outt[s2_][:], reads=[outt_b[s2_]], writes=[bout],
                      sembuf=outt_b[s2_])

            stages = (s0, s1, s2, s3, s4)
            for t in range(4 + len(stages) - 1):
                for si_, fn in enumerate(stages):
                    j = t - si_
                    if 0 <= j < 4:
                        fn(j)

        ng = len(groups)
        f_stage(0)
        front2.flush()
        f_done(0)
        halo_stage(groups[0][0])
        if ng > 1:
            if groups[1][0] != groups[0][0]:
                halo_stage(groups[1][0])
            f_stage(1)
        for gidx in range(ng):
            j_stage(gidx)
            front2.flush()
            if gidx + 1 < ng:
                f_done(gidx + 1)
            if gidx + 2 < ng:
                if groups[gidx + 2][0] != groups[gidx + 1][0]:
                    halo_stage(groups[gidx + 2][0])
                f_stage(gidx + 2)
            if gidx == ng - 1 and half == 0:
                load_wup(1)
            if half == 0:
                w_stage_half0(gidx)
            else:
                w_stage_half1(gidx)
        if half == 0:
            load_wdn(1)
    P.barrier()
    esP2.close()
    esP2w.close()
    esP2c.close()

    return nc, P, dbg_out


def make_in_maps(inputs, cores=range(NCORES)):
    x = np.asarray(inputs["x"], np.float32)
    p = np.asarray(inputs["p"], np.float32)
    f = lambda k: np.ascontiguousarray(np.asarray(inputs[k], np.float32)[0])
    pc = lambda v: np.ascontiguousarray(v.reshape(-1, 128).T)
    shared = {
        "w_in": f("w_in"), "g_mix": f("g_mix"),
        "pos_kT": np.ascontiguousarray(f("cmp_pos_k").T), "pos_vT": np.ascontiguousarray(f("cmp_pos_v").T),
        "w_k1": f("w_cmp_k1"), "w_k2": f("w_cmp_k2"), "w_v1": f("w_cmp_v1"), "w_v2": f("w_cmp_v2"),
        "w_o": f("w_o_nsa"),
        "conv_wT": np.ascontiguousarray(f("conv_w").T.reshape(4, 128, 31).transpose(1, 0, 2)),
        "conv_b": pc(f("conv_b")), "ln_g": pc(f("conv_ln_g")), "ln_b": pc(f("conv_ln_b")),
        "w_co": f("w_conv_out"), "b_co": pc(f("b_conv_out")),
        "w_out": f("w_out"), "g_ffn": f("g_ffn"), "w_up": f("w_up"),
        "fcw": np.ascontiguousarray(f("ffn_conv_w").T.reshape(44, 128, 3).transpose(1, 0, 2)),
        "fcb": pc(f("ffn_conv_b")), "w_down": f("w_down"),
        "g_ple": f("g_ple"), "w_pg": f("w_ple_gate"), "w_ple": f("w_ple"),
        "g_fin": np.ascontiguousarray(np.asarray(inputs["g_final"], np.float32)),
    }
    w = shared["w_in"].copy()
    wq = w[:, 0:512].reshape(D, 2, 4, 64).transpose(0, 2, 1, 3).reshape(D, 512)
    w[:, 0:512] = wq
    shared["w_in"] = w
    consts = {r: host_consts(r) for r in (0, 1)}
    maps = []
    for c in cores:
        b, r = c // 2, c % 2
        m = dict(shared)
        m.update(consts[r])
        m["xc"] = np.ascontiguousarray(x[b])
        xe = np.zeros((NSEG, NEXT * 128, D), np.float32)
        po = np.zeros((NSEG, SEGLEN, 256), np.float32)
        for s, s0 in enumerate(SEG_STARTS[r]):
            lo = s0 - NCTX * 128
            src_lo = max(lo, 0)
            xe[s, src_lo - lo:] = x[b, src_lo:s0 + SEGLEN]
            po[s] = p[0, b, s0:s0 + SEGLEN]
        m["xe"] = xe
        m["pown"] = po
        maps.append(m)
    return maps


_NC_CACHE = {}


def kernel(**inputs):
    if "nc" not in _NC_CACHE:
        _NC_CACHE["nc"] = build()[0]
    nc = _NC_CACHE["nc"]
    maps = make_in_maps(inputs)
    res = run_bass_kernel_spmd(nc, maps, core_ids=list(range(NCORES)))
    outp = np.zeros((NB, S, D), np.float32)
    for c in range(NCORES):
        b, r = c // 2, c % 2
        o = res.results[c]["out"]
        for s, s0 in enumerate(SEG_STARTS[r]):
            outp[b, s0:s0 + SEGLEN] = o[s]
    return outp
```

```python
import contextlib
import numpy as np
import ml_dtypes
import concourse.bass as bass
import concourse.mybir as mybir
from concourse.bass_utils import run_bass_kernel_spmd

F32 = mybir.dt.float32
BF16 = mybir.dt.bfloat16
AF = mybir.ActivationFunctionType
ALU = mybir.AluOpType
AX = mybir.AxisListType

D = 1024
S = 4096
NB = 4
NCORES = 8
SEGLEN = 1024
NSEG = 2
NCTX = 5
NEXT = NCTX + SEGLEN // 128
HALO = 32
OWNW = HALO + SEGLEN
OWN = NSEG * OWNW
SEG_STARTS = {0: [0, 3072], 1: [1024, 2048]}
BIG = 30000.0
EPS = 1e-6
DFF = 2816
NIN = 4376
SAME_SYNC = True
NOSAME = ()
CQ, CKV, CG, CCV, CM = 0, 512, 1280, 1304, 2328


class Buf:
    __slots__ = ("name", "last_w", "readers", "dma_sem")

    def __init__(self, name):
        self.name = name
        self.last_w = None
        self.readers = []
        self.dma_sem = None


class Prog:
    def __init__(self, nc, same_engine_sync=True):
        self.nc = nc
        self.es = contextlib.ExitStack()
        self.cnt = {}
        self.waited = {e: {} for e in ("pe", "act", "dve", "pool", "sp")}
        self.sem = {}
        self.semobj = {}
        self.semcnt = {}
        for e in ("pe", "act", "dve", "pool"):
            s = self.es.enter_context(nc.semaphore("s_" + e))
            self.sem[e] = s
            self.semobj[("eng", e)] = s
            self.semcnt[("eng", e)] = 0
        self.same = same_engine_sync
        self.E = {"pe": nc.tensor, "act": nc.scalar, "dve": nc.vector, "pool": nc.gpsimd,
                  "sp": nc.sync}
        self.ndma = 0
        self.nbuf = 0
        self.ninstr = {e: 0 for e in self.E}

    def sb(self, name, shape, dt, es=None, side=None):
        return (es or self.es).enter_context(self.nc.sbuf_tensor("sb_" + name, shape, dt, side=side))

    def ps(self, name, shape, dt, es=None):
        return (es or self.es).enter_context(self.nc.psum_tensor("pp_" + name, shape, dt))

    def buf(self, name=None):
        self.nbuf += 1
        return Buf(name or f"b{self.nbuf}")

    def bufs(self, n, name="b"):
        return [self.buf(f"{name}{i}") for i in range(n)]

    def new_dma_sem(self):
        self.ndma += 1
        s = self.es.enter_context(self.nc.semaphore(f"sd{self.ndma}"))
        key = ("dma", self.ndma)
        self.semobj[key] = s
        self.semcnt[key] = 0
        return key

    def _deps(self, reads, writes):
        deps = []
        for b in reads:
            if b.last_w is not None:
                deps.append(b.last_w)
        for b in writes:
            if b.last_w is not None:
                deps.append(b.last_w)
            deps.extend(b.readers)
        return deps

    def _emit_waits(self, eng, deps):
        w = self.waited[eng]
        need = {}
        for key, val in deps:
            if key == ("eng", eng) and (eng == "pe" or eng in NOSAME):
                continue
            if w.get(key, 0) < val:
                need[key] = max(need.get(key, 0), val)
        for key, val in need.items():
            w[key] = val
            self.E[eng].wait_ge(self.semobj[key], val)

    def _stamp(self, stamp, reads, writes):
        for b in reads:
            b.readers.append(stamp)
        for b in writes:
            b.last_w = stamp
            b.readers = []

    def op(self, eng, fn, reads=(), writes=()):
        self._emit_waits(eng, self._deps(reads, writes))
        key = ("eng", eng)
        self.semcnt[key] += 1
        stamp = (key, self.semcnt[key])
        fn(self.E[eng]).then_inc(self.sem[eng], 1)
        self.ninstr[eng] += 1
        self._stamp(stamp, reads, writes)

    def dma(self, queue, out, in_, reads=(), writes=(), sembuf=None, **kw):
        b0 = sembuf or (writes[0] if writes else reads[0])
        if b0.dma_sem is None:
            b0.dma_sem = self.new_dma_sem()
        key = b0.dma_sem
        deps = [d for d in self._deps(reads, writes)
                if not (d[0] == key and any(b.last_w == d for b in writes))]
        self._emit_waits(queue, deps)
        self.semcnt[key] += 16
        stamp = (key, self.semcnt[key])
        self.E[queue].dma_start(out, in_, **kw).then_inc(self.semobj[key], 16)
        self.ninstr[queue] += 1
        self._stamp(stamp, reads, writes)

    def barrier(self):
        for e in ("pe", "act", "dve", "pool", "sp"):
            deps = [(k, v) for k, v in self.semcnt.items() if v > 0 and k != ("eng", e)]
            self._emit_waits(e, deps)

    def wait_bufs(self, eng, bufs):
        deps = []
        for b in bufs:
            if b.last_w is not None:
                deps.append(b.last_w)
            deps.extend(b.readers)
        self._emit_waits(eng, deps)


def _bf(a):
    return np.ascontiguousarray(a).astype(ml_dtypes.bfloat16)


def own_token_positions(role):
    pos = np.zeros(OWN, np.int64)
    for s, s0 in enumerate(SEG_STARTS[role]):
        pos[s * OWNW:(s + 1) * OWNW] = np.arange(s0 - HALO, s0 + SEGLEN)
    return pos


def host_consts(role):
    c = {}
    starts = SEG_STARTS[role]
    c["ident"] = _bf(np.eye(128, dtype=np.float32))
    c["identf"] = np.eye(128, dtype=np.float32)
    E = np.zeros((64, S), np.float32)
    E[np.arange(S) // 64, np.arange(S)] = 1.0
    c["E"] = _bf(E)
    k = np.arange(128)[:, None]
    q = np.arange(128)[None, :]
    c["mcausal"] = _bf(np.where(k <= q, 0.0, -BIG))
    c["mfar"] = _bf(np.where(k > q, 0.0, -BIG))
    n_cmp = 255
    c0 = np.arange(n_cmp) * 16
    j0 = np.arange(64) * 64
    lo = np.maximum(c0[:, None], j0[None, :])
    hi = np.minimum(c0[:, None] + 32, j0[None, :] + 64)
    sm = np.zeros((256, 64), np.float32)
    sm[:255] = np.maximum(hi - lo, 0) / 32.0
    c["selmap"] = _bf(sm.reshape(2, 128, 64).transpose(1, 0, 2))
    pos = own_token_positions(role)
    cidx = np.arange(256)
    cend = cidx * 16 + 31
    allowed = (cend[:, None] <= pos[None, :]) & (cidx[:, None] < 255)
    cm = np.where(allowed, 0.0, -BIG).astype(np.float32)
    c["cmask"] = _bf(cm.reshape(2, 128, OWN).transpose(1, 0, 2))
    j = np.arange(64)[None, :]
    cur = (pos // 64)[:, None]
    real = (pos >= 0)[:, None]
    valid = (j <= cur) & real
    forced = ((j == 0) | (j == cur) | (j == cur - 1)) & valid
    mul = (valid & ~forced).astype(np.float32)
    add = np.where(forced, 1e4, np.where(valid, 0.0, -1.0)).astype(np.float32)
    add = np.where(real, add, 0.0)
    dt = np.zeros(OWN, np.int64)
    for s, s0 in enumerate(starts):
        dt[s * OWNW: s * OWNW + HALO] = s0 // 128 - 1
        dt[s * OWNW + HALO:(s + 1) * OWNW] = (s0 + np.arange(SEGLEN)) // 128
    pre2 = np.where(j >= 2 * dt[:, None], -2 * BIG, -BIG).astype(np.float32)
    c["selc"] = np.ascontiguousarray(np.stack([mul, add, pre2], axis=1))
    vf = np.zeros((128, NSEG, NEXT), np.float32)
    for s, s0 in enumerate(starts):
        tpos = s0 - NCTX * 128 + np.arange(NEXT * 128)
        vf[:, s, :] = (tpos >= 0).astype(np.float32).reshape(NEXT, 128).T
    c["vflag"] = vf
    hv = np.zeros((128, NSEG), np.float32)
    for s, s0 in enumerate(starts):
        hv[:, s] = 1.0 if s0 > 0 else 0.0
    c["hvalid"] = hv
    return c


def build(debug=None, stop_after=None):
    debug = debug or set()
    nc = bass.Bass("TRN2", target_bir_lowering=False)
    P = Prog(nc, same_engine_sync=SAME_SYNC)
    dbg_out = {}
    ES = contextlib.ExitStack

    def din(name, shape, dt=F32):
        return nc.dram_tensor(name, list(shape), dt, kind="ExternalInput").ap()

    xc = din("xc", [S, D])
    xe = din("xe", [NSEG, NEXT * 128, D])
    pown = din("pown", [NSEG, SEGLEN, 256])
    w_in = din("w_in", [D, NIN])
    g_mix = din("g_mix", [D])
    pos_kT = din("pos_kT", [64, 32])
    pos_vT = din("pos_vT", [64, 32])
    w_k1 = din("w_k1", [2048, 256])
    w_k2 = din("w_k2", [256, 64])
    w_v1 = din("w_v1", [2048, 256])
    w_v2 = din("w_v2", [256, 64])
    w_o = din("w_o", [512, D])
    conv_wT = din("conv_wT", [128, 4, 31])
    conv_b = din("conv_b", [128, 4])
    ln_g = din("ln_g", [128, 4])
    ln_b = din("ln_b", [128, 4])
    w_co = din("w_co", [512, D])
    b_co = din("b_co", [128, 8])
    w_out = din("w_out", [D, D])
    g_ffn = din("g_ffn", [D])
    w_up = din("w_up", [D, 2 * DFF])
    fcw = din("fcw", [128, 44, 3])
    fcb = din("fcb", [128, 44])
    w_down = din("w_down", [DFF, D])
    g_ple = din("g_ple", [D])
    w_pg = din("w_pg", [D, D])
    w_ple = din("w_ple", [256, D])
    g_fin = din("g_fin", [D])
    c_ident = din("ident", [128, 128], BF16)
    c_identf = din("identf", [128, 128], F32)
    c_E = din("E", [64, S], BF16)
    c_mcausal = din("mcausal", [128, 128], BF16)
    c_mfar = din("mfar", [128, 128], BF16)
    c_selmap = din("selmap", [128, 2, 64], BF16)
    c_cmask = din("cmask", [128, 2, OWN], BF16)
    c_selc = din("selc", [OWN, 3, 64], F32)
    c_vflag = din("vflag", [128, NSEG, NEXT], F32)
    c_hvalid = din("hvalid", [128, NSEG], F32)
    out = nc.dram_tensor("out", [NSEG, SEGLEN, D], F32, kind="ExternalOutput").ap()
    h1_scr = nc.dram_tensor("h1_scr", [NSEG, OWNW, D], F32, kind="Internal").ap()
    uT_scr = nc.dram_tensor("uT_scr", [128, 8, OWN], BF16, kind="Internal").ap()
    cT_scr = nc.dram_tensor("cT_scr", [128, 4, OWN], BF16, kind="Internal").ap()
    bscr_u = [P.buf(f"uTscr{s}") for s in range(NSEG)]
    bscr_c = [P.buf(f"cTscr{s}") for s in range(NSEG)]

    def dump(name, shape, src_ap, reads, dt=F32):
        if name not in debug:
            return
        t = nc.dram_tensor("dbg_" + name, list(shape), dt, kind="ExternalOutput").ap()
        b = P.buf("dbg_" + name)
        P.dma("sp", t, src_ap, reads=reads, writes=[b])
        dbg_out[name] = b

    psb = [P.ps(f"ps{i}", [128, 512], F32) for i in range(8)]
    psbuf = [P.buf(f"ps{i}") for i in range(8)]

    ident = P.sb("ident", [128, 128], BF16)
    identf = P.sb("identf", [128, 128], F32)
    eps_t = P.sb("eps_t", [128, 1], F32)
    vflag = P.sb("vflag", [128, NSEG, NEXT], F32)
    hvalid = P.sb("hvalid", [128, NSEG], F32)
    bconst = P.buf("consts")
    P.dma("sp", ident[:], c_ident, writes=[bconst])
    P.dma("sp", identf[:], c_identf, writes=[bconst])
    P.dma("sp", vflag[:], c_vflag, writes=[bconst])
    P.dma("sp", hvalid[:], c_hvalid, writes=[bconst])
    P.op("dve", lambda E: E.memset(eps_t[:], EPS), writes=[bconst])

    oT_scr = nc.dram_tensor("oT_scr", [128, 4, OWN], BF16, kind="Internal").ap()
    boT = P.buf("oTscr")
    esL2 = ES()
    KsAug = [P.sb(f"KsAug{g}", [128, S], BF16, esL2) for g in range(2)]
    VsAug = P.sb("VsAug", [128, 32, 2, 65], BF16, esL2)
    KcAug = [P.sb(f"KcAug{g}", [128, 256], BF16, esL2) for g in range(2)]
    VcAug = P.sb("VcAug", [128, 2, 2, 65], BF16, esL2)
    KwAug = [P.sb(f"KwAug{g}", [128, NSEG, NEXT * 128], BF16, esL2) for g in range(2)]
    KxAug = [P.sb(f"KxAug{g}", [128, NSEG, NEXT * 128], BF16, esL2) for g in range(2)]
    VwAug = P.sb("VwAug", [128, NSEG, NEXT, 2, 65], BF16, esL2)
    VxAug = P.sb("VxAug", [128, NSEG, NEXT, 2, 65], BF16, esL2)
    QT = P.sb("QT", [128, 4, OWN], BF16, esL2)
    NOT_ = NSEG * 9
    gates = P.sb("gates", [128, NOT_, 24], F32, esL2)
    for g in range(2):
        P.op("pool", lambda E, g=g: E.memset(KcAug[g][:], 0.0), writes=[bconst])
        P.op("pool", lambda E, g=g: E.memset(KwAug[g][:], 0.0), writes=[bconst])
        P.op("dve", lambda E, g=g: E.memset(KxAug[g][:], 0.0), writes=[bconst])
    P.op("dve", lambda E: E.memset(VsAug[:], 1.0), writes=[bconst])
    P.op("dve", lambda E: E.memset(VcAug[:], 1.0), writes=[bconst])
    P.op("pool", lambda E: E.memset(VxAug[:], 1.0), writes=[bconst])
    P.op("pool", lambda E: E.memset(VwAug[:], 1.0), writes=[bconst])
    P.dma("sp", KsAug[0][64:128, :], c_E, writes=[bconst])
    P.dma("sp", KsAug[1][0:64, :], c_E, writes=[bconst])
    P.barrier()
    for g in range(2):
        P.op("dve", lambda E, g=g: E.tensor_copy(VwAug[:, :, :, g, 64], vflag[:]), reads=[bconst],
             writes=[bconst])
        P.op("dve", lambda E, g=g: E.tensor_copy(VxAug[:, :, :, g, 64], vflag[:]), reads=[bconst],
             writes=[bconst])

    esCP = ES()
    cpads = [P.sb(f"cpad{s}", [128, 4, 32 + OWNW], BF16, esCP) for s in range(NSEG)]
    bcpad = P.bufs(NSEG, "cpad")
    esL3 = ES()
    gmix_bc = P.sb("gmix_bc", [128, D], F32, esL3)
    P.dma("sp", gmix_bc[:], g_mix.partition_broadcast(128), writes=[bconst])
    Wkv = P.sb("Wkv", [128, 8, 768], BF16, esL3)
    bW1 = P.buf("Wkv")
    w_in_v = w_in.rearrange("(k p) n -> p k n", p=128)
    P.dma("pool", Wkv[:], w_in_v[:, :, CKV:CKV + 768], writes=[bW1])
    PS_T = 7
    psT = psb[PS_T].bitcast(BF16)

    psTs = {b: psb[b].bitcast(BF16) for b in (6, 7)}

    def make_front(es_, tag, with_xs=True, banks=(6, 7), NXS=2, NUTM=2, depth=1):
        pend = []
        xs = [P.sb(f"xs{tag}{i}", [128, D], F32, es_) for i in range(NXS)] if with_xs else None
        xs_b = P.bufs(NXS, "xs")
        sq_junk = P.sb("sq_junk" + tag, [128, D], BF16, es_)
        sq_b = P.buf("sqj")
        ss = [P.sb(f"ss{tag}{i}", [128, 4], F32, es_) for i in range(NXS)]
        ss_b = P.bufs(NXS, "ss")
        utm = [P.sb(f"utm{tag}{i}", [128, D], BF16, es_) for i in range(NUTM)]
        utm_b = P.bufs(NUTM, "utm")
        front_ctr = [0]

        def front_tile(x_rows_ap, dst_ap, dst_buf, g_bc, nrows=128, xres=None, hold=False):
            i = front_ctr[0]
            front_ctr[0] += 1
            s3 = i % NXS
            s2 = i % NUTM
            if xres is None:
                P.dma("sp", xs[s3][0:nrows, :], x_rows_ap, writes=[xs_b[s3]])
                xin, xb = xs[s3], xs_b[s3]
            else:
                xin, xb = xres
            P.op("act", lambda E: E.activation(sq_junk[0:nrows, :], xin[0:nrows, :], AF.Square,
                                               accum_out=ss[s3][0:nrows, 0:1]),
                 reads=[xb], writes=[sq_b, ss_b[s3]])
            P.op("act", lambda E: E.activation(ss[s3][0:nrows, 1:2], ss[s3][0:nrows, 0:1], AF.Sqrt,
                                               bias=eps_t[0:nrows, 0:1], scale=1.0 / D),
                 reads=[ss_b[s3], bconst], writes=[ss_b[s3]])
            P.op("dve", lambda E: E.reciprocal(ss[s3][0:nrows, 2:3], ss[s3][0:nrows, 1:2]), reads=[ss_b[s3]],
                 writes=[ss_b[s3]])
            P.op("dve", lambda E: E.scalar_tensor_tensor(utm[s2][0:nrows, :], xin[0:nrows, :], ss[s3][0:nrows, 2:3],
                                                         g_bc[0:nrows, :], ALU.mult, ALU.mult),
                 reads=[xb, ss_b[s3], bconst], writes=[utm_b[s2]])
            bki = banks[i % len(banks)]
            psT_ = psTs[bki]

            def stage_b():
                for k in range(8):
                    P.op("pe", lambda E, k=k: E.transpose(psT_[:, k * 128:k * 128 + nrows],
                                                          utm[s2][0:nrows, k * 128:(k + 1) * 128],
                                                          ident[0:nrows, 0:nrows]),
                         reads=[utm_b[s2], bconst], writes=[psbuf[bki]])
                P.op("act", lambda E: E.copy(dst_ap, psT_[:, :].rearrange("p (k t) -> p k t", k=8)[:, :, 0:nrows]),
                     reads=[psbuf[bki]], writes=[dst_buf])

            if not hold:
                while len(pend) >= depth:
                    pend.pop(0)()
            pend.append(stage_b)
            return ss[s3], ss_b[s3]

        def flush(n=None):
            k = 0
            while pend and (n is None or k < n):
                pend.pop(0)()
                k += 1

        front_tile.flush = flush
        return front_tile

    front_tile = make_front(esL3, "a", NXS=4, NUTM=4, depth=3)
    def proj_fm(lhs_fn, rhs_fn, n, bank, rbufs, nk=8):
        for k in range(nk):
            P.op("pe", lambda E, k=k: E.matmul(psb[bank][:, 0:n], lhs_fn(k), rhs_fn(k),
                                               start=(k == 0), stop=(k == nk - 1)),
                 reads=rbufs, writes=[psbuf[bank]])

    esL4 = ES()
    kcmpT = P.sb("kcmpT", [128, S], BF16, esL4)
    vcmpT = P.sb("vcmpT", [128, S], BF16, esL4)
    w1s = P.sb("w1s", [128, 32, 256], BF16, esL4)
    bw1s = P.buf("w1s")

    def load_w1(wsrc):
        v = wsrc.rearrange("(l d) h -> d l h", d=64)
        P.dma("pool", w1s[0:64], v, writes=[bw1s])
        P.dma("pool", w1s[64:128], v, writes=[bw1s])

    load_w1(w_k1)
    esL4b = ES()
    uTg = [P.sb(f"uTg{i}", [128, 8, 512], BF16, esL4b) for i in range(2)]
    uTg_b = P.bufs(2, "uTg")
    bctx = P.bufs(8, "ctx")
    def ph1_front(grp):
        ub = grp % 2
        for j in range(4):
            t = grp * 4 + j
            front_tile(xc[t * 128:(t + 1) * 128, :], uTg[ub][:, :, j * 128:(j + 1) * 128], uTg_b[ub], gmix_bc)

    def ph1_proj(grp):
        ub = grp % 2
        for ch, bank in ((0, 0), (1, 1), (2, 2)):
            proj_fm(lambda k, ch=ch: Wkv[:, k, ch * 128:(ch + 1) * 128], lambda k: uTg[ub][:, k, :],
                    512, bank, [bW1, uTg_b[ub]])
        for j in range(4):
            for k in range(8):
                P.op("pe", lambda E, j=j, k=k: E.matmul(psb[3][:, j * 128:(j + 1) * 128],
                                                        uTg[ub][:, k, j * 128:(j + 1) * 128],
                                                        Wkv[:, k, 384:512], start=(k == 0),
                                                        stop=(k == 7)),
                     reads=[bW1, uTg_b[ub]], writes=[psbuf[3]])

    def ph1_evac(grp):
        cs = slice(grp * 512, (grp + 1) * 512)
        kd_ = kcmpT[:, :].rearrange("p (s c) -> p s c", s=16)[:, :, grp * 32:(grp + 1) * 32]
        vd_ = vcmpT[:, :].rearrange("p (s c) -> p s c", s=16)[:, :, grp * 32:(grp + 1) * 32]
        P.op("act", lambda E: E.copy(kd_, psb[0][:, :].rearrange("p (c s) -> p s c", s=16)), reads=[psbuf[0]],
             writes=[bctx[grp]])
        P.op("dve", lambda E: E.tensor_copy(vd_, psb[1][:, :].rearrange("p (c s) -> p s c", s=16)),
             reads=[psbuf[1]], writes=[bctx[grp]])
        P.op("dve", lambda E: E.tensor_copy(KsAug[0][0:64, cs], psb[2][0:64, :]), reads=[psbuf[2]],
             writes=[bctx[grp]])
        P.op("dve", lambda E: E.tensor_copy(KsAug[1][64:128, cs], psb[2][64:128, :]), reads=[psbuf[2]],
             writes=[bctx[grp]])
        P.op("dve", lambda E: E.tensor_copy(
            VsAug[:, grp * 4:(grp + 1) * 4, :, 0:64],
            psb[3][:, :].rearrange("p (j g d) -> p j g d", j=4, g=2)), reads=[psbuf[3]],
            writes=[bctx[grp]])

    ph1_front(0)
    front_tile.flush()
    for grp in range(8):
        ph1_proj(grp)
        if grp + 1 < 8:
            ph1_front(grp + 1)
        ph1_evac(grp)
        front_tile.flush()
    dump("KsAug0", [128, S], KsAug[0][:], bctx, BF16)
    dump("VsAug", [128, 32, 2, 65], VsAug[:], bctx, BF16)
    P.barrier()
    esL4b.close()
    if stop_after == "ph1":
        return nc, P, dbg_out

    es2 = ES()
    bw2 = P.buf("w_cmp")
    w1 = {"k": w1s, "v": w1s}
    w2 = P.sb("w2", [128, 2, 2, 128], BF16, es2)
    for ki, wsrc in enumerate((w_k2, w_v2)):
        v = wsrc.rearrange("(c p) d -> p c d", p=128)
        P.dma("pool", w2[:, :, ki, 0:64], v, writes=[bw2])
        P.dma("pool", w2[:, :, ki, 64:128], v, writes=[bw2])
    posT = P.sb("posT", [64, 2, 32], BF16, es2)
    P.dma("pool", posT[:, 0, :], pos_kT, writes=[bw2])
    P.dma("pool", posT[:, 1, :], pos_vT, writes=[bw2])
    hb = P.sb("hb", [128, 4], F32, es2)
    bhb = P.buf("hb")
    srcs = {"k": kcmpT, "v": vcmpT}
    hg = {}
    bhg = P.buf("hg")
    for kind in ("k", "v"):
        for g in range(2):
            for hc in range(2):
                hg[(kind, g, hc)] = P.sb(f"hg{kind}{g}{hc}", [128, 256], BF16, es2)
                P.op("pool", lambda E, t=hg[(kind, g, hc)]: E.memset(t[:], 0.0), writes=[bhg])
    cnt = 0
    if stop_after == "ph2a":
        P.barrier()
        return nc, P, dbg_out
    for ki, (kind, wsrc) in enumerate((("k", w_k1), ("v", w_v1))):
        if stop_after == "ph2b" and ki == 1:
            P.barrier()
            return nc, P, dbg_out
        if ki == 1:
            load_w1(wsrc)
        for hc in range(2):
            bank = (ki * 2 + hc) % 4
            for l in range(32):
                P.op("pe", lambda E, l=l, kind=kind, hc=hc, bank=bank, ki=ki: E.matmul(
                    psb[bank][:, 0:1], w1[kind][0:64, l, hc * 128:(hc + 1) * 128], posT[0:64, ki, l:l + 1],
                    start=(l == 0), stop=(l == 31)), reads=[bw2, bw1s], writes=[psbuf[bank]])
            P.op("act", lambda E, bank=bank, ki=ki, hc=hc: E.copy(hb[:, ki * 2 + hc:ki * 2 + hc + 1],
                                                                   psb[bank][:, 0:1]),
                 reads=[psbuf[bank]], writes=[bhb])
        src16 = srcs[kind][:, :].rearrange("p (s c) -> p s c", s=16)
        for g in range(2):
            gh = slice(g * 64, (g + 1) * 64)
            for hc in range(2):
                bank = cnt % 4
                cnt += 1
                for l in range(32):
                    P.op("pe", lambda E, l=l, kind=kind, hc=hc, bank=bank, gh=gh: E.matmul(
                        psb[bank][:, 0:255], w1[kind][gh, l, hc * 128:(hc + 1) * 128],
                        src16[gh, l % 16, l // 16:l // 16 + 255], start=(l == 0), stop=(l == 31)),
                        reads=[bw2, bw1s], writes=[psbuf[bank]])
                P.op("act", lambda E, bank=bank, kind=kind, g=g, hc=hc, ki=ki: E.activation(
                    hg[(kind, g, hc)][:, 0:255], psb[bank][:, 0:255], AF.Gelu_apprx_tanh,
                    bias=hb[:, ki * 2 + hc:ki * 2 + hc + 1]), reads=[psbuf[bank], bhb, bhg], writes=[bhg])
    bkc = P.buf("kc")
    if stop_after == "ph2c":
        P.barrier()
        return nc, P, dbg_out
    for g in range(2):
        gh = slice(g * 64, (g + 1) * 64)
        bank = 4 + g
        for hc in range(2):
            P.op("pe", lambda E, hc=hc, g=g, bank=bank: E.matmul(
                psb[bank][:, 0:256], w2[:, hc, 0, :], hg[("k", g, hc)][:, :], start=(hc == 0), stop=(hc == 1)),
                reads=[bw2, bhg], writes=[psbuf[bank]])
        P.op("act", lambda E, g=g, gh=gh, bank=bank: E.copy(KcAug[g][gh, :], psb[bank][gh, 0:256]),
             reads=[psbuf[bank]], writes=[bkc])
        for ct in range(2):
            bank2 = 6 + ct
            for hc in range(2):
                P.op("pe", lambda E, hc=hc, g=g, ct=ct, bank2=bank2: E.matmul(
                    psb[bank2][:, 0:64], hg[("v", g, hc)][:, ct * 128:(ct + 1) * 128], w2[:, hc, 1, 0:64],
                    start=(hc == 0), stop=(hc == 1)), reads=[bw2, bhg], writes=[psbuf[bank2]])
            P.op("dve", lambda E, g=g, ct=ct, bank2=bank2: E.tensor_copy(VcAug[:, ct, g, 0:64],
                                                                        psb[bank2][:, 0:64]),
                 reads=[psbuf[bank2]], writes=[bkc])
    dump("KcAug0", [128, 256], KcAug[0][:], [bkc], BF16)
    dump("KcAug1", [128, 256], KcAug[1][:], [bkc], BF16)
    dump("VcAug", [128, 2, 2, 65], VcAug[:], [bkc], BF16)
    P.barrier()
    es2.close()
    esL4.close()
    if stop_after == "ph2":
        return nc, P, dbg_out
    es3 = ES()
    uTx = [P.sb("uTx0", [128, 8, 512], BF16, es3), P.sb("uTx1", [128, 8, 128], BF16, es3)]
    uTx_b = P.bufs(2, "uTx")
    uT_seg = P.sb("uT_seg", [128, 8, OWNW], BF16, es3)
    bown = P.buf("own")
    Wq = P.sb("Wq", [128, 8, 512], BF16, es3)
    Wc = P.sb("Wc", [128, 8, 1024], BF16, es3)
    Wg = P.sb("Wg", [128, 8, 24], BF16, es3)
    bW3 = P.buf("W3")
    P.dma("pool", Wq[:], w_in_v[:, :, CQ:CQ + 512], writes=[bW3])
    P.dma("pool", Wc[:], w_in_v[:, :, CCV:CCV + 1024], writes=[bW3])
    P.dma("pool", Wg[:], w_in_v[:, :, CG:CG + 24], writes=[bW3])
    sig = [P.sb(f"sig{i}", [128, 512], F32, es3) for i in range(2)]
    sig_b = P.bufs(2, "sig")
    bext = P.buf("ext")
    bq = P.buf("QT")
    bgates = P.buf("gates")
    sigc = [0]
    bankrr = [0]

    def nextbank(lo=0, hi=6):
        b = lo + bankrr[0] % (hi - lo)
        bankrr[0] += 1
        return b

    for seg in range(NSEG):
        base = seg * OWNW
        egroups = ((0, 4), (4, 1), (5, 4), (9, 4))

        def ext_dst(ft, ntl):
            n = ntl * 128
            if ft >= NCTX:
                c0 = HALO + (ft - NCTX) * 128
                return uT_seg[:, :, c0:c0 + n], bown
            ub = (0 if ft == 0 else 1)
            return uTx[ub][:, :, 0:n], uTx_b[ub]

        def ext_front(ft, ntl):
            dstT, dstb = ext_dst(ft, ntl)
            for j in range(ntl):
                front_tile(xe[seg, (ft + j) * 128:(ft + j + 1) * 128, :], dstT[:, :, j * 128:(j + 1) * 128], dstb,
                           gmix_bc)

        def ext_proj(ft, ntl):
            n = ntl * 128
            dstT, dstb = ext_dst(ft, ntl)
            ecs = slice(ft * 128, ft * 128 + n)
            evacs = []
            if ft == 4:
                evacs.append(lambda: P.op("pool", lambda E: E.tensor_copy(uT_seg[:, :, 0:HALO], uTx[1][:, :, 96:128]),
                                          reads=[uTx_b[1]], writes=[bown]))
            for ch, dst in ((2, KxAug), (4, KwAug)):
                bank = nextbank()
                proj_fm(lambda k, ch=ch: Wkv[:, k, ch * 128:(ch + 1) * 128], lambda k: dstT[:, k, :], n, bank,
                        [bW1, dstb])

                def ev(dst=dst, bank=bank):
                    P.op("act", lambda E: E.copy(dst[0][0:64, seg, ecs], psb[bank][0:64, 0:n]),
                         reads=[psbuf[bank]], writes=[bext])
                    P.op("dve", lambda E: E.tensor_copy(dst[1][64:128, seg, ecs], psb[bank][64:128, 0:n]),
                         reads=[psbuf[bank]], writes=[bext])
                evacs.append(ev)
            for ch, dst in ((3, VxAug), (5, VwAug)):
                bank = nextbank()
                for j in range(ntl):
                    for k in range(8):
                        P.op("pe", lambda E, j=j, k=k, ch=ch, bank=bank: E.matmul(
                            psb[bank][:, j * 128:(j + 1) * 128], dstT[:, k, j * 128:(j + 1) * 128],
                            Wkv[:, k, ch * 128:(ch + 1) * 128], start=(k == 0), stop=(k == 7)),
                            reads=[bW1, dstb], writes=[psbuf[bank]])

                def ev2(dst=dst, bank=bank):
                    P.op("dve", lambda E: E.tensor_copy(
                        dst[:, seg, ft:ft + ntl, :, 0:64],
                        psb[bank][:, 0:n].rearrange("p (j g d) -> p j g d", j=ntl, g=2)),
                        reads=[psbuf[bank]], writes=[bext])
                evacs.append(ev2)
            return evacs

        ext_front(*egroups[0])
        front_tile.flush()
        for gi_, (ft, ntl) in enumerate(egroups):
            evs = ext_proj(ft, ntl)
            if gi_ + 1 < len(egroups):
                ext_front(*egroups[gi_ + 1])
            for ev_ in evs:
                ev_()
            front_tile.flush()
        cpad = cpads[seg]
        P.op("pool", lambda E: E.memset(cpad[:, :, 0:32], 0.0), writes=[bcpad[seg]])
        for (o0, n) in ((0, 352), (352, 352), (704, 352)):
            cs = slice(o0, o0 + n)
            gcs = slice(base + o0, base + o0 + n)
            for hh in range(4):
                bank = nextbank()
                proj_fm(lambda k, hh=hh: Wq[:, k, hh * 128:(hh + 1) * 128], lambda k: uT_seg[:, k, cs], n, bank,
                        [bW3, bown])
                P.op("act", lambda E, bank=bank, hh=hh: E.activation(QT[:, hh, gcs], psb[bank][:, 0:n], AF.Copy,
                                                                      scale=0.125),
                     reads=[psbuf[bank]], writes=[bq])
            for i in range(4):
                bv = nextbank()
                bg_ = nextbank()
                proj_fm(lambda k, i=i: Wc[:, k, i * 128:(i + 1) * 128], lambda k: uT_seg[:, k, cs], n, bv,
                        [bW3, bown])
                proj_fm(lambda k, i=i: Wc[:, k, 512 + i * 128:512 + (i + 1) * 128], lambda k: uT_seg[:, k, cs],
                        n, bg_, [bW3, bown])
                si = sigc[0] % 2
                sigc[0] += 1
                P.op("act", lambda E, si=si, bg_=bg_: E.activation(sig[si][:, 0:n], psb[bg_][:, 0:n], AF.Sigmoid),
                     reads=[psbuf[bg_]], writes=[sig_b[si]])
                P.op("dve", lambda E, si=si, bv=bv, i=i: E.tensor_tensor(
                    cpad[:, i, 32 + o0:32 + o0 + n], psb[bv][:, 0:n], sig[si][:, 0:n], ALU.mult),
                    reads=[psbuf[bv], sig_b[si]], writes=[bcpad[seg]])
        for ti in range(9):
            qw = HALO if ti == 0 else 128
            c0 = 0 if ti == 0 else HALO + (ti - 1) * 128
            bank = nextbank()
            for k in range(8):
                P.op("pe", lambda E, k=k, bank=bank: E.matmul(psb[bank][0:qw, 0:24], uT_seg[:, k, c0:c0 + qw],
                                                              Wg[:, k, :], start=(k == 0), stop=(k == 7)),
                     reads=[bW3, bown], writes=[psbuf[bank]])
            P.op("act", lambda E, bank=bank, ti=ti: E.activation(gates[0:qw, seg * 9 + ti, :], psb[bank][0:qw, 0:24],
                                                                  AF.Sigmoid),
                 reads=[psbuf[bank]], writes=[bgates])
        P.dma("sp", uT_scr[:, :, base:base + OWNW], uT_seg[:], reads=[bown], writes=[bscr_u[seg]])
    dump("QT", [128, 4, OWN], QT[:], [bq], BF16)
    dump("gates", [128, NOT_, 24], gates[:], [bgates])
    dump("KwAug0", [128, NSEG, NEXT * 128], KwAug[0][:], [bext], BF16)
    dump("KxAug1", [128, NSEG, NEXT * 128], KxAug[1][:], [bext], BF16)
    dump("VwAug", [128, NSEG, NEXT, 2, 65], VwAug[:], [bext], BF16)
    dump("cpad0", [128, 4, 32 + OWNW], cpads[0][:], [bcpad[0]], BF16)
    P.barrier()
    es3.close()
    esL3.close()
    if stop_after == "ph3":
        return nc, P, dbg_out
    esCF = ES()
    cw = P.sb("cw", [128, 4, 31], F32, esCF)
    cb = P.sb("cb", [128, 4], F32, esCF)
    lng = P.sb("lng", [128, 4], F32, esCF)
    lnb = P.sb("lnb", [128, 4], F32, esCF)
    bWc = P.buf("Wcf")
    P.dma("sp", cw[:], conv_wT, writes=[bWc])
    with nc.allow_non_contiguous_dma(reason="tiny per-channel vectors"):
        P.dma("sp", cb[:], conv_b, writes=[bWc])
        P.dma("sp", lng[:], ln_g, writes=[bWc])
        P.dma("sp", lnb[:], ln_b, writes=[bWc])
    ones_bf = P.sb("ones_bf", [128, 128], BF16, esCF)
    P.op("dve", lambda E: E.memset(ones_bf[:], 1.0), writes=[bWc])
    cconv = P.sb("cconv", [128, 4, OWNW], F32, esCF)
    cc16 = P.sb("cc16", [128, 4, OWNW], BF16, esCF)
    csq16 = P.sb("csq16", [128, 4, OWNW], BF16, esCF)
    cT_seg = P.sb("cT_seg", [128, 4, OWNW], BF16, esCF)
    bcc = P.bufs(4, "cc")
    bcTs = P.buf("cTs")
    diag = P.sb("diag", [128, 4, 31, 128], BF16, esCF)
    diag_bs = P.bufs(4, "diag")
    lnm = P.sb("lnm", [128, 512], F32, esCF)
    lnr = P.sb("lnr", [128, 512], F32, esCF)
    lnt = [P.sb(f"lnt{i}", [128, 512], F32, esCF) for i in range(2)]
    bln = P.buf("ln")
    lnt_b = P.bufs(2, "lnt")
    diagc = [0]
    groups = ((0, 352), (352, 352), (704, 352))
    for i in range(4):
        for k in range(31):
            P.op("dve", lambda E, i=i, k=k: E.tensor_scalar(diag[:, i, k, :], ident[:], cw[:, i, 30 - k:31 - k], None,
                                                            ALU.mult), reads=[bWc, bconst], writes=[])
    P.op("dve", lambda E: E.memset(lnm[:, 0:8], 0.0), writes=diag_bs)
    for seg in range(NSEG):
        base = seg * OWNW
        cpad = cpads[seg]
        for i in range(4):
            banks = (0, 1, 2) if i % 2 == 0 else (3, 4, 5)
            for k in range(31):
                for gi, (o0, n) in enumerate(groups):
                    P.op("pe", lambda E, i=i, k=k, gi=gi, o0=o0, n=n: E.matmul(
                        psb[banks[gi]][:, 0:n], diag[:, i, k, :], cpad[:, i, 32 + o0 - k:32 + o0 - k + n],
                        start=(k == 0), stop=(k == 30)), reads=[diag_bs[i], bcpad[seg]],
                        writes=[psbuf[banks[gi]]])
            for gi, (o0, n) in enumerate(groups):
                bk = banks[gi]
                P.op("act", lambda E, bk=bk, i=i, o0=o0, n=n: E.activation(
                    cconv[:, i, o0:o0 + n], psb[bk][:, 0:n], AF.Identity, bias=cb[:, i:i + 1]),
                    reads=[psbuf[bk], bWc], writes=[bcc[i]])
                P.op("act", lambda E, bk=bk, i=i, o0=o0, n=n: E.activation(
                    csq16[:, i, o0:o0 + n], psb[bk][:, 0:n], AF.Square, bias=cb[:, i:i + 1]),
                    reads=[psbuf[bk], bWc], writes=[bcc[i]])
                P.op("dve", lambda E, i=i, o0=o0, n=n: E.tensor_copy(cc16[:, i, o0:o0 + n], cconv[:, i, o0:o0 + n]),
                     reads=[bcc[i]], writes=[bcc[i]])
        dump(f"cconv{seg}", [128, 4, OWNW], cconv[:], bcc)
        for (o0, n) in groups:
            for i in range(4):
                P.op("pe", lambda E, i=i: E.matmul(psb[6][:, 0:n], ones_bf[:], cc16[:, i, o0:o0 + n],
                                                   start=(i == 0), stop=(i == 3)),
                     reads=[bWc, bcc[i]], writes=[psbuf[6]])
            for i in range(4):
                P.op("pe", lambda E, i=i: E.matmul(psb[7][:, 0:n], ones_bf[:], csq16[:, i, o0:o0 + n],
                                                   start=(i == 0), stop=(i == 3)),
                     reads=[bWc, bcc[i]], writes=[psbuf[7]])
            P.op("dve", lambda E: E.tensor_scalar(lnm[:, 0:n], psb[6][:, 0:n], 1.0 / 512, None, ALU.mult),
                 reads=[psbuf[6]], writes=[bln])
            P.op("dve", lambda E: E.tensor_tensor(lnr[:, 0:n], lnm[:, 0:n], lnm[:, 0:n], ALU.mult),
                 reads=[bln], writes=[bln])
            P.op("dve", lambda E: E.scalar_tensor_tensor(lnr[:, 0:n], psb[7][:, 0:n], 1.0 / 512, lnr[:, 0:n],
                                                         ALU.mult, ALU.subtract),
                 reads=[psbuf[7], bln], writes=[bln])
            P.op("act", lambda E: E.activation(lnr[:, 0:n], lnr[:, 0:n], AF.Sqrt, bias=eps_t[:, 0:1]),
                 reads=[bln, bconst], writes=[bln])
            P.op("dve", lambda E: E.reciprocal(lnr[:, 0:n], lnr[:, 0:n]), reads=[bln], writes=[bln])
            for i in range(4):
                ti_ = i % 2
                P.op("dve", lambda E, i=i, ti_=ti_: E.tensor_tensor(lnt[ti_][:, 0:n], cconv[:, i, o0:o0 + n],
                                                                    lnm[:, 0:n], ALU.subtract),
                     reads=[bcc[i], bln], writes=[lnt_b[ti_]])
                P.op("dve", lambda E, ti_=ti_: E.tensor_tensor(lnt[ti_][:, 0:n], lnt[ti_][:, 0:n], lnr[:, 0:n],
                                                               ALU.mult),
                     reads=[bln, lnt_b[ti_]], writes=[lnt_b[ti_]])
                P.op("act", lambda E, i=i, ti_=ti_: E.activation(
                    cT_seg[:, i, o0:o0 + n], lnt[ti_][:, 0:n], AF.Silu, bias=lnb[:, i:i + 1],
                    scale=lng[:, i:i + 1]), reads=[lnt_b[ti_], bWc], writes=[bcTs])
        P.dma("sp", cT_scr[:, :, base:base + OWNW], cT_seg[:], reads=[bcTs], writes=[bscr_c[seg]])
        dump(f"cT{seg}", [128, 4, OWNW], cT_seg[:], [bcTs], BF16)
    P.barrier()
    esCF.close()
    esCP.close()
    if stop_after == "conf":
        return nc, P, dbg_out
    esP2c = ES()
    fcw_t = P.sb("fcw_t", [128, 44, 3], F32, esP2c, side="right")
    fcb_t = P.sb("fcb_t", [128, 44], F32, esP2c, side="right")
    gffn_bc = P.sb("gffn_bc", [128, D], F32, esP2c, side="right")
    gple_bc = P.sb("gple_bc", [128, D], F32, esP2c, side="right")
    gfin_bc = P.sb("gfin_bc", [128, D], F32, esP2c, side="right")
    bWp2 = P.buf("Wp2")
    P.dma("sp", fcw_t[:], fcw, writes=[bWp2])
    with nc.allow_non_contiguous_dma(reason="tiny per-channel vector"):
        P.dma("sp", fcb_t[:], fcb, writes=[bWp2])
    P.dma("sp", gffn_bc[:], g_ffn.partition_broadcast(128), writes=[bWp2])
    P.dma("sp", gple_bc[:], g_ple.partition_broadcast(128), writes=[bWp2])
    P.dma("sp", gfin_bc[:], g_fin.partition_broadcast(128), writes=[bWp2])
    esP1w = ES()
    Wo = P.sb("Wo", [128, 4, D], BF16, esP1w, side="right")
    Wco = P.sb("Wco", [128, 4, D], BF16, esP1w, side="right")
    Wm = P.sb("Wm", [128, 8, 2048], BF16, esP1w, side="right")
    Wout = P.sb("Wout", [128, 8, D], BF16, esP1w, side="right")
    bco_t = P.sb("bco_t", [128, 8], F32, esP1w, side="right")
    bWp1 = P.buf("Wp1")
    bWp1b = P.buf("Wp1b")
    P.dma("pool", Wo[:], w_o.rearrange("(k p) n -> p k n", p=128), writes=[bWp1])
    P.dma("pool", Wco[:], w_co.rearrange("(k p) n -> p k n", p=128), writes=[bWp1])
    P.dma("pool", Wm[:, :, 0:1024], w_in_v[:, :, CM:CM + 1024], writes=[bWp1])
    P.dma("pool", Wm[:, :, 1024:2048], w_in_v[:, :, CM + 1024:CM + 2048], writes=[bWp1])
    P.dma("pool", Wout[:], w_out.rearrange("(k p) n -> p k n", p=128), writes=[bWp1])
    with nc.allow_non_contiguous_dma(reason="tiny per-channel vector"):
        P.dma("sp", bco_t[:], b_co, writes=[bWp1b])
    esT = ES()
    bA = P.buf("attc")
    Qaug = [[P.sb(f"Qaug{g}_{i}", [128, 4, 128], BF16, esT) for i in range(2)] for g in range(2)]
    Qq_b = [P.bufs(2, f"Qq{g}") for g in range(2)]
    Qm_b = [P.bufs(2, f"Qm{g}") for g in range(2)]
    for g in range(2):
        for i in range(2):
            P.op("pool", lambda E, g=g, i=i: E.memset(Qaug[g][i][:], 0.0), writes=[Qq_b[g][i], Qm_b[g][i]])
    NPT = 4
    PT = [P.sb(f"PT{i}", [128, 512], BF16, esT) for i in range(NPT)]
    PT_b = P.bufs(NPT, "PT")
    cmk = [P.sb(f"cmk{i}", [128, 2, 128], BF16, esT) for i in range(2)]
    cmk_b = P.bufs(2, "cmk")
    slc = [P.sb(f"slc{i}", [128, 3, 64], F32, esT) for i in range(2)]
    slc_b = P.bufs(2, "slc")
    mcaus = P.sb("mcaus", [128, 128], BF16, esT)
    mfar = P.sb("mfar", [128, 128], BF16, esT)
    selmap = P.sb("selmap", [128, 2, 64], BF16, esT)
    zeros_bf = P.sb("zeros_bf", [128, 320], BF16, esT)
    P.dma("sp", mcaus[:], c_mcausal, writes=[bA])
    P.dma("sp", mfar[:], c_mfar, writes=[bA])
    P.dma("sp", selmap[:], c_selmap, writes=[bA])
    P.op("pool", lambda E: E.memset(zeros_bf[:], 0.0), writes=[bA])
    obf = [P.sb(f"obf{i}", [128, 512], BF16, esT) for i in range(2)]
    obf_b = P.bufs(2, "obf")
    oacc = P.sb("oacc", [128, 2, 256], F32, esT)
    otmp = P.sb("otmp", [128, 2, 256], F32, esT)
    rden = P.sb("rden", [128, 2, 3, 4], F32, esT)
    scl = P.sb("scl", [128, 2, 3, 4], F32, esT)
    imps = P.sb("imps", [128, 2, 64], F32, esT)
    impadj = P.sb("impadj", [128, 2, 64], F32, esT)
    tmpm = P.sb("tmpm", [128, 2, 64], F32, esT)
    self_ = P.sb("self_", [128, 2, 64], F32, esT)
    m8a = P.sb("m8a", [128, 2, 8], F32, esT)
    m8b = P.sb("m8b", [128, 2, 8], F32, esT)
    mnegw = P.sb("mnegw", [128, 2, 128], F32, esT)
    bsel = P.bufs(2, "sel")
    bcomb = P.bufs(2, "comb")
    oTt = [P.sb(f"oTt{i}", [128, 4, 128], BF16, esT) for i in range(2)]
    oTt_b = P.bufs(2, "oTt")
    otc = [0]
    S_BANKS = (0, 1, 2)
    OC, OS, OW, MISC, TR = 3, 4, 5, 6, 7
    sctr = [0]
    pctr = [0]
    psT7 = psb[TR].bitcast(BF16)
    attn_tiles = []
    for seg in range(NSEG):
        for ti in range(9):
            attn_tiles.append((seg, ti))
    if stop_after == "att1":
        attn_tiles = attn_tiles[:3]
    tcount = 0
    units_meta = []
    LA = 3
    pending = []
    deferred = []

    def push(s_fn, pv_fn):
        pi = s_fn()
        pending.append((pv_fn, pi))
        while len(pending) > LA:
            f, p_ = pending.pop(0)
            f(p_)

    def flush():
        while pending:
            f, p_ = pending.pop(0)
            f(p_)

    for (seg, ti) in attn_tiles:
        qw = HALO if ti == 0 else 128
        qc0 = seg * OWNW + (0 if ti == 0 else HALO + (ti - 1) * 128)
        T = NCTX - 1 if ti == 0 else NCTX + ti - 1
        qoff = 128 - HALO if ti == 0 else 0
        n_main = max(max(SEG_STARTS[r][seg] // 128 + (ti - 1 if ti >= 1 else -1), 0) for r in (0, 1))
        tidx = seg * 9 + ti
        sl = tcount % 2
        tcount += 1
        units_meta.append((seg, ti, qw, qc0, T, qoff, n_main, tidx, sl))

    def make_unit(seg, ti, qw, qc0, T, qoff, n_main, tidx, sl, g):
        gh = slice(g * 64, (g + 1) * 64)
        oh = slice((1 - g) * 64, (2 - g) * 64)
        Qa = Qaug[g][sl]
        qb, mb = Qq_b[g][sl], Qm_b[g][sl]
        rhsQ = Qa[:, :, 0:qw]
        bs_ = bsel[g]
        bc_ = bcomb[g]

        def s_step(lhsT, qbufs, mask=None, mbufs=()):
            sbk = S_BANKS[sctr[0] % len(S_BANKS)]
            sctr[0] += 1
            o3 = psb[sbk][:, 0:4 * qw].rearrange("p (h q) -> p h q", h=4)
            P.op("pe", lambda E: E.matmul(o3, lhsT, rhsQ, start=True, stop=(mask is None)),
                 reads=list(qbufs), writes=[psbuf[sbk]])
            if mask is not None:
                P.op("pe", lambda E: E.matmul(o3, ident[:], mask.unsqueeze(1).broadcast_to([128, 4, qw]),
                                              start=False, stop=True),
                     reads=[bconst, bA] + list(mbufs), writes=[psbuf[sbk]])
            pi = pctr[0] % NPT
            pctr[0] += 1
            P.op("act", lambda E: E.activation(PT[pi][:, 0:4 * qw], psb[sbk][:, 0:4 * qw], AF.Exp),
                 reads=[psbuf[sbk]], writes=[PT_b[pi]])
            return pi

        def zero_bank(bank, ncols):
            P.op("pe", lambda E: E.matmul(psb[bank][0:qw, 0:ncols], zeros_bf[:, 0:qw], zeros_bf[:, 0:ncols],
                                          start=True, stop=False, skip_group_check=True),
                 reads=[bA], writes=[psbuf[bank]])

        def pv(bank, pi, V_ap, last, w=65):
            for h in range(4):
                P.op("pe", lambda E, h=h: E.matmul(psb[bank][0:qw, h * w:(h + 1) * w],
                                                   PT[pi][:, h * qw:(h + 1) * qw], V_ap,
                                                   start=False, stop=last, skip_group_check=True),
                     reads=[PT_b[pi]], writes=[psbuf[bank]])

        def bview(bank):
            return psb[bank][0:qw, 0:260].rearrange("p (h e) -> p h e", e=65)[:, :, 0:64]

        def sview(br):
            return scl[0:qw, g, br, :].unsqueeze(2).broadcast_to([qw, 4, 64])

        oa = oacc[0:qw, g, :].rearrange("p (h d) -> p h d", h=4)
        ot = otmp[0:qw, g, :].rearrange("p (h d) -> p h d", h=4)
        ob = obf[sl][0:qw, g * 256:(g + 1) * 256].rearrange("p (h d) -> p h d", h=4)

        def den_scale(br, bank):
            bv = psb[bank][0:qw, 0:260].rearrange("p (h e) -> p h e", e=65)
            if br != 0:
                P.op("dve", lambda E: E.tensor_scalar(rden[0:qw, g, br, :], bv[:, :, 64], 1e-30, None, ALU.add),
                     reads=[psbuf[bank], bs_], writes=[bs_])
                P.op("dve", lambda E: E.reciprocal(rden[0:qw, g, br, :], rden[0:qw, g, br, :]),
                     reads=[bs_], writes=[bs_])
            P.op("dve", lambda E: E.tensor_tensor(
                scl[0:qw, g, br, :], rden[0:qw, g, br, :],
                gates[0:qw, tidx, br * 8 + g * 4:br * 8 + g * 4 + 4], ALU.mult),
                reads=[bs_], writes=[bs_])

        def cmp_pv(ct):
            def f(pi):
                if ct == 0:
                    zero_bank(OC, 260)
                    zero_bank(MISC, 256)
                pv(OC, pi, VcAug[:, ct, g, :], ct == 1)
                pv(MISC, pi, selmap[:, ct, :], ct == 1, w=64)
                if ct == 1:
                    selection()
            return f

        def selection():
            ocv = psb[OC][0:qw, 0:260].rearrange("p (h e) -> p h e", e=65)
            P.op("dve", lambda E: E.tensor_scalar(rden[0:qw, g, 0, :], ocv[:, :, 64], 1e-30, None, ALU.add),
                 reads=[psbuf[OC]], writes=[bs_])
            P.op("dve", lambda E: E.reciprocal(rden[0:qw, g, 0, :], rden[0:qw, g, 0, :]), reads=[bs_],
                 writes=[bs_])
            P.op("dve", lambda E: E.tensor_scalar(imps[0:qw, g, :], psb[MISC][0:qw, 0:64], rden[0:qw, g, 0, 0:1],
                                                  None, ALU.mult), reads=[psbuf[MISC], bs_], writes=[bs_])
            for h in range(1, 4):
                P.op("dve", lambda E, h=h: E.scalar_tensor_tensor(
                    imps[0:qw, g, :], psb[MISC][0:qw, h * 64:(h + 1) * 64], rden[0:qw, g, 0, h:h + 1],
                    imps[0:qw, g, :], ALU.mult, ALU.add), reads=[psbuf[MISC], bs_], writes=[bs_])
            P.op("dve", lambda E: E.tensor_tensor(impadj[0:qw, g, :], imps[0:qw, g, :], slc[sl][0:qw, 0, :],
                                                  ALU.mult), reads=[bs_, slc_b[sl]], writes=[bs_])
            P.op("dve", lambda E: E.tensor_tensor(impadj[0:qw, g, :], impadj[0:qw, g, :], slc[sl][0:qw, 1, :],
                                                  ALU.add), reads=[bs_, slc_b[sl]], writes=[bs_])
            P.op("dve", lambda E: E.max(m8a[0:qw, g, :], impadj[0:qw, g, :]), reads=[bs_], writes=[bs_])
            P.op("dve", lambda E: E.match_replace(tmpm[0:qw, g, :], m8a[0:qw, g, :], impadj[0:qw, g, :], -1e9),
                 reads=[bs_], writes=[bs_])
            P.op("dve", lambda E: E.max(m8b[0:qw, g, :], tmpm[0:qw, g, :]), reads=[bs_], writes=[bs_])
            P.op("dve", lambda E: E.tensor_scalar(self_[0:qw, g, :], impadj[0:qw, g, :], m8b[0:qw, g, 7:8], None,
                                                  ALU.is_ge), reads=[bs_], writes=[bs_])
            for half in range(2):
                P.op("dve", lambda E, half=half: E.scalar_tensor_tensor(
                    mnegw[0:qw, g, half * 64:(half + 1) * 64], self_[0:qw, g, :], BIG, slc[sl][0:qw, 2, :],
                    ALU.mult, ALU.add), reads=[bs_, slc_b[sl]], writes=[bs_])
            den_scale(0, OC)
            P.op("dve", lambda E: E.tensor_tensor(oa, bview(OC), sview(0), ALU.mult),
                 reads=[psbuf[OC], bs_], writes=[bc_])


        def c_part():
            if g == 0:
                P.dma("sp", cmk[sl][:, :, 0:qw], c_cmask[:, :, qc0:qc0 + qw], writes=[cmk_b[sl]])
                P.dma("sp", slc[sl][0:qw], c_selc[qc0:qc0 + qw], writes=[slc_b[sl]])
            P.op("pool", lambda E: E.tensor_copy(Qa[gh, :, 0:qw], QT[gh, :, qc0:qc0 + qw]), writes=[qb])
            for ct in range(2):
                push(lambda ct=ct: s_step(KcAug[g][:, ct * 128:(ct + 1) * 128], [qb], cmk[sl][:, ct, 0:qw],
                                          [cmk_b[sl]]), cmp_pv(ct))
            while deferred:
                deferred.pop(0)()

        def w_part():
            wsteps = [(T - 4, mfar)] + [(T - j, None) for j in (3, 2, 1)] + [(T, mcaus)]

            def win_pv(wi, kt):
                def f(pi):
                    if wi == 0:
                        zero_bank(OW, 260)
                    last = wi == len(wsteps) - 1
                    pv(OW, pi, VwAug[:, seg, kt, g, :], last)
                    if last:
                        den_scale(2, OW)
                        P.op("dve", lambda E: E.tensor_tensor(ot, bview(OW), sview(2), ALU.mult),
                             reads=[psbuf[OW], bs_], writes=[bc_])
                        P.op("dve", lambda E: E.tensor_tensor(oa, oa, ot, ALU.add), reads=[bc_], writes=[bc_])
                return f

            for wi, (kt, msk) in enumerate(wsteps):
                m_ap = None if msk is None else msk[:, qoff:qoff + qw]
                push(lambda kt=kt, m_ap=m_ap: s_step(KwAug[g][:, seg, kt * 128:(kt + 1) * 128], [qb], m_ap),
                     win_pv(wi, kt))

        def s_part():
            P.op("pe", lambda E: E.transpose(psb[MISC][:, 256:256 + qw], mnegw[0:qw, g, :], identf[0:qw, 0:qw]),
                 reads=[bs_, bconst], writes=[psbuf[MISC]])
            P.op("act", lambda E: E.copy(Qa[oh, :, 0:qw],
                                         psb[MISC][oh, 256:256 + qw].unsqueeze(1).broadcast_to([64, 4, qw])),
                 reads=[psbuf[MISC]], writes=[mb])
            def sel_pv(idx, V_ap):
                def f(pi):
                    if idx == 0:
                        zero_bank(OS, 260)
                    last = idx == n_main
                    pv(OS, pi, V_ap, last)
                    if last:
                        den_scale(1, OS)
                        P.op("dve", lambda E: E.tensor_tensor(ot, bview(OS), sview(1), ALU.mult),
                             reads=[psbuf[OS], bs_], writes=[bc_])
                        P.op("dve", lambda E: E.tensor_tensor(ob, oa, ot, ALU.add), reads=[bc_],
                             writes=[bc_, obf_b[sl]])
                        if g == 1:
                            deferred.append(o_transposes)
                return f

            def o_transposes():
                for c in range(4):
                    P.op("pe", lambda E, c=c: E.transpose(psT7[:, 512 + c * 128:512 + c * 128 + qw],
                                                          obf[sl][0:qw, c * 128:(c + 1) * 128], ident[0:qw, 0:qw]),
                         reads=[obf_b[sl], bconst], writes=[psbuf[TR]])
                oi = otc[0] % 2
                otc[0] += 1
                P.op("act", lambda E: E.copy(oTt[oi][:, :, 0:qw],
                                             psT7[:, 512:1024].rearrange("p (c t) -> p c t", c=4)[:, :, 0:qw]),
                     reads=[psbuf[TR]], writes=[oTt_b[oi]])
                P.dma("sp", oT_scr[:, :, qc0:qc0 + qw], oTt[oi][:, :, 0:qw], reads=[oTt_b[oi]], writes=[boT],
                      sembuf=oTt_b[oi])

            for kt in range(n_main):
                push(lambda kt=kt: s_step(KsAug[g][:, kt * 128:(kt + 1) * 128], [qb, mb]),
                     sel_pv(kt, VsAug[:, kt, g, :]))
            push(lambda: s_step(KxAug[g][:, seg, T * 128:(T + 1) * 128], [qb, mb], mcaus[:, qoff:qoff + qw]),
                 sel_pv(n_main, VxAug[:, seg, T, g, :]))


        return c_part, w_part, s_part

    units = [make_unit(*m, g) for m in units_meta for g in range(2)]
    for ui, (cp, wp, sp_) in enumerate(units):
        if ui == 0:
            cp()
        wp()
        if ui + 1 < len(units):
            units[ui + 1][0]()
        sp_()
    flush()
    while deferred:
        deferred.pop(0)()
    if "oT" in debug:
        t_ = nc.dram_tensor("dbg_oT", [128, 4, OWN], BF16, kind="ExternalOutput").ap()
        P.dma("sp", t_, oT_scr, reads=[boT], writes=[P.buf("dbg_oT")])
    P.barrier()
    esT.close()
    esL2.close()
    if stop_after in ("att", "att1"):
        return nc, P, dbg_out
    esP2w = ES()
    NJ = 11
    WPIECES = ((0, 4), (4, 4), (8, 3))
    Wup = P.sb("Wup", [128, 8, 2 * NJ * 128], BF16, esP2w)
    bWup = P.bufs(len(WPIECES), "Wup")
    w_up_v = w_up.rearrange("(k p) n -> p k n", p=128)

    def load_wup(half):
        j0_ = half * NJ
        for pi_, (pj, pn) in enumerate(WPIECES):
            P.dma("pool", Wup[:, :, pj * 128:(pj + pn) * 128],
                  w_up_v[:, :, (j0_ + pj) * 128:(j0_ + pj + pn) * 128], writes=[bWup[pi_]])
            P.dma("pool", Wup[:, :, (NJ + pj) * 128:(NJ + pj + pn) * 128],
                  w_up_v[:, :, DFF + (j0_ + pj) * 128:DFF + (j0_ + pj + pn) * 128], writes=[bWup[pi_]])

    esP1 = ES()
    load_wup(0)
    oTr = [P.sb(f"oTr{i}", [128, 4, 512], BF16, esP1) for i in range(2)]
    oTr_b = P.bufs(2, "oTr")
    uTt = [P.sb(f"uTt{i}", [128, 8, 512], BF16, esP1) for i in range(2)]
    uTt_b = P.bufs(2, "uTt")
    cTt = [P.sb(f"cTt{i}", [128, 4, 512], BF16, esP1) for i in range(2)]
    cTt_b = P.bufs(2, "cTt")
    mT = P.sb("mT", [128, 8, 512], BF16, esP1)
    mT_b = P.bufs(8, "mT")
    gab = [P.sb(f"gab{i}", [128, 2, 512], F32, esP1) for i in range(2)]
    gab_b = P.bufs(2, "gab")
    t12 = [P.sb(f"t12{i}", [128, 2, 512], F32, esP1) for i in range(2)]
    t12_b = P.bufs(2, "t12")
    xres = [P.sb(f"xres{i}", [128, D], F32, esP1) for i in range(2)]
    xres_b = P.bufs(2, "xres")
    h1t = [P.sb(f"h1t{i}", [128, D], F32, esP1) for i in range(2)]
    h1t_b = P.bufs(2, "h1t")
    bh1s = [P.buf(f"h1scr{s}") for s in range(NSEG)]
    brr = [0]

    def nb():
        b = brr[0] % 8
        brr[0] += 1
        return b

    gcnt = 0
    fcnt = 0
    tcnt = 0
    p1groups = [(seg, o0, n) for seg in range(NSEG) for (o0, n) in ((0, HALO + 256), (HALO + 256, 384), (HALO + 640, 384))]

    def p1_tiles(o0, n):
        tl = []
        c = 0
        if o0 == 0:
            tl.append((0, HALO))
            c = HALO
        while c < n:
            tl.append((c, 128))
            c += 128
        return tl

    def p1_loads(gidx):
        seg_, o0_, n_ = p1groups[gidx]
        base_ = seg_ * OWNW
        gi_ = gidx % 2
        P.dma("sp", uTt[gi_][:, :, 0:n_], uT_scr[:, :, base_ + o0_:base_ + o0_ + n_], reads=[bscr_u[seg_]],
              writes=[uTt_b[gi_]])
        P.dma("sp", cTt[gi_][:, :, 0:n_], cT_scr[:, :, base_ + o0_:base_ + o0_ + n_], reads=[bscr_c[seg_]],
              writes=[cTt_b[gi_]])
        P.dma("sp", oTr[gi_][:, :, 0:n_], oT_scr[:, :, base_ + o0_:base_ + o0_ + n_], reads=[boT],
              writes=[oTr_b[gi_]])

    p1_loads(0)
    for gidx, (seg, o0, n) in enumerate(p1groups):
        base = seg * OWNW
        if True:
            gi = gcnt % 2
            gcnt += 1
            if gidx + 1 < len(p1groups):
                p1_loads(gidx + 1)
            for f in range(8):
                fs = slice(f * 128, (f + 1) * 128)
                bA_, bB_, bC_, bD_ = nb(), nb(), nb(), nb()
                proj_fm(lambda k: Wo[:, k, fs], lambda k: oTr[gi][:, k, 0:n], n, bA_, [bWp1, oTr_b[gi]], nk=4)
                proj_fm(lambda k: Wco[:, k, fs], lambda k: cTt[gi][:, k, 0:n], n, bB_, [bWp1, cTt_b[gi]], nk=4)
                proj_fm(lambda k: Wm[:, k, fs], lambda k: uTt[gi][:, k, 0:n], n, bC_, [bWp1, uTt_b[gi]])
                proj_fm(lambda k: Wm[:, k, 1024 + f * 128:1024 + (f + 1) * 128], lambda k: uTt[gi][:, k, 0:n], n, bD_,
                        [bWp1, uTt_b[gi]])
                fi = fcnt % 2
                fcnt += 1
                P.op("act", lambda E: E.activation(gab[fi][:, 0, 0:n], psb[bC_][:, 0:n], AF.Sigmoid),
                     reads=[psbuf[bC_]], writes=[gab_b[fi]])
                P.op("act", lambda E: E.activation(gab[fi][:, 1, 0:n], psb[bD_][:, 0:n], AF.Sigmoid),
                     reads=[psbuf[bD_]], writes=[gab_b[fi]])
                P.op("dve", lambda E: E.tensor_tensor(t12[fi][:, 0, 0:n], psb[bA_][:, 0:n], gab[fi][:, 0, 0:n], ALU.mult),
                     reads=[psbuf[bA_], gab_b[fi]], writes=[t12_b[fi]])
                P.op("dve", lambda E: E.scalar_tensor_tensor(t12[fi][:, 1, 0:n], psb[bB_][:, 0:n], bco_t[:, f:f + 1],
                                                             gab[fi][:, 1, 0:n], ALU.add, ALU.mult),
                     reads=[psbuf[bB_], gab_b[fi], bWp1b], writes=[t12_b[fi]])
                P.op("pool", lambda E: E.tensor_tensor(mT[:, f, 0:n], t12[fi][:, 0, 0:n], t12[fi][:, 1, 0:n], ALU.add),
                     reads=[t12_b[fi]], writes=[mT_b[f]])
            tiles = p1_tiles(o0, n)
            erow0 = NCTX * 128 - HALO + o0
            t0_, w0_ = tiles[0]
            P.dma("sp", xres[tcnt % 2][0:w0_, :], xe[seg, erow0 + t0_:erow0 + t0_ + w0_, :], writes=[xres_b[tcnt % 2]])
            for j, (toff, tw) in enumerate(tiles):
                xi = tcnt % 2
                tcnt += 1
                if j + 1 < len(tiles):
                    t1_, w1_ = tiles[j + 1]
                    P.dma("sp", xres[tcnt % 2][0:w1_, :], xe[seg, erow0 + t1_:erow0 + t1_ + w1_, :],
                          writes=[xres_b[tcnt % 2]])
                for hf in range(2):
                    bk = nb()
                    for f in range(8):
                        P.op("pe", lambda E, f=f: E.matmul(psb[bk][0:tw, :], mT[:, f, toff:toff + tw],
                                                           Wout[:, f, hf * 512:(hf + 1) * 512],
                                                           start=(f == 0), stop=(f == 7)),
                             reads=[mT_b[f], bWp1], writes=[psbuf[bk]])
                    P.op("dve", lambda E: E.tensor_tensor(h1t[xi][0:tw, hf * 512:(hf + 1) * 512], psb[bk][0:tw, :],
                                                          xres[xi][0:tw, hf * 512:(hf + 1) * 512], ALU.add),
                         reads=[psbuf[bk], xres_b[xi]], writes=[h1t_b[xi]])
                P.dma("sp", h1_scr[seg, o0 + toff:o0 + toff + tw, :], h1t[xi][0:tw, :], reads=[h1t_b[xi]],
                      writes=[bh1s[seg]], sembuf=h1t_b[xi])
    if "h1" in debug:
        t = nc.dram_tensor("dbg_h1", [NSEG, OWNW, D], F32, kind="ExternalOutput").ap()
        bd = P.buf("dbg_h1")
        P.dma("sp", t, h1_scr, reads=bh1s, writes=[bd])
    P.barrier()
    esP1.close()
    esP1w.close()
    if stop_after == "p1":
        return nc, P, dbg_out
    esP2 = ES()
    Wdn = P.sb("Wdn", [128, NJ, D], BF16, esP2)
    bWdn = P.bufs(len(WPIECES), "Wdn")
    w_dn_v = w_down.rearrange("(j p) n -> p j n", p=128)

    def load_wdn(half):
        j0_ = half * NJ
        for pi_, (pj, pn) in enumerate(WPIECES):
            P.dma("pool", Wdn[:, pj:pj + pn, :], w_dn_v[:, j0_ + pj:j0_ + pj + pn, :], writes=[bWdn[pi_]])

    def piece_of(jj):
        for pi_, (pj, pn) in enumerate(WPIECES):
            if pj <= jj < pj + pn:
                return pi_

    load_wdn(0)
    Wpg = P.sb("Wpg", [128, 8, D], BF16, esP2)
    Wpl = P.sb("Wpl", [128, 2, D], BF16, esP2)
    bWp3 = P.buf("Wp3")
    P.dma("pool", Wpg[:], w_pg.rearrange("(k p) n -> p k n", p=128), writes=[bWp3])
    P.dma("pool", Wpl[:], w_ple.rearrange("(k p) n -> p k n", p=128), writes=[bWp3])
    front2 = make_front(esP2, "b", with_xs=False, NUTM=4)
    front3 = make_front(esP2, "c", with_xs=False, NUTM=2)
    tails = [P.sb(f"tail{i}", [128, 2 * NJ, 2], F32, esP2) for i in range(2)]
    btails = [P.bufs(2 * NJ, f"tail{i}") for i in range(2)]
    h1f = [P.sb(f"h1f{i}", [128, D], F32, esP2) for i in range(2)]
    h1f_b = P.bufs(2, "h1f")
    u2T = [P.sb(f"u2T{i}", [128, 8, 512], BF16, esP2) for i in range(2)]
    u2T_b = P.bufs(2, "u2T")
    u2h = P.sb("u2h", [128, 8, HALO], BF16, esP2)
    u2h_b = P.buf("u2h")
    aT = P.sb("aT", [128, NJ, 512], BF16, esP2)
    aT_b = P.bufs(NJ, "aT")
    vbuf = [[P.sb(f"vbuf{kd}{i}", [128, 516], F32, esP2) for i in range(2)] for kd in range(2)]
    vbuf_b = [P.bufs(2, f"vbuf{kd}") for kd in range(2)]
    cgv = [[P.sb(f"cgv{kd}{i}", [128, 512], F32, esP2) for i in range(2)] for kd in range(2)]
    cgv_b = [P.bufs(2, f"cgv{kd}") for kd in range(2)]
    NH2 = 4
    h2t = [P.sb(f"h2t{i}", [128, D], F32, esP2) for i in range(NH2)]
    h2t_b = P.bufs(NH2, "h2t")
    u3T = [P.sb(f"u3T{i}", [128, 8, 128], BF16, esP2) for i in range(2)]
    u3T_b = P.bufs(2, "u3T")
    gate_s = [P.sb("gate_s0", [128, D], F32, esP2)] * 2
    gate_b = [P.buf("gate_s")] * 2
    pt = [P.sb(f"pt{i}", [128, 256], F32, esP2) for i in range(4)]
    pt_b = P.bufs(4, "pt")
    pbf = [P.sb(f"pbf{i}", [128, 256], BF16, esP2) for i in range(2)]
    pbf_b = P.bufs(2, "pbf")
    pT = [P.sb(f"pT{i}", [128, 2, 128], BF16, esP2) for i in range(2)]
    pT_b = P.bufs(2, "pT")
    fss = [P.sb(f"fss{i}", [128, 4], F32, esP2) for i in range(2)]
    fss_b = P.bufs(2, "fss")
    outt = [P.sb(f"outt{i}", [128, D], F32, esP2) for i in range(2)]
    outt_b = P.bufs(2, "outt")
    h2_scr = nc.dram_tensor("h2_scr", [NSEG, SEGLEN, D], F32, kind="Internal").ap()
    u2_scr = nc.dram_tensor("u2_scr", [2 * NSEG, 128, 8, 512], BF16, kind="Internal").ap()
    bu2s = P.bufs(2 * NSEG, "u2s")
    u2h_scr = nc.dram_tensor("u2h_scr", [NSEG, 128, 8, HALO], BF16, kind="Internal").ap()
    bu2h = P.bufs(NSEG, "u2h")
    bh2s = [[P.buf(f"h2s{s}_{t}") for t in range(8)] for s in range(NSEG)]
    bout = P.buf("out")
    brr2 = [0]

    reserved = set()

    def nb2(lo=0, hi=5):
        while True:
            hi_ = hi + (1 if (hi == 5 and cur_half[0] == 0) else 0)
            b = lo + brr2[0] % (hi_ - lo)
            brr2[0] += 1
            if b not in reserved:
                return b

    PBANK = 5
    psP = psb[PBANK].bitcast(BF16)
    cctr = [0]
    fctr = [0]
    groups = [(seg, grp) for seg in range(NSEG) for grp in range(2)]

    cur_half = [0]
    for half in range(2):
        j0 = half * NJ
        cur_half[0] = half

        def halo_stage(seg):
            tail, btail = tails[seg % 2], btails[seg % 2]
            if half == 0:
                P.dma("sp", h1f[0][0:HALO, :], h1_scr[seg, 0:HALO, :], reads=[bh1s[seg]], writes=[h1f_b[0]])
                front2(None, u2h[:, :, :], u2h_b, gffn_bc, nrows=HALO, xres=(h1f[0], h1f_b[0]))
                front2.flush()
                P.dma("sp", u2h_scr[seg], u2h[:], reads=[u2h_b], writes=[bu2h[seg]])
            else:
                P.dma("sp", u2h[:], u2h_scr[seg], reads=[bu2h[seg]], writes=[u2h_b])
            bk = nb2()
            for kd in range(2):
                for jj in range(NJ):
                    cidx = kd * NJ + jj
                    for k in range(8):
                        P.op("pe", lambda E, k=k: E.matmul(psb[bk][:, 2 * cidx:2 * cidx + 2],
                                                           Wup[:, k, cidx * 128:(cidx + 1) * 128],
                                                           u2h[:, k, HALO - 2:HALO], start=(k == 0), stop=(k == 7)),
                             reads=[bWup[piece_of(jj)], u2h_b], writes=[psbuf[bk]])
            P.op("act", lambda E: E.activation(tail[:, :, :],
                                               psb[bk][:, 0:4 * NJ].rearrange("p (c t) -> p c t", t=2),
                                               AF.Identity, scale=hvalid[:, seg:seg + 1]),
                 reads=[psbuf[bk], bconst], writes=btail)

        def f_stage(gidx):
            seg, grp = groups[gidx]
            ub = gidx % 2
            if half == 1:
                P.dma("sp", u2T[ub][:], u2_scr[gidx], reads=[bu2s[gidx]], writes=[u2T_b[ub]])
                return
            for j in range(4):
                r0 = HALO + grp * 512 + j * 128
                fi = fctr[0] % 2
                fctr[0] += 1
                P.dma("sp", h1f[fi][:], h1_scr[seg, r0:r0 + 128, :], reads=[bh1s[seg]], writes=[h1f_b[fi]])
                front2(None, u2T[ub][:, :, j * 128:(j + 1) * 128], u2T_b[ub], gffn_bc, xres=(h1f[fi], h1f_b[fi]),
                       hold=True)

        def f_done(gidx):
            if half == 0:
                ub = gidx % 2
                P.dma("sp", u2_scr[gidx], u2T[ub][:], reads=[u2T_b[ub]], writes=[bu2s[gidx]])

        pre = {}

        def j_proj(gidx, jj):
            ub = gidx % 2
            banks = []
            for kd in range(2):
                cidx = kd * NJ + jj
                bk = nb2()
                proj_fm(lambda k: Wup[:, k, cidx * 128:(cidx + 1) * 128], lambda k: u2T[ub][:, k, :], 512, bk,
                        [bWup[piece_of(jj)], u2T_b[ub]])
                banks.append(bk)
            return banks

        def j_chain(gidx, jj, banks):
            tail, btail = tails[groups[gidx][0] % 2], btails[groups[gidx][0] % 2]
            res = []
            for kd in range(2):
                cidx = kd * NJ + jj
                ch = kd * 22 + j0 + jj
                bk = banks[kd]
                vi = cctr[0] % 2
                vb, vbb = vbuf[kd][vi], vbuf_b[kd][vi]
                cg, cgb = cgv[kd][vi], cgv_b[kd][vi]
                P.op("pool", lambda E: E.tensor_copy(vb[:, 0:2], tail[:, cidx, :]), reads=[btail[cidx]],
                     writes=[vbb])
                P.op("act", lambda E: E.copy(vb[:, 2:514], psb[bk][:, :]), reads=[psbuf[bk]], writes=[vbb])
                P.op("pool", lambda E: E.tensor_copy(tail[:, cidx, :], vb[:, 512:514]), reads=[vbb],
                     writes=[btail[cidx]])
                P.op("act", lambda E: E.activation(cg[:, :], psb[bk][:, :], AF.Identity,
                                                   bias=fcb_t[:, ch:ch + 1], scale=fcw_t[:, ch, 2:3]),
                     reads=[psbuf[bk], bWp2], writes=[cgb])
                P.op("dve", lambda E: E.scalar_tensor_tensor(cg[:, :], vb[:, 1:513], fcw_t[:, ch, 1:2], cg[:, :],
                                                             ALU.mult, ALU.add),
                     reads=[vbb, cgb, bWp2], writes=[cgb])
                P.op("dve", lambda E: E.scalar_tensor_tensor(cg[:, :], vb[:, 0:512], fcw_t[:, ch, 0:1], cg[:, :],
                                                             ALU.mult, ALU.add),
                     reads=[vbb, cgb, bWp2], writes=[cgb])
                res.append((cg, cgb))
            cctr[0] += 1
            (cgg, cggb), (cgvv, cgvb) = res
            P.op("act", lambda E: E.activation(cgg[:, :], cgg[:, :], AF.Gelu_apprx_tanh), reads=[cggb],
                 writes=[cggb])
            P.op("dve", lambda E: E.tensor_tensor(aT[:, jj, :], cgg[:, :], cgvv[:, :], ALU.mult),
                 reads=[cggb, cgvb], writes=[aT_b[jj]])

        def j_stage(gidx):
            for jj in range(NJ):
                if (gidx, jj) in pre:
                    banks = pre.pop((gidx, jj))
                    for b_ in banks:
                        reserved.discard(b_)
                else:
                    banks = j_proj(gidx, jj)
                j_chain(gidx, jj, banks)

        def j_preissue(gidx, jj):
            banks = j_proj(gidx, jj)
            pre[(gidx, jj)] = banks
            reserved.update(banks)

        def down_mm(j, hf, bk):
            for jj in range(NJ):
                P.op("pe", lambda E, jj=jj: E.matmul(psb[bk][:, :], aT[:, jj, j * 128:(j + 1) * 128],
                                                     Wdn[:, jj, hf * 512:(hf + 1) * 512],
                                                     start=(jj == 0), stop=(jj == NJ - 1)),
                     reads=[aT_b[jj], bWdn[piece_of(jj)]], writes=[psbuf[bk]])

        def w_stage_half0(gidx):
            seg, grp = groups[gidx]
            for j in range(4):
                r0 = grp * 512 + j * 128
                P.dma("sp", h2t[j % NH2][:], h1_scr[seg, HALO + r0:HALO + r0 + 128, :], reads=[bh1s[seg]],
                      writes=[h2t_b[j % NH2]])
            for j in range(4):
                r0 = grp * 512 + j * 128
                si = j % NH2
                for hf in range(2):
                    bk = nb2()
                    down_mm(j, hf, bk)
                    hs = slice(hf * 512, (hf + 1) * 512)
                    P.op("dve", lambda E: E.tensor_tensor(h2t[si][:, hs], psb[bk][:, :], h2t[si][:, hs], ALU.add),
                         reads=[psbuf[bk], h2t_b[si]], writes=[h2t_b[si]])
                P.dma("sp", h2_scr[seg, r0:r0 + 128, :], h2t[si][:], reads=[h2t_b[si]],
                      writes=[bh2s[seg][grp * 4 + j]], sembuf=h2t_b[si])

        def w_stage_half1(gidx):
            seg, grp = groups[gidx]

            for j in range(4):
                r0 = grp * 512 + j * 128
                P.dma("sp", h2t[j % NH2][:], h2_scr[seg, r0:r0 + 128, :], reads=[bh2s[seg][grp * 4 + j]],
                      writes=[h2t_b[j % NH2]])
                P.dma("sp", pt[j][:], pown[seg, r0:r0 + 128, :], writes=[pt_b[j]])

            def s0(j):
                r0 = grp * 512 + j * 128
                si = j % NH2
                for hf in range(2):
                    bk = nb2()
                    down_mm(j, hf, bk)
                    hs = slice(hf * 512, (hf + 1) * 512)
                    P.op("dve", lambda E: E.tensor_tensor(h2t[si][:, hs], psb[bk][:, :], h2t[si][:, hs], ALU.add),
                         reads=[psbuf[bk], h2t_b[si]], writes=[h2t_b[si]])

            def s1(j):
                r0 = grp * 512 + j * 128
                si, s2_ = j % NH2, j % 2
                front3(None, u3T[s2_][:, :, :], u3T_b[s2_], gple_bc, xres=(h2t[si], h2t_b[si]), hold=True)
                P.op("pool", lambda E: E.tensor_copy(pbf[s2_][:], pt[j][:]), reads=[pt_b[j]], writes=[pbf_b[s2_]])

            def s2(j):
                s2_ = j % 2
                front3.flush(1)
                for kp in range(2):
                    P.op("pe", lambda E, kp=kp: E.transpose(psP[:, kp * 128:(kp + 1) * 128],
                                                            pbf[s2_][:, kp * 128:(kp + 1) * 128], ident[:]),
                         reads=[pbf_b[s2_], bconst], writes=[psbuf[PBANK]])
                P.op("act", lambda E: E.copy(pT[s2_][:], psP[:, 0:256].rearrange("p (k t) -> p k t", k=2)),
                     reads=[psbuf[PBANK]], writes=[pT_b[s2_]])

            def s3(j):
                si, s2_ = j % NH2, j % 2
                for hf in range(2):
                    hs = slice(hf * 512, (hf + 1) * 512)
                    bkg = nb2()
                    for k in range(8):
                        P.op("pe", lambda E, k=k: E.matmul(psb[bkg][:, :], u3T[s2_][:, k, :], Wpg[:, k, hs],
                                                           start=(k == 0), stop=(k == 7)),
                             reads=[u3T_b[s2_], bWp3], writes=[psbuf[bkg]])
                    P.op("act", lambda E: E.activation(gate_s[s2_][:, hs], psb[bkg][:, :], AF.Sigmoid),
                         reads=[psbuf[bkg]], writes=[gate_b[s2_]])
                    bkp = nb2()
                    for kp in range(2):
                        P.op("pe", lambda E, kp=kp: E.matmul(psb[bkp][:, :], pT[s2_][:, kp, :], Wpl[:, kp, hs],
                                                             start=(kp == 0), stop=(kp == 1)),
                             reads=[pT_b[s2_], bWp3], writes=[psbuf[bkp]])
                    P.op("dve", lambda E: E.tensor_tensor(gate_s[s2_][:, hs], psb[bkp][:, :], gate_s[s2_][:, hs],
                                                          ALU.mult),
                         reads=[psbuf[bkp], gate_b[s2_]], writes=[gate_b[s2_]])
                    P.op("pool", lambda E: E.tensor_tensor(h2t[si][:, hs], h2t[si][:, hs], gate_s[s2_][:, hs], ALU.add),
                         reads=[gate_b[s2_], h2t_b[si]], writes=[h2t_b[si]])

            def s4(j):
                r0 = grp * 512 + j * 128
                si, s2_ = j % NH2, j % 2
                P.op("act", lambda E: E.activation(outt[s2_][:], h2t[si][:], AF.Square, accum_out=fss[s2_][:, 0:1]),
                     reads=[h2t_b[si]], writes=[fss_b[s2_], outt_b[s2_]])
                P.op("act", lambda E: E.activation(fss[s2_][:, 1:2], fss[s2_][:, 0:1], AF.Sqrt, bias=eps_t[:, 0:1],
                                                   scale=1.0 / D), reads=[fss_b[s2_], bconst], writes=[fss_b[s2_]])
                P.op("dve", lambda E: E.reciprocal(fss[s2_][:, 2:3], fss[s2_][:, 1:2]), reads=[fss_b[s2_]],
                     writes=[fss_b[s2_]])
                P.op("dve", lambda E: E.scalar_tensor_tensor(outt[s2_][:], h2t[si][:], fss[s2_][:, 2:3], gfin_bc[:],
                                                             ALU.mult, ALU.mult),
                     reads=[h2t_b[si], fss_b[s2_], bconst], writes=[outt_b[s2_]])
                P.dma("sp", out[seg, r0:r0 + 128, :], outt[s2_][:], reads=[outt_b[s2_]], writes=[bout],
                      sembuf=outt_b[s2_])

            stages = (s0, s1, s2, s3, s4)
            for t in range(4 + len(stages) - 1):
                for si_, fn in enumerate(stages):
                    j = t - si_
                    if 0 <= j < 4:
                        fn(j)

        ng = len(groups)
        f_stage(0)
        front2.flush()
        f_done(0)
        halo_stage(groups[0][0])
        if ng > 1:
            if groups[1][0] != groups[0][0]:
                halo_stage(groups[1][0])
            f_stage(1)
        for gidx in range(ng):
            j_stage(gidx)
            front2.flush()
            if gidx + 1 < ng:
                f_done(gidx + 1)
            if gidx + 2 < ng:
                if groups[gidx + 2][0] != groups[gidx + 1][0]:
                    halo_stage(groups[gidx + 2][0])
                f_stage(gidx + 2)
            if gidx == ng - 1 and half == 0:
                load_wup(1)
            if half == 0:
                w_stage_half0(gidx)
            else:
                w_stage_half1(gidx)
        if half == 0:
            load_wdn(1)
    P.barrier()
    esP2.close()
    esP2w.close()
    esP2c.close()

    return nc, P, dbg_out


def make_in_maps(inputs, cores=range(NCORES)):
    x = np.asarray(inputs["x"], np.float32)
    p = np.asarray(inputs["p"], np.float32)
    f = lambda k: np.ascontiguousarray(np.asarray(inputs[k], np.float32)[0])
    pc = lambda v: np.ascontiguousarray(v.reshape(-1, 128).T)
    shared = {
        "w_in": f("w_in"), "g_mix": f("g_mix"),
        "pos_kT": np.ascontiguousarray(f("cmp_pos_k").T), "pos_vT": np.ascontiguousarray(f("cmp_pos_v").T),
        "w_k1": f("w_cmp_k1"), "w_k2": f("w_cmp_k2"), "w_v1": f("w_cmp_v1"), "w_v2": f("w_cmp_v2"),
        "w_o": f("w_o_nsa"),
        "conv_wT": np.ascontiguousarray(f("conv_w").T.reshape(4, 128, 31).transpose(1, 0, 2)),
        "conv_b": pc(f("conv_b")), "ln_g": pc(f("conv_ln_g")), "ln_b": pc(f("conv_ln_b")),
        "w_co": f("w_conv_out"), "b_co": pc(f("b_conv_out")),
        "w_out": f("w_out"), "g_ffn": f("g_ffn"), "w_up": f("w_up"),
        "fcw": np.ascontiguousarray(f("ffn_conv_w").T.reshape(44, 128, 3).transpose(1, 0, 2)),
        "fcb": pc(f("ffn_conv_b")), "w_down": f("w_down"),
        "g_ple": f("g_ple"), "w_pg": f("w_ple_gate"), "w_ple": f("w_ple"),
        "g_fin": np.ascontiguousarray(np.asarray(inputs["g_final"], np.float32)),
    }
    w = shared["w_in"].copy()
    wq = w[:, 0:512].reshape(D, 2, 4, 64).transpose(0, 2, 1, 3).reshape(D, 512)
    w[:, 0:512] = wq
    shared["w_in"] = w
    consts = {r: host_consts(r) for r in (0, 1)}
    maps = []
    for c in cores:
        b, r = c // 2, c % 2
        m = dict(shared)
        m.update(consts[r])
        m["xc"] = np.ascontiguousarray(x[b])
        xe = np.zeros((NSEG, NEXT * 128, D), np.float32)
        po = np.zeros((NSEG, SEGLEN, 256), np.float32)
        for s, s0 in enumerate(SEG_STARTS[r]):
            lo = s0 - NCTX * 128
            src_lo = max(lo, 0)
            xe[s, src_lo - lo:] = x[b, src_lo:s0 + SEGLEN]
            po[s] = p[0, b, s0:s0 + SEGLEN]
        m["xe"] = xe
        m["pown"] = po
        maps.append(m)
    return maps


_NC_CACHE = {}


def kernel(**inputs):
    if "nc" not in _NC_CACHE:
        _NC_CACHE["nc"] = build()[0]
    nc = _NC_CACHE["nc"]
    maps = make_in_maps(inputs)
    res = run_bass_kernel_spmd(nc, maps, core_ids=list(range(NCORES)))
    outp = np.zeros((NB, S, D), np.float32)
    for c in range(NCORES):
        b, r = c // 2, c % 2
        o = res.results[c]["out"]
        for s, s0 in enumerate(SEG_STARTS[r]):
            outp[b, s0:s0 + SEGLEN] = o[s]
    return outp
```
